# Optimizing a Trainium2 kernel written in Bass

```python
import jax, jax.numpy as jnp
from jax import lax
import numpy as np

D_MODEL = 1024
BATCH = 8
SEQ = 2048
DEPTH = 4

N_META = 16
BLOCK = 128
WINDOW = 128
RET_HEADS = 8
RET_QK_DIM = D_MODEL // 16
RET_V_DIM = D_MODEL // 8
ATT_Q_HEADS = 8
ATT_KV_HEADS = 2
ATT_GROUP = ATT_Q_HEADS // ATT_KV_HEADS
ATT_HEAD_DIM = D_MODEL // 16
ROPE_DIM = ATT_HEAD_DIM // 4
ROPE_THETA = 500000.0
XPOS_THETA = 10000.0
D_FF = 4 * D_MODEL
EPS = 1e-6
NEG_INF = -1e30

RET_QK = RET_HEADS * RET_QK_DIM
RET_V = RET_HEADS * RET_V_DIM
ATT_Q = ATT_Q_HEADS * ATT_HEAD_DIM
ATT_KV = ATT_KV_HEADS * ATT_HEAD_DIM
SPLITS = (RET_QK, RET_QK, RET_V, RET_V, ATT_Q, ATT_KV, ATT_KV, D_MODEL, D_MODEL)
D_IN = sum(SPLITS)

kernel_name = "hybrid_retention_swa_gated_encoder"


def rms_norm(x, g):
    xf = x.astype(jnp.float32)
    y = xf * lax.rsqrt(jnp.mean(xf * xf, axis=-1, keepdims=True) + EPS)
    return (y * g.astype(jnp.float32)).astype(x.dtype)


def rotate(x, pos, theta, rot_dim):
    half = rot_dim // 2
    freqs = jnp.power(jnp.float32(theta), -jnp.arange(0, rot_dim, 2, dtype=jnp.float32) / rot_dim)
    ang = pos.astype(jnp.float32)[:, None] * freqs[None, :]
    cos = jnp.cos(ang)[:, None, :]
    sin = jnp.sin(ang)[:, None, :]
    xf = x.astype(jnp.float32)
    x1, x2 = xf[..., :half], xf[..., half:rot_dim]
    out = jnp.concatenate([x1 * cos - x2 * sin, x2 * cos + x1 * sin, xf[..., rot_dim:]], axis=-1)
    return out.astype(x.dtype)


def retention_dir(q, k, v, log_gamma, include_diag):
    C = q.shape[-2]
    idx = jnp.arange(C, dtype=jnp.float32)
    rel = idx[:, None] - idx[None, :]
    lg = log_gamma[:, None, None]
    mask = (rel >= 0) if include_diag else (rel > 0)
    decay_in = jnp.where(mask[None], jnp.exp(lg * jnp.maximum(rel, 0.0)[None]), 0.0)
    s = jnp.einsum('bhncd,bhnkd->bhnck', q, k) * decay_in[:, None]
    o_in = jnp.einsum('bhnck,bhnke->bhnce', s, v)
    k_dec = k * jnp.exp(lg * (C - 1 - idx)[None, :])[:, None, :, :].transpose(0, 1, 3, 2)
    kv = jnp.einsum('bhncd,bhnce->bhnde', k_dec, v)
    chunk_decay = jnp.exp(log_gamma * C)[:, None, None]

    def step(state, kv_n):
        return state * chunk_decay + kv_n, state

    init = jnp.zeros(kv.shape[:2] + kv.shape[3:], jnp.float32)
    _, s_prev = lax.scan(step, init, jnp.moveaxis(kv, 2, 0))
    s_prev = jnp.moveaxis(s_prev, 0, 2)
    q_dec = q * jnp.exp(lg * (idx + 1.0)[None, :])[:, None, :, :].transpose(0, 1, 3, 2)
    o_x = jnp.einsum('bhncd,bhnde->bhnce', q_dec, s_prev)
    return o_in + o_x


def retention_branch(q, k, v, gate, log_decay, pad):
    B, L = q.shape[:2]
    Lp = L + pad
    nc = Lp // BLOCK
    pos = jnp.arange(L)
    q = rotate(q, pos, XPOS_THETA, RET_QK_DIM) * (RET_QK_DIM ** -0.5)
    k = rotate(k, pos, XPOS_THETA, RET_QK_DIM)

    def chunk(t):
        t = jnp.pad(t.astype(jnp.float32), ((0, 0), (pad, 0), (0, 0), (0, 0)))
        return t.reshape(B, nc, BLOCK, t.shape[2], t.shape[3]).transpose(0, 3, 1, 2, 4)

    qc, kc, vc = chunk(q), chunk(k), chunk(v)
    flip = lambda t: t[:, :, ::-1, ::-1]
    o = retention_dir(qc, kc, vc, log_decay[0], True) + flip(
        retention_dir(flip(qc), flip(kc), flip(vc), log_decay[1], False))
    o = o.transpose(0, 2, 3, 1, 4).reshape(B, Lp, RET_HEADS, RET_V_DIM)[:, pad:]
    mu = jnp.mean(o, axis=-1, keepdims=True)
    var = jnp.mean(jnp.square(o - mu), axis=-1, keepdims=True)
    o = (o - mu) * lax.rsqrt(var + EPS)
    o = o.reshape(B, L, RET_V).astype(gate.dtype)
    return o * jax.nn.silu(gate)


def attention_branch(q, k, v, sink, pad):
    B, L = q.shape[:2]
    Lp = L + pad
    nb = Lp // BLOCK
    pos = jnp.arange(L)
    q = rotate(q, pos, ROPE_THETA, ROPE_DIM)
    k = rotate(k, pos, ROPE_THETA, ROPE_DIM)
    meta_k, meta_v = k[:, :N_META], v[:, :N_META]
    qb = jnp.pad(q, ((0, 0), (pad, 0), (0, 0), (0, 0))).reshape(
        B, nb, BLOCK, ATT_KV_HEADS, ATT_GROUP, ATT_HEAD_DIM)
    ext = ((0, 0), (pad + BLOCK, BLOCK), (0, 0), (0, 0))
    kb = jnp.pad(k, ext).reshape(B, nb + 2, BLOCK, ATT_KV_HEADS, ATT_HEAD_DIM)
    vb = jnp.pad(v, ext).reshape(B, nb + 2, BLOCK, ATT_KV_HEADS, ATT_HEAD_DIM)

    def band(t):
        return jnp.concatenate([t[:, :-2], t[:, 1:-1], t[:, 2:]], axis=2)

    kband, vband = band(kb), band(vb)
    qpos = jnp.arange(nb)[:, None] * BLOCK + jnp.arange(BLOCK)[None, :]
    kpos = (jnp.arange(nb)[:, None] - 1) * BLOCK + jnp.arange(3 * BLOCK)[None, :]
    valid = ((kpos[:, None, :] >= pad + N_META) & (kpos[:, None, :] < Lp)
             & (jnp.abs(qpos[:, :, None] - kpos[:, None, :]) <= WINDOW))
    scale = ATT_HEAD_DIM ** -0.5
    s_band = jnp.einsum('bnqgrd,bnkgd->bngrqk', qb, kband).astype(jnp.float32) * scale
    s_band = jnp.where(valid[None, :, None, None], s_band, NEG_INF)
    s_meta = jnp.einsum('bnqgrd,bmgd->bngrqm', qb, meta_k).astype(jnp.float32) * scale
    s_sink = jnp.broadcast_to(sink.astype(jnp.float32).reshape(1, 1, ATT_KV_HEADS, ATT_GROUP, 1, 1),
                              s_meta.shape[:-1] + (1,))
    p = jax.nn.softmax(jnp.concatenate([s_sink, s_meta, s_band], axis=-1), axis=-1).astype(v.dtype)
    o = (jnp.einsum('bngrqm,bmgd->bnqgrd', p[..., 1:1 + N_META], meta_v)
         + jnp.einsum('bngrqk,bnkgd->bnqgrd', p[..., 1 + N_META:], vband))
    return o.reshape(B, Lp, ATT_Q)[:, pad:]


def setup_inputs(seed: int = 0) -> dict:
    key = jax.random.key(seed)
    ks = jax.random.split(key, 14)
    f = jnp.float32
    nrm = lambda k, shape, s: jax.random.normal(k, shape, f) * s
    heads = np.arange(RET_HEADS)
    base = np.log(-np.log(1.0 - 2.0 ** (-5.0 - heads))).astype(np.float32)
    return {
        "x": nrm(ks[0], (BATCH, SEQ, D_MODEL), 1.0),
        "meta_tokens": nrm(ks[1], (N_META, D_MODEL), 1.0),
        "w_in": nrm(ks[2], (DEPTH, D_MODEL, D_IN), D_MODEL ** -0.5),
        "w_ret_o": nrm(ks[3], (DEPTH, RET_V, D_MODEL), RET_V ** -0.5),
        "w_att_o": nrm(ks[4], (DEPTH, ATT_Q, D_MODEL), ATT_Q ** -0.5),
        "w_mix_o": nrm(ks[5], (DEPTH, D_MODEL, D_MODEL), D_MODEL ** -0.5),
        "w_ff1": nrm(ks[6], (DEPTH, D_MODEL, D_FF), D_MODEL ** -0.5),
        "w_ff2": nrm(ks[7], (DEPTH, D_FF, D_MODEL), D_FF ** -0.5),
        "norm_mix_pre": 1.0 + nrm(ks[8], (DEPTH, D_MODEL), 0.02),
        "norm_mix_post": 1.0 + nrm(ks[9], (DEPTH, D_MODEL), 0.02),
        "norm_ff_pre": 1.0 + nrm(ks[10], (DEPTH, D_MODEL), 0.02),
        "norm_ff_post": 1.0 + nrm(ks[11], (DEPTH, D_MODEL), 0.02),
        "ret_decay": jnp.asarray(base)[None, None, :] + nrm(ks[12], (DEPTH, 2, RET_HEADS), 0.05),
        "attn_sink": nrm(ks[13], (DEPTH, ATT_Q_HEADS), 0.5),
    }


def reference(x, meta_tokens, w_in, w_ret_o, w_att_o, w_mix_o, w_ff1, w_ff2,
              norm_mix_pre, norm_mix_post, norm_ff_pre, norm_ff_post, ret_decay, attn_sink):
    B = x.shape[0]
    h = jnp.concatenate([jnp.broadcast_to(meta_tokens[None].astype(x.dtype), (B, N_META, D_MODEL)), x], axis=1)
    L = h.shape[1]
    pad = (-L) % BLOCK
    for l in range(DEPTH):
        u = rms_norm(h, norm_mix_pre[l])
        proj = u @ w_in[l]
        parts, off = [], 0
        for size in SPLITS:
            parts.append(proj[..., off:off + size])
            off += size
        q_r, k_r, v_r, g_r, q_a, k_a, v_a, gate_r, gate_a = parts
        log_decay = -jnp.exp(ret_decay[l].astype(jnp.float32))
        y_r = retention_branch(q_r.reshape(B, L, RET_HEADS, RET_QK_DIM),
                               k_r.reshape(B, L, RET_HEADS, RET_QK_DIM),
                               v_r.reshape(B, L, RET_HEADS, RET_V_DIM),
                               g_r, log_decay, pad) @ w_ret_o[l]
        y_a = attention_branch(q_a.reshape(B, L, ATT_Q_HEADS, ATT_HEAD_DIM),
                               k_a.reshape(B, L, ATT_KV_HEADS, ATT_HEAD_DIM),
                               v_a.reshape(B, L, ATT_KV_HEADS, ATT_HEAD_DIM),
                               attn_sink[l], pad) @ w_att_o[l]
        mix = (jax.nn.sigmoid(gate_r) * y_r + jax.nn.sigmoid(gate_a) * y_a) @ w_mix_o[l]
        h = h + rms_norm(mix, norm_mix_post[l])
        u = rms_norm(h, norm_ff_pre[l])
        ff = jnp.square(jax.nn.relu(u @ w_ff1[l])) @ w_ff2[l]
        h = h + rms_norm(ff, norm_ff_post[l])
    return h[:, N_META:]
```

```python
import math
import os
from contextlib import ExitStack
import numpy as np
import concourse.bass as bass
import concourse.mybir as mybir
from concourse.bass_utils import run_bass_kernel_spmd

F32 = mybir.dt.float32
BF16 = mybir.dt.bfloat16
AF = mybir.ActivationFunctionType
ALU = mybir.AluOpType

NCH = 17
T = NCH * 128
DEPTH = 4
EPS = 1e-6
NEG = -30000.0
TILES = [[0, 1, 2, 3, 4], [5, 6, 7, 8], [9, 10, 11, 12], [13, 14, 15, 16]]
HU = lambda hp: hp * 768
QA, KV, GR, GA = 3072, 3584, 3840, 4864


class V:
    __slots__ = ("ap", "reg", "lo", "hi", "esz")

    def __init__(s, ap, reg, lo, hi, esz):
        s.ap, s.reg, s.lo, s.hi, s.esz = ap, reg, lo, hi, esz

    def c(s, a, b):
        if s.reg.startswith("ps"):
            return V(s.ap[:, a:b], s.reg, s.lo, s.hi, s.esz)
        return V(s.ap[:, a:b], s.reg, s.lo + a * s.esz, s.lo + b * s.esz, s.esz)

    def p(s, a, b):
        return V(s.ap[a:b], s.reg, s.lo, s.hi, s.esz)

    def w(s, ap):
        return V(ap, s.reg, s.lo, s.hi, s.esz)

    def r(s, pat, **kw):
        return V(s.ap.rearrange(pat, **kw), s.reg, s.lo, s.hi, s.esz)


def span(vs):
    vs = list(vs)
    return V(None, vs[0].reg, min(v.lo for v in vs), max(v.hi for v in vs), vs[0].esz)


class Op:
    __slots__ = ("idx", "eng", "fn", "dma", "slot", "ndma", "phase", "deps", "sig", "signo", "semkey")


class Prog:
    ENGS = ("pe", "act", "dve", "pool", "sp")

    def __init__(s):
        s.ops = []
        s.segs = {}
        s.phase = 0
        s.muted = False
        s.marks = []

    def _access(s, v, idx, ekey, write, deps):
        write = write or v.reg.startswith("ps")
        segs = s.segs.setdefault(v.reg, [])
        lo, hi = v.lo, v.hi
        for x in (lo, hi):
            for i, sg in enumerate(segs):
                if sg[0] < x < sg[1]:
                    segs.insert(i + 1, [x, sg[1], sg[2], dict(sg[3])])
                    sg[1] = x
                    break
        cov = [sg for sg in segs if sg[0] >= lo and sg[1] <= hi]
        cur = lo
        new = []
        for sg in cov:
            if sg[0] > cur:
                new.append([cur, sg[0], None, {}])
            cur = sg[1]
        if cur < hi:
            new.append([cur, hi, None, {}])
        if new:
            segs.extend(new)
            segs.sort(key=lambda g: g[0])
        for sg in cov + new:
            if sg[2] is not None:
                deps.add(sg[2])
            if write:
                deps.update(sg[3].values())
                sg[2] = idx
                sg[3] = {}
            else:
                sg[3][ekey] = idx

    def op(s, eng, fn, reads=(), writes=(), dma=False, slot=None, ndma=1, force=()):
        if s.muted:
            return None
        o = Op()
        o.idx = len(s.ops)
        o.eng, o.fn, o.dma, o.slot, o.ndma, o.phase = eng, fn, dma, slot, ndma, s.phase
        o.sig = dma
        o.signo = 0
        ekey = ("dma", o.idx) if dma else eng
        deps = set()
        for v in reads:
            s._access(v, o.idx, ekey, False, deps)
        for v in writes:
            s._access(v, o.idx, ekey, True, deps)
        deps.discard(o.idx)
        best = {}
        keep = []
        for d in deps:
            dop = s.ops[d]
            if dop.dma:
                keep.append(d)
            elif dop.eng == "pe" and eng == "pe" and not dma:
                continue
            elif best.get(dop.eng, -1) < d:
                best[dop.eng] = d
        keep.extend(best.values())
        for d in force:
            if d is not None and d not in keep:
                keep.append(d)
        for d in keep:
            s.ops[d].sig = True
        o.deps = keep
        s.ops.append(o)
        return o

    def emit(s, nc, es):
        cnt = {}
        for o in s.ops:
            if o.dma:
                o.semkey = ("dma", o.slot)
                cnt[o.semkey] = cnt.get(o.semkey, 0) + 16 * o.ndma
                o.signo = cnt[o.semkey]
            elif o.sig:
                o.semkey = (o.eng, o.phase)
                cnt[o.semkey] = cnt.get(o.semkey, 0) + 1
                o.signo = cnt[o.semkey]
        sems = {k: es.enter_context(nc.semaphore("s%d" % i)) for i, k in enumerate(cnt)}
        s.nsems = len(sems)
        streams = {e: [o for o in s.ops if o.eng == e] for e in s.ENGS}
        ops = s.ops

        def run(en):
            def f(e):
                lastw = {}
                for o in streams[en]:
                    for d in o.deps:
                        dop = ops[d]
                        if lastw.get(dop.semkey, 0) < dop.signo:
                            lastw[dop.semkey] = dop.signo
                            e.wait_ge(sems[dop.semkey], dop.signo)
                    r = o.fn(e)
                    if o.sig:
                        if o.dma:
                            rs = r if isinstance(r, (list, tuple)) else [r]
                            assert len(rs) == o.ndma
                            for ri in rs:
                                ri.then_inc(sems[o.semkey], 16)
                        else:
                            r.then_inc(sems[o.semkey], 1)
            return f

        block = es.enter_context(nc.Block())
        block.tensor(run("pe"))
        block.scalar(run("act"))
        block.vector(run("dve"))
        block.gpsimd(run("pool"))
        block.sync(run("sp"))


class Bank:
    def __init__(s, v):
        s.v = v
        s.first = True

    def new(s):
        s.first = True
        return s

    def st(s):
        f = s.first
        s.first = False
        return f


def build(layers=(0, 1, 2, 3), stop=None):
    nc = bass.Bass("TRN2", target_bir_lowering=False)
    es = ExitStack()
    P = Prog()

    def din(name, shape):
        return nc.dram_tensor(name, list(shape), F32, kind="ExternalInput").ap()

    x_d = din("x", [2048, 1024])
    meta_d = din("meta", [16, 1024])
    w_in_d = din("w_in", [4, 1024, 5888])
    w_ro_d = din("w_ret_o", [4, 1024, 1024])
    w_ao_d = din("w_att_o", [4, 512, 1024])
    w_mx_d = din("w_mix_o", [4, 1024, 1024])
    w_f1_d = din("w_ff1", [4, 1024, 4096])
    w_f2_d = din("w_ff2", [4, 4096, 1024])
    gains_d = din("gains", [128, 128])
    rdec_d = din("ret_decay", [64])
    sink_d = din("attn_sink", [32])
    cosr_d = din("cosr", [128, NCH * 32])
    sinr_d = din("sinr", [128, NCH * 32])
    cosa_d = din("cosa", [128, NCH * 8])
    sina_d = din("sina", [128, NCH * 8])
    rfp_d = din("rfp", [128, 128])
    rb_d = din("rbm", [128, 128])
    pcols_d = din("pcols", [128, 4])
    mprev_d = din("mprev", [128, 128])
    mnext_d = din("mnext", [128, 128])
    mmeta_d = din("mmeta", [128, 128])
    ident_d = din("ident", [128, 128])
    y_d = nc.dram_tensor("y", [2048, 1024], F32, kind="ExternalOutput").ap()

    def DR(name):
        return V(None, "d_" + name, 0, 1, 1)

    def sb(name, ncols, dt):
        t = es.enter_context(nc.sbuf_tensor("s_" + name, [128, ncols], dt))
        esz = 4 if dt == F32 else 2
        return V(t[:], name, 0, ncols * esz, esz)

    def sub(rv, blo, bhi, dt=BF16):
        ap = rv.ap[:, blo // 2: bhi // 2]
        if dt == F32:
            ap = ap.bitcast(F32)
        return V(ap, rv.reg, rv.lo + blo, rv.lo + bhi, 4 if dt == F32 else 2)

    hT_all = sb("hT", 8 * T, F32)
    hT = [hT_all.c(c * T, (c + 1) * T) for c in range(8)]
    hT3 = hT_all.ap.rearrange("p (c t) -> p c t", c=8)
    cosr = sb("cosr", NCH * 32, F32)
    sinr = sb("sinr", NCH * 32, F32)
    cosa = sb("cosa", NCH * 8, F32)
    sina = sb("sina", NCH * 8, F32)
    rfp = sb("rfp", 128, F32)
    rbm = sb("rbm", 128, F32)
    pcols = sb("pcols", 4, F32)
    DT = sb("DT", 8 * 128, F32)
    ident_f = sb("identf", 128, F32)
    ident_b = sb("identb", 128, BF16)
    ones_b = sb("onesb", 128, BF16)
    mprev = sb("mprev", 128, BF16)
    mnext = sb("mnext", 128, BF16)
    mmeta = sb("mmeta", 128, BF16)
    gains = sb("gains", 128, F32)
    rdec = sb("rdec", 64, F32)
    sinkb = sb("sinkb", 32, F32)
    lg = sb("lg", 16, F32)
    nhalf = sb("nhalf", 1, F32)
    dcol = sb("dcol", 56, F32)
    qfc, qbc, kdfc, kdbc, cdf, cdb, esink = [dcol.c(i * 8, (i + 1) * 8) for i in range(7)]
    KaT = sb("KaT", T, BF16)
    Va = sb("Va", NCH * 130, BF16)
    SB = sb("SB", 9 * 1024, BF16)
    Srun = sb("Srun", 1024, F32)
    U_all = sb("U", 8 * 640, BF16)
    uT = [U_all.c(c * 640, (c + 1) * 640) for c in range(8)]
    RA = sb("RA", 16896 // 2, BF16)
    RB = sb("RB", 33792 // 2, BF16)
    SC = sb("SC", 8192 // 2, BF16)
    NW = 3
    wslots = [sb("w%d" % i, 4096, BF16) for i in range(NW)]

    orT = [sub(RA, k * 1280, (k + 1) * 1280) for k in range(8)]
    oaT = [sub(RA, 10240 + k * 1280, 10240 + (k + 1) * 1280) for k in range(4)]
    oaT3 = sub(RA, 10240, 15360).ap.rearrange("p (k t) -> p k t", k=4)
    mixT_all = sub(RA, 0, 16896, F32)
    mixT = [mixT_all.c(m * 528, (m + 1) * 528) for m in range(8)]
    mixT3 = mixT_all.ap.rearrange("p (m t) -> p m t", m=8)
    zT_all = sub(RB, 0, 8448)
    zT = [zT_all.c(m * 528, (m + 1) * 528) for m in range(8)]
    zT3 = zT_all.ap.rearrange("p (m t) -> p m t", m=8)
    hid_all = sub(RB, 0, 33792)
    hid = [hid_all.c(f * 528, (f + 1) * 528) for f in range(32)]
    hid3 = hid_all.ap.rearrange("p (f t) -> p f t", f=32)
    TB = 8448
    stage = [sub(RB, 0, 4096, F32), sub(RB, 4096, 8192, F32)]
    sq = [sub(SC, 0, 1024), sub(SC, 1024, 2048)]
    rt = sub(SC, 2048, 4160, F32)
    rstd = sub(SC, 4160, 6272, F32)
    scx = sub(SC, 6272, 8192, F32)
    ffr = [sub(SC, 0, 2048, F32), sub(SC, 2048, 4096, F32)]

    banks = []
    for i in range(8):
        t = es.enter_context(nc.psum_tensor("ps%d" % i, [128, 512], F32))
        banks.append(Bank(V(t[:], "ps%d" % i, 0, 2048, 4)))
    SSB, MB = banks[6], banks[7]
    ring = [0]

    def rbank():
        b = banks[ring[0] % 6]
        ring[0] += 1
        return b.new()

    def bfv(v):
        return V(v.ap.bitcast(BF16), v.reg, v.lo, v.hi, 2)

    def rw(x):
        return [x] if isinstance(x, V) else list(x)

    lastpe = {}

    def mm(out, lhsT, rhs, start, stop):
        sg_ = (lhsT.ap.base_partition(), lhsT.ap.shape[0])
        prev = lastpe.get(out.reg)
        force = [prev[1]] if (prev is not None and prev[0] != sg_) else []
        o = P.op("pe", lambda e: e.matmul(out.ap, lhsT.ap, rhs.ap, start=start, stop=stop, skip_group_check=True),
                 reads=[lhsT, rhs], writes=[out], force=force)
        if o is not None:
            lastpe[out.reg] = (sg_, o.idx)

    def tr(out, in_, idn):
        sg_ = (0, 128)
        prev = lastpe.get(out.reg)
        force = [prev[1]] if (prev is not None and prev[0] != sg_) else []
        o = P.op("pe", lambda e: e.transpose(out.ap, in_.ap, idn.ap), reads=[in_, idn], writes=[out], force=force)
        if o is not None:
            lastpe[out.reg] = (sg_, o.idx)

    def act(out, in_, func, scale=None, bias=None, accum=None, eng="act"):
        kw = {}
        rd = [in_]
        wr = [out]
        if scale is not None:
            if isinstance(scale, V):
                kw["scale"] = scale.ap
                rd.append(scale)
            else:
                kw["scale"] = float(scale)
        if bias is not None:
            if isinstance(bias, V):
                kw["bias"] = bias.ap
                rd.append(bias)
            else:
                kw["bias"] = float(bias)
        if accum is not None:
            kw["accum_out"] = accum.ap
            wr.append(accum)
        P.op("act", lambda e: e.activation(out.ap, in_.ap, func, **kw), reads=rd, writes=wr)

    def tt(out, a, b, op, eng="dve"):
        P.op(eng, lambda e: e.tensor_tensor(out.ap, a.ap, b.ap, op), reads=[a, b], writes=[out])

    def ts(out, a, s1, s2, op0, op1=None, eng="dve"):
        rd = [a]
        a1 = s1.ap if isinstance(s1, V) else s1
        a2 = s2.ap if isinstance(s2, V) else s2
        if isinstance(s1, V):
            rd.append(s1)
        if isinstance(s2, V):
            rd.append(s2)
        if op1 is None:
            P.op(eng, lambda e: e.tensor_scalar(out.ap, a.ap, a1, None, op0), reads=rd, writes=[out])
        else:
            P.op(eng, lambda e: e.tensor_scalar(out.ap, a.ap, a1, a2, op0, op1), reads=rd, writes=[out])

    def stt(out, a, sc, b, op0, op1, eng="dve"):
        rd = [a, b]
        a1 = sc.ap if isinstance(sc, V) else sc
        if isinstance(sc, V):
            rd.append(sc)
        P.op(eng, lambda e: e.scalar_tensor_tensor(out.ap, a.ap, a1, b.ap, op0, op1), reads=rd, writes=[out])

    def cp(out, in_, eng="dve"):
        if eng == "act":
            act(out, in_, AF.Copy)
        else:
            P.op(eng, lambda e: e.tensor_copy(out.ap, in_.ap), reads=[in_], writes=[out])

    def recip(out, in_):
        P.op("dve", lambda e: e.reciprocal(out.ap, in_.ap), reads=[in_], writes=[out])

    def rstd_from(out, ssq, n_scale, tmp):
        act(tmp, ssq, AF.Sqrt, scale=n_scale, bias=EPS)
        recip(out, tmp)

    def memset(v, val, eng="dve"):
        P.op(eng, lambda e: e.memset(v.ap, val), writes=[v])

    def dma(q, out, in_ap_or_v, slot, in_tok=None, out_tok=None):
        if isinstance(out, V):
            o_ap, wr = out.ap, [out]
        else:
            o_ap, wr = out, [out_tok]
        if isinstance(in_ap_or_v, V):
            i_ap, rd = in_ap_or_v.ap, [in_ap_or_v]
        else:
            i_ap, rd = in_ap_or_v, ([in_tok] if in_tok else [])
        P.op(q, lambda e: e.dma_start(out=o_ap, in_=i_ap), reads=rd, writes=wr, dma=True, slot=slot)

    class WS:
        def __init__(s, plan=None):
            s.plan = plan
            s.rec = []
            s.i = 0
            s.issued = 0

        def use(s, src, nk, ncols, live=2):
            if s.plan is None:
                s.rec.append((src, nk, ncols))
                return wslots[0]
            i = s.i
            s.i += 1
            if P.muted:
                return wslots[0]
            while s.issued < min(len(s.plan), i + NW + 1 - live):
                j = s.issued
                sr, k2, n2 = s.plan[j]
                sl = wslots[j % NW]
                dma("pool", sl.c(0, k2 * n2).w(sl.ap[:, 0:k2 * n2].rearrange("p (k n) -> p k n", k=k2)),
                    sr.rearrange("(k p) n -> p k n", p=128), slot=("w", j % NW))
                s.issued += 1
            return wslots[i % NW]

    def tc0(t):
        return TILES[t][0] * 128

    def g_main(t):
        return (128, 640) if t == 0 else (tc0(t), tc0(t) + 512)

    def ca_main(t):
        return (128, 640) if t == 0 else (0, 512)

    def vo_main(t):
        return (16, 528) if t == 0 else (0, 512)

    def gidx(l, j):
        return (l * 4 + j) * 8

    def emit_all(ws):
        Mfirst = [True]

        def mstart():
            f = Mfirst[0]
            Mfirst[0] = False
            return f

        P.phase = 0
        for (v, d, q) in ((cosr, cosr_d, "sp"), (sinr, sinr_d, "sp"), (cosa, cosa_d, "sp"), (sina, sina_d, "sp"),
                          (rfp, rfp_d, "sp"), (rbm, rb_d, "sp"), (pcols, pcols_d, "sp"), (ident_f, ident_d, "sp"),
                          (gains, gains_d, "sp"), (ident_b, ident_d, "pool"), (mprev, mprev_d, "pool"),
                          (mnext, mnext_d, "pool"), (mmeta, mmeta_d, "pool")):
            dma(q, v, d, slot=("c", v.reg))
        dma("sp", rdec, rdec_d.partition_broadcast(128), slot=("c", "rdec"))
        dma("sp", sinkb, sink_d.partition_broadcast(128), slot=("c", "sink"))
        memset(ones_b, 1.0)
        memset(nhalf, -0.5)
        memset(Va, 1.0)
        memset(U_all, 0.0)

        for n in range(NCH):
            st = stage[n % 2]
            if n == 0:
                memset(st, 0.0)
                dma("sp", st.p(112, 128), meta_d, slot=("st", 0))
            else:
                dma("sp", st, x_d[(n - 1) * 128: n * 128, :], slot=("st", n % 2))
            for half in range(2):
                b = rbank()
                for j in range(4):
                    c = half * 4 + j
                    tr(b.v.c(j * 128, (j + 1) * 128), st.c(c * 128, (c + 1) * 128), ident_f)
                o = span(hT[half * 4 + j].c(n * 128, (n + 1) * 128) for j in range(4))
                o.ap = hT3[:, half * 4:(half + 1) * 4, n * 128:(n + 1) * 128]
                cp(o, b.v.r("p (j t) -> p j t", j=4), eng=("act" if half else "dve"))

        def prenorm(t, l, j):
            pieces = [g_main(t)] + ([(112, 128)] if t == 0 else [])
            if t == 0:
                z = span(u.c(0, 112) for u in uT)
                z.ap = U_all.ap.rearrange("p (c t) -> p c t", c=8)[:, :, 0:112]
                memset(z, 0.0)
            for (c0, c1) in pieces:
                n = c1 - c0
                ssb = SSB.new() if n > 16 else rbank()
                for c in range(8):
                    s_ = sq[c % 2].c(0, n)
                    act(s_, hT[c].c(c0, c1), AF.Square)
                    mm(ssb.v.c(0, n), ones_b, s_, start=ssb.st(), stop=(c == 7))
                rstd_from(rstd.c(0, n), ssb.v.c(0, n), 1.0 / 1024, rt.c(0, n))
                for c in range(8):
                    g = gains.c(gidx(l, j) + c, gidx(l, j) + c + 1)
                    stt(uT[c].c(c0 - tc0(t), c1 - tc0(t)), hT[c].c(c0, c1), g, rstd.c(0, n), ALU.mult, ALU.mult)

        def post_residual(t, l, j, outT, outT3):
            a, b = vo_main(t)
            n = b - a
            ga, gb = g_main(t)
            rstd_from(rstd.c(0, n), SSB.v.c(0, n), 1.0 / 1024, rt.c(0, n))
            for m in range(8):
                g = gains.c(gidx(l, j) + m, gidx(l, j) + m + 1)
                tt(outT[m].c(a, b), outT[m].c(a, b), rstd.c(0, n), ALU.mult)
                stt(hT[m].c(ga, gb), outT[m].c(a, b), g, hT[m].c(ga, gb), ALU.mult, ALU.add)
            if t == 0:
                om = span(o.c(0, 16) for o in outT)
                om.ap = outT3[:, :, 0:16]
                cp(om, MB.v.c(0, 128).r("p (m t) -> p m t", m=8), eng="act")
                sqm = sub(SC, 0, 256)
                act(sqm, MB.v.c(0, 128), AF.Square)
                b2 = rbank()
                for m in range(8):
                    mm(b2.v.c(0, 16), ones_b, sqm.c(m * 16, (m + 1) * 16), start=b2.st(), stop=(m == 7))
                rtm = scx.c(0, 16)
                rsm = scx.c(16, 32)
                rstd_from(rsm, b2.v.c(0, 16), 1.0 / 1024, rtm)
                t1 = scx.c(32, 160)
                t13 = t1.r("p (m t) -> p m t", m=8)
                tt(t13, om, rsm.w(rsm.ap.unsqueeze(1).to_broadcast([128, 8, 16])), ALU.mult)
                gv = gains.c(gidx(l, j), gidx(l, j) + 8)
                tt(t13, t13, gv.w(gv.ap.unsqueeze(2).to_broadcast([128, 8, 16])), ALU.mult)
                hm = span(h.c(112, 128) for h in hT)
                hm.ap = hT3[:, :, 112:128]
                tt(hm, hm, t13, ALU.add)

        def fm_mm(bank, mcol, lhs_list, rhs_list, t, rng_main, rng_meta, last=True):
            K = len(lhs_list)
            a, b = rng_main
            for k in range(K):
                mm(bank.v.c(0, b - a), lhs_list[k], rhs_list[k].c(a, b), start=bank.st(), stop=(last and k == K - 1))
            if t == 0 and mcol is not None:
                ma, mb_ = rng_meta
                for k in range(K):
                    mm(MB.v.c(mcol, mcol + 16), lhs_list[k], rhs_list[k].c(ma, mb_), start=mstart(),
                       stop=(last and k == K - 1))

        def rope64(dst, src, n, nq, A, B1, B2, eng="dve"):
            cs = cosr.c(n * 32, (n + 1) * 32)
            sn = sinr.c(n * 32, (n + 1) * 32)
            s4 = src.ap.rearrange("p (q a f) -> p q a f", q=nq, a=2)
            d4 = dst.ap.rearrange("p (q a f) -> p q a f", q=nq, a=2)
            A4 = A.ap.rearrange("p (q a f) -> p q a f", q=nq, a=2)
            tt(A.w(A4), src.w(s4), cs.w(cs.ap.unsqueeze(1).unsqueeze(1).to_broadcast([128, nq, 2, 32])), ALU.mult, eng=eng)
            snb = sn.w(sn.ap.unsqueeze(1).to_broadcast([128, nq, 32]))
            b1 = B1.r("p (q f) -> p q f", q=nq)
            b2 = B2.r("p (q f) -> p q f", q=nq)
            tt(b1, src.w(s4[:, :, 1, :]), snb, ALU.mult, eng=eng)
            tt(b2, src.w(s4[:, :, 0, :]), snb, ALU.mult, eng=eng)
            tt(dst.w(d4[:, :, 0, :]), A.w(A4[:, :, 0, :]), b1, ALU.subtract, eng=eng)
            tt(dst.w(d4[:, :, 1, :]), A.w(A4[:, :, 1, :]), b2, ALU.add, eng=eng)

        def warm(k):
            for _ in range(k):
                mm(MB.v.c(0, 512), ones_b, U_all.c(0, 512), start=True, stop=True)

        def pipeline(items, stages, skews, first=0, junk=0, last=()):
            n = len(items)
            mid = [j for j in range(first, len(stages)) if j not in last]
            order = list(range(first)) + sorted(mid, key=lambda j: (-skews[j], j)) + list(last)
            for s_ in range(n + max(skews)):
                for j in order:
                    fn, sk = stages[j], skews[j]
                    i = s_ - sk
                    if 0 <= i < n:
                        fn(i, items[i])
                warm(junk)

        def chk(name):
            P.marks.append((name, sum(1 for o in P.ops if o.eng == "pe")))
            if stop == name:
                P.muted = True

        for l in layers:
            P.phase = l
            act(lg, rdec.c(l * 16, (l + 1) * 16), AF.Exp)
            ts(lg, lg, -1.0, None, ALU.mult)
            tA = sub(RB, TB, TB + 512, F32)
            tB = sub(RB, TB + 512, TB + 1024, F32)
            for h in range(8):
                ts(tA, rfp, lg.c(h, h + 1), None, ALU.mult)
                stt(tB, rbm, lg.c(8 + h, 9 + h), tA, ALU.mult, ALU.add)
                act(DT.c(h * 128, (h + 1) * 128), tB, AF.Exp)
            l8 = math.log(0.125)
            act(qfc, lg.c(0, 8), AF.Exp, scale=pcols.c(0, 1), bias=l8)
            act(qbc, lg.c(8, 16), AF.Exp, scale=pcols.c(1, 2), bias=l8)
            act(kdfc, lg.c(0, 8), AF.Exp, scale=pcols.c(2, 3))
            act(kdbc, lg.c(8, 16), AF.Exp, scale=pcols.c(3, 4))
            act(cdf, lg.c(0, 8), AF.Exp, scale=128.0)
            act(cdb, lg.c(8, 16), AF.Exp, scale=128.0)
            act(esink, sinkb.c(l * 8, (l + 1) * 8), AF.Exp)

            chk("L%d_start" % l)
            memset(Srun, 0.0)
            for t in (3, 2, 1, 0):
                prenorm(t, l, 0)
                chunks = TILES[t]
                wkv = ws.use(w_in_d[l, :, KV:KV + 256], 8, 256, live=1)
                for n in chunks:
                    nl = n - chunks[0]
                    base = TB + (n % 2) * 2048
                    xa = sub(RB, base, base + 1024, F32)
                    rA = sub(RB, base + 1024, base + 1152, F32)
                    rB1 = sub(RB, base + 1152, base + 1216, F32)
                    rB2 = sub(RB, base + 1216, base + 1280, F32)
                    kab = sub(RB, base + 1280, base + 1536)
                    b = banks[4].new()
                    for c in range(8):
                        mm(b.v.c(0, 256), uT[c].c(nl * 128, (nl + 1) * 128), wkv.c(c * 256, (c + 1) * 256),
                           start=b.st(), stop=(c == 7))
                    cp(xa, b.v.c(0, 256), eng="act")
                    xk = xa.c(0, 128)
                    x4 = xk.ap.rearrange("p (h d) -> p h d", h=2)
                    x16 = xk.w(x4[:, :, 0:16].rearrange("p h (a f) -> p h a f", a=2))
                    cs = cosa.c(n * 8, (n + 1) * 8)
                    sn = sina.c(n * 8, (n + 1) * 8)
                    tt(rA.r("p (h a f) -> p h a f", h=2, a=2), x16,
                       cs.w(cs.ap.unsqueeze(1).unsqueeze(1).to_broadcast([128, 2, 2, 8])), ALU.mult)
                    snb = sn.w(sn.ap.unsqueeze(1).to_broadcast([128, 2, 8]))
                    tt(rB1.r("p (h f) -> p h f", h=2), xk.w(x4[:, :, 8:16]), snb, ALU.mult)
                    tt(rB2.r("p (h f) -> p h f", h=2), xk.w(x4[:, :, 0:8]), snb, ALU.mult)
                    k3 = kab.ap.rearrange("p (h d) -> p h d", h=2)
                    rA4 = rA.ap.rearrange("p (h a f) -> p h a f", h=2, a=2)
                    tt(kab.w(k3[:, :, 0:8]), rA.w(rA4[:, :, 0, :]), rB1.r("p (h f) -> p h f", h=2), ALU.subtract)
                    tt(kab.w(k3[:, :, 8:16]), rA.w(rA4[:, :, 1, :]), rB2.r("p (h f) -> p h f", h=2), ALU.add)
                    cp(kab.w(k3[:, :, 16:64]), xk.w(x4[:, :, 16:64]))
                    van = Va.c(n * 130, (n + 1) * 130)
                    cp(van.w(van.ap.rearrange("p (g d) -> p g d", g=2)[:, :, 0:64]),
                       xa.c(128, 256).r("p (g d) -> p g d", g=2), eng="act")
                    b2 = banks[5].new()
                    tr(bfv(b2.v.c(0, 64)), kab, ident_b)
                    cp(KaT.c(n * 128, (n + 1) * 128), bfv(b2.v.c(0, 64)))
                items = [(hp, n) for hp in range(4) for n in reversed(chunks)]
                units = {}

                def p1A(i, it, t=t, chunks=chunks):
                    hp, n = it
                    if n == chunks[-1]:
                        units[hp] = ws.use(w_in_d[l, :, HU(hp) + 128:HU(hp) + 512], 8, 384, live=1)
                    wu = units[hp]
                    nl = n - chunks[0]
                    b = banks[i % 2].new()
                    for c in range(8):
                        mm(b.v.c(0, 384), uT[c].c(nl * 128, (nl + 1) * 128), wu.c(c * 384, (c + 1) * 384),
                           start=b.st(), stop=(c == 7))

                def p1set(i):
                    base = TB + (i % 4) * 3584
                    return dict(x=sub(RB, base, base + 512, F32), A=sub(RB, base + 512, base + 1024, F32),
                                B1=sub(RB, base + 1024, base + 1280, F32), B2=sub(RB, base + 1280, base + 1536, F32),
                                kr=sub(RB, base + 1536, base + 2048, F32), Kd=sub(RB, base + 2048, base + 2560),
                                Vb=sub(RB, base + 2560, base + 3072))

                def p1B(i, it):
                    hp, n = it
                    S = p1set(i)
                    b = banks[i % 2]
                    cp(S["x"], b.v.c(0, 128), eng="act")
                    cp(S["Vb"], b.v.c(128, 384), eng="act")
                    rope64(S["kr"], S["x"], n, 2, S["A"], S["B1"], S["B2"])
                    kr = S["kr"]
                    kb = kr.w(kr.ap.rearrange("p (h d) -> p h d", h=2).unsqueeze(2).to_broadcast([128, 2, 2, 64]))
                    dc = kdbc.c(2 * hp, 2 * hp + 2)
                    tt(S["Kd"].r("p (h a d) -> p h a d", h=2, a=2), kb,
                       dc.w(dc.ap.unsqueeze(2).unsqueeze(3).to_broadcast([128, 2, 2, 64])), ALU.mult)

                def p1D(i, it):
                    S = p1set(i)
                    b = banks[2 + i % 2].new()
                    for j in range(2):
                        mm(b.v.c(j * 128, (j + 1) * 128), S["Kd"].c(j * 128, (j + 1) * 128),
                           S["Vb"].c(j * 128, (j + 1) * 128), start=b.st(), stop=True)

                def p1E(i, it):
                    hp, n = it
                    pp = n % 2
                    b = banks[2 + i % 2]
                    sr = Srun.c(hp * 256, (hp + 1) * 256)
                    slot = SB.c((n // 2) * 1024 + hp * 256, (n // 2) * 1024 + (hp + 1) * 256)
                    cp(slot.p(pp * 64, (pp + 1) * 64), sr.p(pp * 64, (pp + 1) * 64), eng="act")
                    dc = cdb.c(2 * hp, 2 * hp + 2)
                    sr3 = sr.r("p (h e) -> p h e", h=2)
                    tt(sr3, sr3, dc.w(dc.ap.unsqueeze(2).to_broadcast([128, 2, 128])), ALU.mult)
                    tt(sr, sr, b.v.c(0, 256), ALU.add)

                pipeline(items, [p1A, p1B, p1D, p1E], [0, 0, 2, 2], first=1, last=(1,), junk=int(os.environ.get('JP', '4')))
            chk("pass1")

            memset(Srun, 0.0)
            for t in range(4):
                chunks = TILES[t]
                prenorm(t, l, 0)
                items = [(hp, n) for hp in range(4) for n in chunks]
                unitsA, unitsB = {}, {}
                junk = sub(SC, 6272, 7296, F32)

                def rset(i):
                    o = {}
                    eb = (i % 2) * 5120
                    lb = 10240 + (i % 4) * 5696
                    off = [0, 0]

                    def take(name, nb, dt=BF16, late=False):
                        k = 1 if late else 0
                        base = lb if late else eb
                        o[name] = sub(RB, base + off[k], base + off[k] + nb, dt)
                        off[k] += nb
                    take("x", 1024, F32)
                    take("A", 1024, F32)
                    take("B1", 512, F32)
                    take("B2", 512, F32)
                    take("qk", 1024, F32)
                    take("QQ", 512)
                    take("Kdup", 512)
                    take("G", 1024, F32, True)
                    take("Kd", 512, BF16, True)
                    take("Vb", 512, BF16, True)
                    take("QKT", 1024, BF16, True)
                    take("PT", 512, BF16, True)
                    take("Sfb", 512, BF16, True)
                    take("o32", 1024, F32, True)
                    take("og", 512, BF16, True)
                    take("sm", 64, F32, True)
                    assert off[0] == 5120 and off[1] == 5696
                    return o

                def rA_(i, it, chunks=chunks):
                    hp, n = it
                    if n == chunks[0]:
                        unitsA[hp] = ws.use(w_in_d[l, :, HU(hp):HU(hp) + 512], 8, 512)
                        unitsB[hp] = ws.use(w_in_d[l, :, HU(hp) + 512:HU(hp) + 768], 8, 256)
                    wa, wb = unitsA[hp], unitsB[hp]
                    nl = n - chunks[0]
                    b = banks[0].new()
                    for c in range(8):
                        mm(b.v.c(0, 512), uT[c].c(nl * 128, (nl + 1) * 128), wa.c(c * 512, (c + 1) * 512),
                           start=b.st(), stop=(c == 7))
                    b = banks[1].new()
                    for c in range(8):
                        mm(b.v.c(0, 256), uT[c].c(nl * 128, (nl + 1) * 128), wb.c(c * 256, (c + 1) * 256),
                           start=b.st(), stop=(c == 7))

                def hb(col, hp, d):
                    dc = col.c(2 * hp, 2 * hp + 2)
                    return dc.w(dc.ap.unsqueeze(2).to_broadcast([128, 2, d]))

                def rB_(i, it):
                    hp, n = it
                    pp = n % 2
                    S = rset(i)
                    cp(S["x"], banks[0].v.c(0, 256), eng="act")
                    cp(S["Vb"], banks[0].v.c(256, 512), eng="act")
                    act(S["G"], banks[1].v.c(0, 256), AF.Silu)
                    rope64(S["qk"], S["x"], n, 4, S["A"], S["B1"], S["B2"])
                    q = S["qk"].c(0, 128).r("p (h d) -> p h d", h=2)
                    k = S["qk"].c(128, 256)
                    QQ3 = S["QQ"].ap.rearrange("p (h a d) -> p h a d", h=2, a=2)
                    tt(S["QQ"].w(QQ3[:, :, pp, :]), q, hb(qbc, hp, 64), ALU.mult)
                    tt(S["QQ"].w(QQ3[:, :, 1 - pp, :]), q, hb(qfc, hp, 64), ALU.mult)
                    kb = k.w(k.ap.rearrange("p (h d) -> p h d", h=2).unsqueeze(2).to_broadcast([128, 2, 2, 64]))
                    dc = kdfc.c(2 * hp, 2 * hp + 2)
                    tt(S["Kd"].r("p (h a d) -> p h a d", h=2, a=2), kb,
                       dc.w(dc.ap.unsqueeze(2).unsqueeze(3).to_broadcast([128, 2, 2, 64])), ALU.mult, eng="pool")
                    cp(S["Kdup"].r("p (h a d) -> p h a d", h=2, a=2), kb, eng="pool")

                def rC_(i, it):
                    S = rset(i)
                    b = banks[2].new()
                    for j in range(2):
                        tr(bfv(b.v.c(j * 64, (j + 1) * 64)), S["QQ"].c(j * 128, (j + 1) * 128), ident_b)
                    for j in range(2):
                        tr(bfv(b.v.c(128 + j * 64, 128 + (j + 1) * 64)), S["Kdup"].c(j * 128, (j + 1) * 128), ident_b)

                def rC2_(i, it):
                    S = rset(i)
                    cp(S["QKT"], bfv(banks[2].v.c(0, 256)))

                def rD_(i, it):
                    hp, n = it
                    pp = n % 2
                    S = rset(i)
                    hf = 1 - pp
                    b = banks[3].new()
                    QKT = S["QKT"]
                    for j in range(2):
                        mm(b.v.c(256 + j * 128, 256 + (j + 1) * 128), S["Kd"].c(j * 128, (j + 1) * 128),
                           S["Vb"].c(j * 128, (j + 1) * 128), start=b.st(), stop=True)
                    for j in range(2):
                        mm(b.v.c(j * 128, (j + 1) * 128), QKT.p(hf * 64, (hf + 1) * 64).c(256 + j * 128, 256 + (j + 1) * 128),
                           QKT.p(hf * 64, (hf + 1) * 64).c(j * 128, (j + 1) * 128), start=b.st(), stop=True)

                def rE_(i, it):
                    hp, n = it
                    pp = n % 2
                    hf = 1 - pp
                    S = rset(i)
                    b = banks[3]
                    tt(S["PT"], b.v.c(0, 256), DT.c(hp * 256, (hp + 1) * 256), ALU.mult)
                    sr = Srun.c(hp * 256, (hp + 1) * 256)
                    cp(S["Sfb"].p(hf * 64, (hf + 1) * 64), sr.p(hf * 64, (hf + 1) * 64), eng="pool")
                    sr3 = sr.r("p (h e) -> p h e", h=2)
                    tt(sr3, sr3, hb(cdf, hp, 128), ALU.mult)
                    tt(sr, sr, b.v.c(256, 512), ALU.add)

                def rF_(i, it):
                    hp, n = it
                    pp = n % 2
                    hf = 1 - pp
                    S = rset(i)
                    b = banks[4].new()
                    QKT = S["QKT"]
                    slot = SB.c((n // 2) * 1024 + hp * 256, (n // 2) * 1024 + (hp + 1) * 256)
                    for j in range(2):
                        mm(b.v.c(j * 128, (j + 1) * 128), S["PT"].c(j * 128, (j + 1) * 128), S["Vb"].c(j * 128, (j + 1) * 128),
                           start=b.st(), stop=False)
                    for j in range(2):
                        mm(b.v.c(j * 128, (j + 1) * 128), QKT.p(pp * 64, (pp + 1) * 64).c(j * 128, (j + 1) * 128),
                           slot.p(pp * 64, (pp + 1) * 64).c(j * 128, (j + 1) * 128), start=b.st(), stop=False)
                    for j in range(2):
                        mm(b.v.c(j * 128, (j + 1) * 128), QKT.p(hf * 64, (hf + 1) * 64).c(j * 128, (j + 1) * 128),
                           S["Sfb"].p(hf * 64, (hf + 1) * 64).c(j * 128, (j + 1) * 128), start=b.st(), stop=True)

                def rG_(i, it):
                    S = rset(i)
                    b = banks[4]
                    sm = S["sm"]
                    s1, s2, mean, msq, var, rs = [sm.c(2 * j, 2 * j + 2) for j in range(6)]
                    for j in range(2):
                        act(S["o32"].c(j * 128, (j + 1) * 128), b.v.c(j * 128, (j + 1) * 128), AF.Copy, accum=s1.c(j, j + 1))
                    for j in range(2):
                        act(junk.c(0, 128), S["o32"].c(j * 128, (j + 1) * 128), AF.Square, accum=s2.c(j, j + 1))
                    ts(mean, s1, 1.0 / 128, None, ALU.mult)
                    tt(msq, mean, mean, ALU.mult)
                    ts(var, s2, 1.0 / 128, EPS, ALU.mult, ALU.add)
                    tt(var, var, msq, ALU.subtract)
                    tt(rs, var, nhalf.w(nhalf.ap.to_broadcast([128, 2])), ALU.pow, eng="pool")
                    o3 = S["o32"].r("p (h e) -> p h e", h=2)
                    tt(o3, o3, mean.w(mean.ap.unsqueeze(2).to_broadcast([128, 2, 128])), ALU.subtract)
                    tt(o3, o3, rs.w(rs.ap.unsqueeze(2).to_broadcast([128, 2, 128])), ALU.mult)
                    tt(S["og"], S["o32"], S["G"], ALU.mult, eng="pool")

                def rH_(i, it):
                    S = rset(i)
                    b = banks[5].new()
                    for j in range(2):
                        tr(bfv(b.v.c(j * 64, (j + 1) * 64)), S["og"].c(j * 128, (j + 1) * 128), ident_b)

                def rH2_(i, it, chunks=chunks):
                    hp, n = it
                    nl = n - chunks[0]
                    o = span(orT[2 * hp + j].c(nl * 128, (nl + 1) * 128) for j in range(2))
                    o.ap = sub(RA, 0, 10240).ap.rearrange("p (k t) -> p k t", k=8)[:, 2 * hp:2 * hp + 2, nl * 128:(nl + 1) * 128]
                    cp(o, bfv(banks[5].v.c(0, 128)).r("p (j t) -> p j t", j=2), eng="act")

                pipeline(items, [rA_, rB_, rC_, rC2_, rD_, rE_, rF_, rG_, rH_, rH2_], [0, 0, 1, 1, 2, 2, 3, 3, 4, 4], first=1, last=(1,), junk=int(os.environ.get('JR', '6')))
                chk("ret%d" % t)
                wqa = ws.use(w_in_d[l, :, QA:QA + 512], 8, 512, live=1)

                def aset(n):
                    base = TB + (n % 2) * 11264
                    return dict(
                        xq=sub(RB, base, base + 2048, F32), rA=sub(RB, base + 2048, base + 2560, F32),
                        rB1=sub(RB, base + 2560, base + 2816, F32), rB2=sub(RB, base + 2816, base + 3072, F32),
                        qab=sub(RB, base + 3072, base + 4096), QaT=sub(RB, base + 4096, base + 5120),
                        Pa=[[sub(RB, base + 5120 + (g * 2 + j) * 1024, base + 5120 + (g * 2 + j + 1) * 1024)
                             for j in range(2)] for g in range(2)],
                        oat=sub(RB, base + 9216, base + 10240), den=sub(RB, base + 10240, base + 10272, F32))

                def a0(i, n, chunks=chunks):
                    S = aset(n)
                    nl = n - chunks[0]
                    xq, rA, rB1, rB2, qab = S["xq"], S["rA"], S["rB1"], S["rB2"], S["qab"]
                    b = banks[0].new()
                    for c in range(8):
                        mm(b.v.c(0, 512), uT[c].c(nl * 128, (nl + 1) * 128), wqa.c(c * 512, (c + 1) * 512),
                           start=b.st(), stop=(c == 7))

                def a0b(i, n):
                    S = aset(n)
                    xq, rA, rB1, rB2, qab = S["xq"], S["rA"], S["rB1"], S["rB2"], S["qab"]
                    cp(xq, banks[0].v.c(0, 512), eng="act")
                    x4 = xq.ap.rearrange("p (h d) -> p h d", h=8)
                    x16 = xq.w(x4[:, :, 0:16].rearrange("p h (a f) -> p h a f", a=2))
                    cs = cosa.c(n * 8, (n + 1) * 8)
                    sn = sina.c(n * 8, (n + 1) * 8)
                    rA4 = rA.ap.rearrange("p (h a f) -> p h a f", h=8, a=2)
                    tt(rA.w(rA4), x16, cs.w(cs.ap.unsqueeze(1).unsqueeze(1).to_broadcast([128, 8, 2, 8])), ALU.mult)
                    snb = sn.w(sn.ap.unsqueeze(1).to_broadcast([128, 8, 8]))
                    tt(rB1.r("p (h f) -> p h f", h=8), xq.w(x4[:, :, 8:16]), snb, ALU.mult)
                    tt(rB2.r("p (h f) -> p h f", h=8), xq.w(x4[:, :, 0:8]), snb, ALU.mult)
                    q3 = qab.ap.rearrange("p (h d) -> p h d", h=8)
                    tt(qab.w(q3[:, :, 0:8]), rA.w(rA4[:, :, 0, :]), rB1.r("p (h f) -> p h f", h=8), ALU.subtract)
                    tt(qab.w(q3[:, :, 8:16]), rA.w(rA4[:, :, 1, :]), rB2.r("p (h f) -> p h f", h=8), ALU.add)
                    cp(qab.w(q3[:, :, 16:64]), xq.w(x4[:, :, 16:64]), eng="pool")

                def a1(i, n):
                    S = aset(n)
                    b = banks[1].new()
                    for j in range(4):
                        tr(bfv(b.v.c(j * 64, (j + 1) * 64)), S["qab"].c(j * 128, (j + 1) * 128), ident_b)
                    cp(S["QaT"], bfv(b.v.c(0, 256)))

                def a2(i, n):
                    S = aset(n)
                    QaT, Pa, oat, den = S["QaT"], S["Pa"], S["oat"], S["den"]
                    blocks = [(0, mmeta)]
                    if n >= 2:
                        blocks.append((n - 1, mprev))
                    if n >= 1:
                        blocks.append((n, None))
                    if n <= 15:
                        blocks.append((n + 1, mnext))
                    ov = [banks[4].new(), banks[5].new()]
                    steps = [(bi, m, mask, g) for bi, (m, mask) in enumerate(blocks) for g in range(2)]

                    def sc_(k):
                        bi, m, mask, g = steps[k]
                        bs = banks[2 + k % 2].new()
                        if mask is not None:
                            mm(bs.v.r("p (j q) -> p j q", j=4), ident_b,
                               mask.w(mask.ap.unsqueeze(1).to_broadcast([128, 4, 128])), start=bs.st(), stop=False)
                        for j in range(4):
                            mm(bs.v.c(j * 128, (j + 1) * 128), KaT.p(g * 64, (g + 1) * 64).c(m * 128, (m + 1) * 128),
                               QaT.p(g * 64, (g + 1) * 64).c(j * 128, (j + 1) * 128), start=bs.st(), stop=True)
                        act(Pa[g][bi % 2], bs.v.c(0, 512), AF.Exp, scale=0.125)

                    def pv_(k):
                        bi, m, mask, g = steps[k]
                        pa = Pa[g][bi % 2]
                        for j in range(4):
                            mm(ov[g].v.c(j * 65, (j + 1) * 65), pa.c(j * 128, (j + 1) * 128),
                               Va.c(m * 130 + g * 65, m * 130 + (g + 1) * 65), start=ov[g].st(),
                               stop=(bi == len(blocks) - 1))

                    for k in range(len(steps) + 1):
                        if k < len(steps):
                            sc_(k)
                        if k >= 1:
                            pv_(k - 1)
                        warm(int(os.environ.get('JA', '1')))
                    for g in range(2):
                        o3 = ov[g].v.c(0, 260)
                        o3a = o3.ap.rearrange("p (j d) -> p j d", j=4)
                        dn = den.c(g * 4, (g + 1) * 4)
                        tt(dn, o3.w(o3a[:, :, 64]), esink.c(g * 4, (g + 1) * 4), ALU.add)
                        recip(dn, dn)
                        tt(oat.c(g * 256, (g + 1) * 256).r("p (j d) -> p j d", j=4), o3.w(o3a[:, :, 0:64]),
                           dn.w(dn.ap.unsqueeze(2).to_broadcast([128, 4, 64])), ALU.mult)

                def a3(i, n, chunks=chunks):
                    S = aset(n)
                    nl = n - chunks[0]
                    b = banks[1].new()
                    for k in range(4):
                        tr(bfv(b.v.c(k * 64, (k + 1) * 64)), S["oat"].c(k * 128, (k + 1) * 128), ident_b)
                    o = span(oaT[k].c(nl * 128, (nl + 1) * 128) for k in range(4))
                    o.ap = oaT3[:, :, nl * 128:(nl + 1) * 128]
                    cp(o, bfv(b.v.c(0, 256)).r("p (k t) -> p k t", k=4), eng="act")

                pipeline(chunks, [a0, a0b, a1, a2, a3], [0, 0, 1, 2, 3], first=1, last=(1,))

                chk("att%d" % t)
                Mfirst[0] = True
                cam, cmeta = ca_main(t), (112, 128)
                vm = vo_main(t)
                nmain = vm[1] - vm[0]
                GS = [[sub(RB, TB + (k * 4 + m4) * 2048, TB + (k * 4 + m4 + 1) * 2048, F32) for m4 in range(4)]
                      for k in range(2)]
                t1 = sub(RB, TB + 16384, TB + 18432, F32)
                t2 = sub(RB, TB + 18432, TB + 20480, F32)
                for mg in range(2):
                    wgr = ws.use(w_in_d[l, :, GR + mg * 512:GR + (mg + 1) * 512], 8, 512)
                    wga = ws.use(w_in_d[l, :, GA + mg * 512:GA + (mg + 1) * 512], 8, 512)
                    for m4 in range(4):
                        m = mg * 4 + m4
                        blk = lambda wv, K: [wv.c(k * 512 + m4 * 128, k * 512 + (m4 + 1) * 128) for k in range(K)]
                        bgr, bga = rbank(), rbank()
                        fm_mm(bgr, (0 * 8 + m) * 16, blk(wgr, 8), uT, t, cam, cmeta)
                        fm_mm(bga, (1 * 8 + m) * 16, blk(wga, 8), uT, t, cam, cmeta)
                        act(GS[0][m4].c(0, nmain), bgr.v.c(0, nmain), AF.Sigmoid)
                        act(GS[1][m4].c(0, nmain), bga.v.c(0, nmain), AF.Sigmoid)
                    wro = ws.use(w_ro_d[l, :, mg * 512:(mg + 1) * 512], 8, 512)
                    wao = ws.use(w_ao_d[l, :, mg * 512:(mg + 1) * 512], 4, 512)
                    for m4 in range(4):
                        m = mg * 4 + m4
                        blk = lambda wv, K: [wv.c(k * 512 + m4 * 128, k * 512 + (m4 + 1) * 128) for k in range(K)]
                        byr, bya = rbank(), rbank()
                        fm_mm(byr, (2 * 8 + m) * 16, blk(wro, 8), orT, t, cam, cmeta)
                        fm_mm(bya, (3 * 8 + m) * 16, blk(wao, 4), oaT, t, cam, cmeta)
                        tt(t1.c(0, nmain), GS[0][m4].c(0, nmain), byr.v.c(0, nmain), ALU.mult)
                        tt(t2.c(0, nmain), GS[1][m4].c(0, nmain), bya.v.c(0, nmain), ALU.mult)
                        tt(zT[m].c(vm[0], vm[1]), t1.c(0, nmain), t2.c(0, nmain), ALU.add)
                if t == 0:
                    sg = sub(RB, TB + 20480, TB + 21504, F32)
                    pr = sub(RB, TB + 21504, TB + 22528, F32)
                    act(sg, MB.v.c(0, 256), AF.Sigmoid)
                    tt(pr, sg, MB.v.c(256, 512), ALU.mult)
                    zm = span(z.c(0, 16) for z in zT)
                    zm.ap = zT3[:, :, 0:16]
                    tt(zm, pr.c(0, 128).r("p (m t) -> p m t", m=8), pr.c(128, 256).r("p (m t) -> p m t", m=8), ALU.add)

                chk("merge%d" % t)
                Mfirst[0] = True
                SSB.new()
                for mg in range(2):
                    wmx = ws.use(w_mx_d[l, :, mg * 512:(mg + 1) * 512], 8, 512, live=1)
                    for m4 in range(4):
                        m = mg * 4 + m4
                        b = rbank()
                        fm_mm(b, m * 16, [wmx.c(k * 512 + m4 * 128, k * 512 + (m4 + 1) * 128) for k in range(8)],
                              zT, t, vm, (0, 16))
                        cp(mixT[m].c(vm[0], vm[1]), b.v.c(0, nmain), eng="act")
                        s_ = sq[m % 2].c(0, nmain)
                        act(s_, mixT[m].c(vm[0], vm[1]), AF.Square)
                        mm(SSB.v.c(0, nmain), ones_b, s_, start=SSB.st(), stop=(m == 7))
                post_residual(t, l, 1, mixT, mixT3)

                chk("mix%d" % t)
                prenorm(t, l, 2)
                Mfirst[0] = True
                for fg in range(8):
                    w1 = ws.use(w_f1_d[l, :, fg * 512:(fg + 1) * 512], 8, 512, live=1)
                    for f4 in range(4):
                        f = fg * 4 + f4
                        b = rbank()
                        fm_mm(b, f * 16, [w1.c(k * 512 + f4 * 128, k * 512 + (f4 + 1) * 128) for k in range(8)],
                              uT, t, cam, cmeta)
                        r_ = ffr[f % 2].c(0, nmain)
                        act(r_, b.v.c(0, nmain), AF.Square)
                        stt(hid[f].c(vm[0], vm[1]), b.v.c(0, nmain), 0.0, r_, ALU.is_gt, ALU.mult)
                if t == 0:
                    r_ = ffr[0]
                    act(r_, MB.v.c(0, 512), AF.Square)
                    hm = span(hh.c(0, 16) for hh in hid)
                    hm.ap = hid3[:, :, 0:16]
                    stt(hm, MB.v.c(0, 512).r("p (f t) -> p f t", f=32), 0.0, r_.r("p (f t) -> p f t", f=32),
                        ALU.is_gt, ALU.mult)
                Mfirst[0] = True
                SSB.new()
                for mp in range(4):
                    bb = [rbank(), rbank()]
                    for fq in range(4):
                        w2 = ws.use(w_f2_d[l, fq * 1024:(fq + 1) * 1024, mp * 256:(mp + 1) * 256], 8, 256, live=1)
                        for f8 in range(8):
                            f = fq * 8 + f8
                            for m2 in range(2):
                                m = mp * 2 + m2
                                L = w2.c(f8 * 256 + m2 * 128, f8 * 256 + (m2 + 1) * 128)
                                mm(bb[m2].v.c(0, nmain), L, hid[f].c(vm[0], vm[1]), start=bb[m2].st(), stop=(f == 31))
                                if t == 0:
                                    mm(MB.v.c(m * 16, (m + 1) * 16), L, hid[f].c(0, 16), start=mstart(), stop=(f == 31))
                    for m2 in range(2):
                        m = mp * 2 + m2
                        cp(mixT[m].c(vm[0], vm[1]), bb[m2].v.c(0, nmain), eng="act")
                        s_ = sq[m % 2].c(0, nmain)
                        act(s_, mixT[m].c(vm[0], vm[1]), AF.Square)
                        mm(SSB.v.c(0, nmain), ones_b, s_, start=SSB.st(), stop=(m == 7))
                post_residual(t, l, 3, mixT, mixT3)
                chk("ffn%d" % t)

        P.muted = False
        P.phase = 3
        ostage = [sub(SC, 0, 4096, F32), sub(SC, 4096, 8192, F32)]
        for n in range(1, NCH):
            st = ostage[n % 2]
            for half in range(2):
                b = rbank()
                for j in range(4):
                    c = half * 4 + j
                    tr(b.v.c(j * 128, (j + 1) * 128), hT[c].c(n * 128, (n + 1) * 128), ident_f)
                cp(st.c(half * 512, (half + 1) * 512), b.v.c(0, 512), eng=("act" if half else "dve"))
            dma("sp", y_d[(n - 1) * 128: n * 128, :], st, slot=("o", n % 2), out_tok=V(None, "d_y", n, n + 1, 1))
        P.op("sp", lambda e: e.nop(), reads=[V(None, "d_y", 0, NCH + 1, 1)])

    print('sbuf bytes remaining', nc.sbuf_bytes_remaining)
    rec = WS(None)
    P0 = P
    emit_all(rec)
    P = Prog()
    for b in banks:
        b.first = True
    ring[0] = 0
    lastpe.clear()
    ws = WS(rec.rec)
    emit_all(ws)
    P.emit(nc, es)
    es.close()
    return nc, P


def _consts():
    f32 = np.float32
    p = np.arange(128)
    pos = (np.arange(NCH)[None, :] * 128 + p[:, None] - 112).astype(f32)

    def tab(theta, rot):
        fr = np.power(f32(theta), -np.arange(0, rot, 2, dtype=f32) / f32(rot)).astype(f32)
        ang = (pos[:, :, None] * fr[None, None, :]).astype(f32)
        return (np.cos(ang).astype(f32).reshape(128, -1), np.sin(ang).astype(f32).reshape(128, -1))

    cosr, sinr = tab(10000.0, 64)
    cosa, sina = tab(500000.0, 16)
    j = p[:, None]
    i = p[None, :]
    rfp = (-(np.minimum(i, j) + 1)).astype(f32)
    rbm = np.maximum(j - i, 0).astype(f32)
    pcols = np.stack([p + 1, 128 - p, 127 - p, p], axis=1).astype(f32)
    k, q = j, i
    mprev = np.where(k >= q, 0.0, NEG).astype(f32)
    mnext = np.where(k <= q, 0.0, NEG).astype(f32)
    mmeta = np.where(k >= 112, 0.0, NEG).astype(f32) + np.zeros((128, 128), f32)
    return dict(cosr=cosr, sinr=sinr, cosa=cosa, sina=sina, rfp=rfp, rbm=rbm, pcols=pcols,
                mprev=mprev, mnext=mnext, mmeta=mmeta, ident=np.eye(128, dtype=f32))


def _perm():
    cols = []
    for hp in range(4):
        a, b = 2 * hp, 2 * hp + 1
        for h in (a, b):
            cols += list(range(h * 64, (h + 1) * 64))
        for h in (a, b):
            cols += list(range(512 + h * 64, 512 + (h + 1) * 64))
        for h in (a, b):
            cols += list(range(1024 + h * 128, 1024 + (h + 1) * 128))
        for h in (a, b):
            cols += list(range(2048 + h * 128, 2048 + (h + 1) * 128))
    for j in range(4):
        cols += list(range(3072 + j * 64, 3072 + (j + 1) * 64))
        cols += list(range(3072 + (4 + j) * 64, 3072 + (5 + j) * 64))
    cols += list(range(3584, 5888))
    return np.asarray(cols)


_CACHE = {}


def host_inputs(x, meta_tokens, w_in, w_ret_o, w_att_o, w_mix_o, w_ff1, w_ff2, norm_mix_pre, norm_mix_post,
                norm_ff_pre, norm_ff_post, ret_decay, attn_sink, ncores=8):
    f32 = np.float32
    A = lambda a: np.ascontiguousarray(np.asarray(a, dtype=f32))
    w_in_p = np.ascontiguousarray(A(w_in)[:, :, _perm()])
    norms = np.stack([A(norm_mix_pre), A(norm_mix_post), A(norm_ff_pre), A(norm_ff_post)], axis=1)
    gains = np.ascontiguousarray(norms.reshape(4, 4, 8, 128).transpose(3, 0, 1, 2).reshape(128, 128))
    shared = dict(meta=A(meta_tokens), w_in=w_in_p, w_ret_o=A(w_ret_o), w_att_o=A(w_att_o), w_mix_o=A(w_mix_o),
                  w_ff1=A(w_ff1), w_ff2=A(w_ff2), gains=gains, ret_decay=A(ret_decay).reshape(64),
                  attn_sink=A(attn_sink).reshape(32))
    shared.update(_consts())
    xs = A(x)
    return [dict(shared, x=xs[b]) for b in range(ncores)]


def kernel(**inputs):
    if "nc" not in _CACHE:
        _CACHE["nc"] = build()[0]
    nc = _CACHE["nc"]
    in_maps = host_inputs(**inputs)
    res = run_bass_kernel_spmd(nc, in_maps, core_ids=list(range(8)))
    return np.stack([np.asarray(r["y"], dtype=np.float32) for r in res.results], axis=0)
```

```python
import math
import os
from contextlib import ExitStack
import numpy as np
import concourse.bass as bass
import concourse.mybir as mybir
from concourse.bass_utils import run_bass_kernel_spmd

F32 = mybir.dt.float32
BF16 = mybir.dt.bfloat16
AF = mybir.ActivationFunctionType
ALU = mybir.AluOpType

NCH = 17
T = NCH * 128
DEPTH = 4
EPS = 1e-6
NEG = -30000.0
TILES = [[0, 1, 2, 3, 4], [5, 6, 7, 8], [9, 10, 11, 12], [13, 14, 15, 16]]
HU = lambda hp: hp * 768
QA, KV, GR, GA = 3072, 3584, 3840, 4864


class V:
    __slots__ = ("ap", "reg", "lo", "hi", "esz")

    def __init__(s, ap, reg, lo, hi, esz):
        s.ap, s.reg, s.lo, s.hi, s.esz = ap, reg, lo, hi, esz

    def c(s, a, b):
        if s.reg.startswith("ps"):
            return V(s.ap[:, a:b], s.reg, s.lo, s.hi, s.esz)
        return V(s.ap[:, a:b], s.reg, s.lo + a * s.esz, s.lo + b * s.esz, s.esz)

    def p(s, a, b):
        return V(s.ap[a:b], s.reg, s.lo, s.hi, s.esz)

    def w(s, ap):
        return V(ap, s.reg, s.lo, s.hi, s.esz)

    def r(s, pat, **kw):
        return V(s.ap.rearrange(pat, **kw), s.reg, s.lo, s.hi, s.esz)


def span(vs):
    vs = list(vs)
    return V(None, vs[0].reg, min(v.lo for v in vs), max(v.hi for v in vs), vs[0].esz)


class Op:
    __slots__ = ("idx", "eng", "fn", "dma", "slot", "ndma", "phase", "deps", "sig", "signo", "semkey")


class Prog:
    ENGS = ("pe", "act", "dve", "pool", "sp")

    def __init__(s):
        s.ops = []
        s.segs = {}
        s.phase = 0
        s.muted = False
        s.marks = []

    def _access(s, v, idx, ekey, write, deps):
        write = write or v.reg.startswith("ps")
        segs = s.segs.setdefault(v.reg, [])
        lo, hi = v.lo, v.hi
        for x in (lo, hi):
            for i, sg in enumerate(segs):
                if sg[0] < x < sg[1]:
                    segs.insert(i + 1, [x, sg[1], sg[2], dict(sg[3])])
                    sg[1] = x
                    break
        cov = [sg for sg in segs if sg[0] >= lo and sg[1] <= hi]
        cur = lo
        new = []
        for sg in cov:
            if sg[0] > cur:
                new.append([cur, sg[0], None, {}])
            cur = sg[1]
        if cur < hi:
            new.append([cur, hi, None, {}])
        if new:
            segs.extend(new)
            segs.sort(key=lambda g: g[0])
        for sg in cov + new:
            if sg[2] is not None:
                deps.add(sg[2])
            if write:
                deps.update(sg[3].values())
                sg[2] = idx
                sg[3] = {}
            else:
                sg[3][ekey] = idx

    def op(s, eng, fn, reads=(), writes=(), dma=False, slot=None, ndma=1, force=()):
        if s.muted:
            return None
        o = Op()
        o.idx = len(s.ops)
        o.eng, o.fn, o.dma, o.slot, o.ndma, o.phase = eng, fn, dma, slot, ndma, s.phase
        o.sig = dma
        o.signo = 0
        ekey = ("dma", o.idx) if dma else eng
        deps = set()
        for v in reads:
            s._access(v, o.idx, ekey, False, deps)
        for v in writes:
            s._access(v, o.idx, ekey, True, deps)
        deps.discard(o.idx)
        best = {}
        keep = []
        for d in deps:
            dop = s.ops[d]
            if dop.dma:
                keep.append(d)
            elif dop.eng == "pe" and eng == "pe" and not dma:
                continue
            elif best.get(dop.eng, -1) < d:
                best[dop.eng] = d
        keep.extend(best.values())
        for d in force:
            if d is not None and d not in keep:
                keep.append(d)
        for d in keep:
            s.ops[d].sig = True
        o.deps = keep
        s.ops.append(o)
        return o

    def emit(s, nc, es):
        cnt = {}
        for o in s.ops:
            if o.dma:
                o.semkey = ("dma", o.slot)
                cnt[o.semkey] = cnt.get(o.semkey, 0) + 16 * o.ndma
                o.signo = cnt[o.semkey]
            elif o.sig:
                o.semkey = (o.eng, o.phase)
                cnt[o.semkey] = cnt.get(o.semkey, 0) + 1
                o.signo = cnt[o.semkey]
        sems = {k: es.enter_context(nc.semaphore("s%d" % i)) for i, k in enumerate(cnt)}
        s.nsems = len(sems)
        streams = {e: [o for o in s.ops if o.eng == e] for e in s.ENGS}
        ops = s.ops

        def run(en):
            def f(e):
                lastw = {}
                for o in streams[en]:
                    for d in o.deps:
                        dop = ops[d]
                        if lastw.get(dop.semkey, 0) < dop.signo:
                            lastw[dop.semkey] = dop.signo
                            e.wait_ge(sems[dop.semkey], dop.signo)
                    r = o.fn(e)
                    if o.sig:
                        if o.dma:
                            rs = r if isinstance(r, (list, tuple)) else [r]
                            assert len(rs) == o.ndma
                            for ri in rs:
                                ri.then_inc(sems[o.semkey], 16)
                        else:
                            r.then_inc(sems[o.semkey], 1)
            return f

        block = es.enter_context(nc.Block())
        block.tensor(run("pe"))
        block.scalar(run("act"))
        block.vector(run("dve"))
        block.gpsimd(run("pool"))
        block.sync(run("sp"))


class Bank:
    def __init__(s, v):
        s.v = v
        s.first = True

    def new(s):
        s.first = True
        return s

    def st(s):
        f = s.first
        s.first = False
        return f


def build(layers=(0, 1, 2, 3), stop=None):
    nc = bass.Bass("TRN2", target_bir_lowering=False)
    es = ExitStack()
    P = Prog()

    def din(name, shape):
        return nc.dram_tensor(name, list(shape), F32, kind="ExternalInput").ap()

    x_d = din("x", [2048, 1024])
    meta_d = din("meta", [16, 1024])
    w_in_d = din("w_in", [4, 1024, 5888])
    w_ro_d = din("w_ret_o", [4, 1024, 1024])
    w_ao_d = din("w_att_o", [4, 512, 1024])
    w_mx_d = din("w_mix_o", [4, 1024, 1024])
    w_f1_d = din("w_ff1", [4, 1024, 4096])
    w_f2_d = din("w_ff2", [4, 4096, 1024])
    gains_d = din("gains", [128, 128])
    rdec_d = din("ret_decay", [64])
    sink_d = din("attn_sink", [32])
    cosr_d = din("cosr", [128, NCH * 32])
    sinr_d = din("sinr", [128, NCH * 32])
    cosa_d = din("cosa", [128, NCH * 8])
    sina_d = din("sina", [128, NCH * 8])
    rfp_d = din("rfp", [128, 128])
    rb_d = din("rbm", [128, 128])
    pcols_d = din("pcols", [128, 4])
    mprev_d = din("mprev", [128, 128])
    mnext_d = din("mnext", [128, 128])
    mmeta_d = din("mmeta", [128, 128])
    ident_d = din("ident", [128, 128])
    y_d = nc.dram_tensor("y", [2048, 1024], F32, kind="ExternalOutput").ap()

    def DR(name):
        return V(None, "d_" + name, 0, 1, 1)

    def sb(name, ncols, dt):
        t = es.enter_context(nc.sbuf_tensor("s_" + name, [128, ncols], dt))
        esz = 4 if dt == F32 else 2
        return V(t[:], name, 0, ncols * esz, esz)

    def sub(rv, blo, bhi, dt=BF16):
        ap = rv.ap[:, blo // 2: bhi // 2]
        if dt == F32:
            ap = ap.bitcast(F32)
        return V(ap, rv.reg, rv.lo + blo, rv.lo + bhi, 4 if dt == F32 else 2)

    hT_all = sb("hT", 8 * T, F32)
    hT = [hT_all.c(c * T, (c + 1) * T) for c in range(8)]
    hT3 = hT_all.ap.rearrange("p (c t) -> p c t", c=8)
    cosr = sb("cosr", NCH * 32, F32)
    sinr = sb("sinr", NCH * 32, F32)
    cosa = sb("cosa", NCH * 8, F32)
    sina = sb("sina", NCH * 8, F32)
    rfp = sb("rfp", 128, F32)
    rbm = sb("rbm", 128, F32)
    pcols = sb("pcols", 4, F32)
    DT = sb("DT", 8 * 128, F32)
    ident_f = sb("identf", 128, F32)
    ident_b = sb("identb", 128, BF16)
    ones_b = sb("onesb", 128, BF16)
    mprev = sb("mprev", 128, BF16)
    mnext = sb("mnext", 128, BF16)
    mmeta = sb("mmeta", 128, BF16)
    gains = sb("gains", 128, F32)
    rdec = sb("rdec", 64, F32)
    sinkb = sb("sinkb", 32, F32)
    lg = sb("lg", 16, F32)
    nhalf = sb("nhalf", 1, F32)
    dcol = sb("dcol", 56, F32)
    qfc, qbc, kdfc, kdbc, cdf, cdb, esink = [dcol.c(i * 8, (i + 1) * 8) for i in range(7)]
    KaT = sb("KaT", T, BF16)
    Va = sb("Va", NCH * 130, BF16)
    SB = sb("SB", 9 * 1024, BF16)
    Srun = sb("Srun", 1024, F32)
    U_all = sb("U", 8 * 640, BF16)
    uT = [U_all.c(c * 640, (c + 1) * 640) for c in range(8)]
    RA = sb("RA", 16896 // 2, BF16)
    RB = sb("RB", 33792 // 2, BF16)
    SC = sb("SC", 8192 // 2, BF16)
    NW = 3
    wslots = [sb("w%d" % i, 4096, BF16) for i in range(NW)]

    orT = [sub(RA, k * 1280, (k + 1) * 1280) for k in range(8)]
    oaT = [sub(RA, 10240 + k * 1280, 10240 + (k + 1) * 1280) for k in range(4)]
    oaT3 = sub(RA, 10240, 15360).ap.rearrange("p (k t) -> p k t", k=4)
    mixT_all = sub(RA, 0, 16896, F32)
    mixT = [mixT_all.c(m * 528, (m + 1) * 528) for m in range(8)]
    mixT3 = mixT_all.ap.rearrange("p (m t) -> p m t", m=8)
    zT_all = sub(RB, 0, 8448)
    zT = [zT_all.c(m * 528, (m + 1) * 528) for m in range(8)]
    zT3 = zT_all.ap.rearrange("p (m t) -> p m t", m=8)
    hid_all = sub(RB, 0, 33792)
    hid = [hid_all.c(f * 528, (f + 1) * 528) for f in range(32)]
    hid3 = hid_all.ap.rearrange("p (f t) -> p f t", f=32)
    TB = 8448
    stage = [sub(RB, 0, 4096, F32), sub(RB, 4096, 8192, F32)]
    sq = [sub(SC, 0, 1024), sub(SC, 1024, 2048)]
    rt = sub(SC, 2048, 4160, F32)
    rstd = sub(SC, 4160, 6272, F32)
    scx = sub(SC, 6272, 8192, F32)
    ffr = [sub(SC, 0, 2048, F32), sub(SC, 2048, 4096, F32)]

    banks = []
    for i in range(8):
        t = es.enter_context(nc.psum_tensor("ps%d" % i, [128, 512], F32))
        banks.append(Bank(V(t[:], "ps%d" % i, 0, 2048, 4)))
    SSB, MB = banks[6], banks[7]
    ring = [0]

    def rbank():
        b = banks[ring[0] % 6]
        ring[0] += 1
        return b.new()

    def bfv(v):
        return V(v.ap.bitcast(BF16), v.reg, v.lo, v.hi, 2)

    def rw(x):
        return [x] if isinstance(x, V) else list(x)

    lastpe = {}

    def mm(out, lhsT, rhs, start, stop):
        sg_ = (lhsT.ap.base_partition(), lhsT.ap.shape[0])
        prev = lastpe.get(out.reg)
        force = [prev[1]] if (prev is not None and prev[0] != sg_) else []
        o = P.op("pe", lambda e: e.matmul(out.ap, lhsT.ap, rhs.ap, start=start, stop=stop, skip_group_check=True),
                 reads=[lhsT, rhs], writes=[out], force=force)
        if o is not None:
            lastpe[out.reg] = (sg_, o.idx)

    def tr(out, in_, idn):
        sg_ = (0, 128)
        prev = lastpe.get(out.reg)
        force = [prev[1]] if (prev is not None and prev[0] != sg_) else []
        o = P.op("pe", lambda e: e.transpose(out.ap, in_.ap, idn.ap), reads=[in_, idn], writes=[out], force=force)
        if o is not None:
            lastpe[out.reg] = (sg_, o.idx)

    def act(out, in_, func, scale=None, bias=None, accum=None, eng="act"):
        kw = {}
        rd = [in_]
        wr = [out]
        if scale is not None:
            if isinstance(scale, V):
                kw["scale"] = scale.ap
                rd.append(scale)
            else:
                kw["scale"] = float(scale)
        if bias is not None:
            if isinstance(bias, V):
                kw["bias"] = bias.ap
                rd.append(bias)
            else:
                kw["bias"] = float(bias)
        if accum is not None:
            kw["accum_out"] = accum.ap
            wr.append(accum)
        P.op("act", lambda e: e.activation(out.ap, in_.ap, func, **kw), reads=rd, writes=wr)

    def tt(out, a, b, op, eng="dve"):
        P.op(eng, lambda e: e.tensor_tensor(out.ap, a.ap, b.ap, op), reads=[a, b], writes=[out])

    def ts(out, a, s1, s2, op0, op1=None, eng="dve"):
        rd = [a]
        a1 = s1.ap if isinstance(s1, V) else s1
        a2 = s2.ap if isinstance(s2, V) else s2
        if isinstance(s1, V):
            rd.append(s1)
        if isinstance(s2, V):
            rd.append(s2)
        if op1 is None:
            P.op(eng, lambda e: e.tensor_scalar(out.ap, a.ap, a1, None, op0), reads=rd, writes=[out])
        else:
            P.op(eng, lambda e: e.tensor_scalar(out.ap, a.ap, a1, a2, op0, op1), reads=rd, writes=[out])

    def stt(out, a, sc, b, op0, op1, eng="dve"):
        rd = [a, b]
        a1 = sc.ap if isinstance(sc, V) else sc
        if isinstance(sc, V):
            rd.append(sc)
        P.op(eng, lambda e: e.scalar_tensor_tensor(out.ap, a.ap, a1, b.ap, op0, op1), reads=rd, writes=[out])

    def cp(out, in_, eng="dve"):
        if eng == "act":
            act(out, in_, AF.Copy)
        else:
            P.op(eng, lambda e: e.tensor_copy(out.ap, in_.ap), reads=[in_], writes=[out])

    def recip(out, in_):
        P.op("dve", lambda e: e.reciprocal(out.ap, in_.ap), reads=[in_], writes=[out])

    def rstd_from(out, ssq, n_scale, tmp):
        act(tmp, ssq, AF.Sqrt, scale=n_scale, bias=EPS)
        recip(out, tmp)

    def memset(v, val, eng="dve"):
        P.op(eng, lambda e: e.memset(v.ap, val), writes=[v])

    def dma(q, out, in_ap_or_v, slot, in_tok=None, out_tok=None):
        if isinstance(out, V):
            o_ap, wr = out.ap, [out]
        else:
            o_ap, wr = out, [out_tok]
        if isinstance(in_ap_or_v, V):
            i_ap, rd = in_ap_or_v.ap, [in_ap_or_v]
        else:
            i_ap, rd = in_ap_or_v, ([in_tok] if in_tok else [])
        P.op(q, lambda e: e.dma_start(out=o_ap, in_=i_ap), reads=rd, writes=wr, dma=True, slot=slot)

    class WS:
        def __init__(s, plan=None):
            s.plan = plan
            s.rec = []
            s.i = 0
            s.issued = 0

        def use(s, src, nk, ncols, live=2):
            if s.plan is None:
                s.rec.append((src, nk, ncols))
                return wslots[0]
            i = s.i
            s.i += 1
            if P.muted:
                return wslots[0]
            while s.issued < min(len(s.plan), i + NW + 1 - live):
                j = s.issued
                sr, k2, n2 = s.plan[j]
                sl = wslots[j % NW]
                dma("pool", sl.c(0, k2 * n2).w(sl.ap[:, 0:k2 * n2].rearrange("p (k n) -> p k n", k=k2)),
                    sr.rearrange("(k p) n -> p k n", p=128), slot=("w", j % NW))
                s.issued += 1
            return wslots[i % NW]

    def tc0(t):
        return TILES[t][0] * 128

    def g_main(t):
        return (128, 640) if t == 0 else (tc0(t), tc0(t) + 512)

    def ca_main(t):
        return (128, 640) if t == 0 else (0, 512)

    def vo_main(t):
        return (16, 528) if t == 0 else (0, 512)

    def gidx(l, j):
        return (l * 4 + j) * 8

    def emit_all(ws):
        Mfirst = [True]

        def mstart():
            f = Mfirst[0]
            Mfirst[0] = False
            return f

        P.phase = 0
        for (v, d, q) in ((cosr, cosr_d, "sp"), (sinr, sinr_d, "sp"), (cosa, cosa_d, "sp"), (sina, sina_d, "sp"),
                          (rfp, rfp_d, "sp"), (rbm, rb_d, "sp"), (pcols, pcols_d, "sp"), (ident_f, ident_d, "sp"),
                          (gains, gains_d, "sp"), (ident_b, ident_d, "pool"), (mprev, mprev_d, "pool"),
                          (mnext, mnext_d, "pool"), (mmeta, mmeta_d, "pool")):
            dma(q, v, d, slot=("c", v.reg))
        dma("sp", rdec, rdec_d.partition_broadcast(128), slot=("c", "rdec"))
        dma("sp", sinkb, sink_d.partition_broadcast(128), slot=("c", "sink"))
        memset(ones_b, 1.0)
        memset(nhalf, -0.5)
        memset(Va, 1.0)
        memset(U_all, 0.0)

        for n in range(NCH):
            st = stage[n % 2]
            if n == 0:
                memset(st, 0.0)
                dma("sp", st.p(112, 128), meta_d, slot=("st", 0))
            else:
                dma("sp", st, x_d[(n - 1) * 128: n * 128, :], slot=("st", n % 2))
            for half in range(2):
                b = rbank()
                for j in range(4):
                    c = half * 4 + j
                    tr(b.v.c(j * 128, (j + 1) * 128), st.c(c * 128, (c + 1) * 128), ident_f)
                o = span(hT[half * 4 + j].c(n * 128, (n + 1) * 128) for j in range(4))
                o.ap = hT3[:, half * 4:(half + 1) * 4, n * 128:(n + 1) * 128]
                cp(o, b.v.r("p (j t) -> p j t", j=4), eng=("act" if half else "dve"))

        def prenorm(t, l, j):
            pieces = [g_main(t)] + ([(112, 128)] if t == 0 else [])
            if t == 0:
                z = span(u.c(0, 112) for u in uT)
                z.ap = U_all.ap.rearrange("p (c t) -> p c t", c=8)[:, :, 0:112]
                memset(z, 0.0)
            for (c0, c1) in pieces:
                n = c1 - c0
                ssb = SSB.new() if n > 16 else rbank()
                for c in range(8):
                    s_ = sq[c % 2].c(0, n)
                    act(s_, hT[c].c(c0, c1), AF.Square)
                    mm(ssb.v.c(0, n), ones_b, s_, start=ssb.st(), stop=(c == 7))
                rstd_from(rstd.c(0, n), ssb.v.c(0, n), 1.0 / 1024, rt.c(0, n))
                for c in range(8):
                    g = gains.c(gidx(l, j) + c, gidx(l, j) + c + 1)
                    stt(uT[c].c(c0 - tc0(t), c1 - tc0(t)), hT[c].c(c0, c1), g, rstd.c(0, n), ALU.mult, ALU.mult)

        def post_residual(t, l, j, outT, outT3):
            a, b = vo_main(t)
            n = b - a
            ga, gb = g_main(t)
            rstd_from(rstd.c(0, n), SSB.v.c(0, n), 1.0 / 1024, rt.c(0, n))
            for m in range(8):
                g = gains.c(gidx(l, j) + m, gidx(l, j) + m + 1)
                tt(outT[m].c(a, b), outT[m].c(a, b), rstd.c(0, n), ALU.mult)
                stt(hT[m].c(ga, gb), outT[m].c(a, b), g, hT[m].c(ga, gb), ALU.mult, ALU.add)
            if t == 0:
                om = span(o.c(0, 16) for o in outT)
                om.ap = outT3[:, :, 0:16]
                cp(om, MB.v.c(0, 128).r("p (m t) -> p m t", m=8), eng="act")
                sqm = sub(SC, 0, 256)
                act(sqm, MB.v.c(0, 128), AF.Square)
                b2 = rbank()
                for m in range(8):
                    mm(b2.v.c(0, 16), ones_b, sqm.c(m * 16, (m + 1) * 16), start=b2.st(), stop=(m == 7))
                rtm = scx.c(0, 16)
                rsm = scx.c(16, 32)
                rstd_from(rsm, b2.v.c(0, 16), 1.0 / 1024, rtm)
                t1 = scx.c(32, 160)
                t13 = t1.r("p (m t) -> p m t", m=8)
                tt(t13, om, rsm.w(rsm.ap.unsqueeze(1).to_broadcast([128, 8, 16])), ALU.mult)
                gv = gains.c(gidx(l, j), gidx(l, j) + 8)
                tt(t13, t13, gv.w(gv.ap.unsqueeze(2).to_broadcast([128, 8, 16])), ALU.mult)
                hm = span(h.c(112, 128) for h in hT)
                hm.ap = hT3[:, :, 112:128]
                tt(hm, hm, t13, ALU.add)

        def fm_mm(bank, mcol, lhs_list, rhs_list, t, rng_main, rng_meta, last=True):
            K = len(lhs_list)
            a, b = rng_main
            for k in range(K):
                mm(bank.v.c(0, b - a), lhs_list[k], rhs_list[k].c(a, b), start=bank.st(), stop=(last and k == K - 1))
            if t == 0 and mcol is not None:
                ma, mb_ = rng_meta
                for k in range(K):
                    mm(MB.v.c(mcol, mcol + 16), lhs_list[k], rhs_list[k].c(ma, mb_), start=mstart(),
                       stop=(last and k == K - 1))

        def rope64(dst, src, n, nq, A, B1, B2, eng="dve"):
            cs = cosr.c(n * 32, (n + 1) * 32)
            sn = sinr.c(n * 32, (n + 1) * 32)
            s4 = src.ap.rearrange("p (q a f) -> p q a f", q=nq, a=2)
            d4 = dst.ap.rearrange("p (q a f) -> p q a f", q=nq, a=2)
            A4 = A.ap.rearrange("p (q a f) -> p q a f", q=nq, a=2)
            tt(A.w(A4), src.w(s4), cs.w(cs.ap.unsqueeze(1).unsqueeze(1).to_broadcast([128, nq, 2, 32])), ALU.mult, eng=eng)
            snb = sn.w(sn.ap.unsqueeze(1).to_broadcast([128, nq, 32]))
            b1 = B1.r("p (q f) -> p q f", q=nq)
            b2 = B2.r("p (q f) -> p q f", q=nq)
            tt(b1, src.w(s4[:, :, 1, :]), snb, ALU.mult, eng=eng)
            tt(b2, src.w(s4[:, :, 0, :]), snb, ALU.mult, eng=eng)
            tt(dst.w(d4[:, :, 0, :]), A.w(A4[:, :, 0, :]), b1, ALU.subtract, eng=eng)
            tt(dst.w(d4[:, :, 1, :]), A.w(A4[:, :, 1, :]), b2, ALU.add, eng=eng)

        def warm(k):
            for _ in range(k):
                mm(MB.v.c(0, 512), ones_b, U_all.c(0, 512), start=True, stop=True)

        def pipeline(items, stages, skews, first=0, junk=0, last=()):
            n = len(items)
            mid = [j for j in range(first, len(stages)) if j not in last]
            order = list(range(first)) + sorted(mid, key=lambda j: (-skews[j], j)) + list(last)
            for s_ in range(n + max(skews)):
                for j in order:
                    fn, sk = stages[j], skews[j]
                    i = s_ - sk
                    if 0 <= i < n:
                        fn(i, items[i])
                warm(junk)

        def chk(name):
            P.marks.append((name, sum(1 for o in P.ops if o.eng == "pe")))
            if stop == name:
                P.muted = True

        for l in layers:
            P.phase = l
            act(lg, rdec.c(l * 16, (l + 1) * 16), AF.Exp)
            ts(lg, lg, -1.0, None, ALU.mult)
            tA = sub(RB, TB, TB + 512, F32)
            tB = sub(RB, TB + 512, TB + 1024, F32)
            for h in range(8):
                ts(tA, rfp, lg.c(h, h + 1), None, ALU.mult)
                stt(tB, rbm, lg.c(8 + h, 9 + h), tA, ALU.mult, ALU.add)
                act(DT.c(h * 128, (h + 1) * 128), tB, AF.Exp)
            l8 = math.log(0.125)
            act(qfc, lg.c(0, 8), AF.Exp, scale=pcols.c(0, 1), bias=l8)
            act(qbc, lg.c(8, 16), AF.Exp, scale=pcols.c(1, 2), bias=l8)
            act(kdfc, lg.c(0, 8), AF.Exp, scale=pcols.c(2, 3))
            act(kdbc, lg.c(8, 16), AF.Exp, scale=pcols.c(3, 4))
            act(cdf, lg.c(0, 8), AF.Exp, scale=128.0)
            act(cdb, lg.c(8, 16), AF.Exp, scale=128.0)
            act(esink, sinkb.c(l * 8, (l + 1) * 8), AF.Exp)

            chk("L%d_start" % l)
            memset(Srun, 0.0)
            for t in (3, 2, 1, 0):
                prenorm(t, l, 0)
                chunks = TILES[t]
                wkv = ws.use(w_in_d[l, :, KV:KV + 256], 8, 256, live=1)
                for n in chunks:
                    nl = n - chunks[0]
                    base = TB + (n % 2) * 2048
                    xa = sub(RB, base, base + 1024, F32)
                    rA = sub(RB, base + 1024, base + 1152, F32)
                    rB1 = sub(RB, base + 1152, base + 1216, F32)
                    rB2 = sub(RB, base + 1216, base + 1280, F32)
                    kab = sub(RB, base + 1280, base + 1536)
                    b = banks[4].new()
                    for c in range(8):
                        mm(b.v.c(0, 256), uT[c].c(nl * 128, (nl + 1) * 128), wkv.c(c * 256, (c + 1) * 256),
                           start=b.st(), stop=(c == 7))
                    cp(xa, b.v.c(0, 256), eng="act")
                    xk = xa.c(0, 128)
                    x4 = xk.ap.rearrange("p (h d) -> p h d", h=2)
                    x16 = xk.w(x4[:, :, 0:16].rearrange("p h (a f) -> p h a f", a=2))
                    cs = cosa.c(n * 8, (n + 1) * 8)
                    sn = sina.c(n * 8, (n + 1) * 8)
                    tt(rA.r("p (h a f) -> p h a f", h=2, a=2), x16,
                       cs.w(cs.ap.unsqueeze(1).unsqueeze(1).to_broadcast([128, 2, 2, 8])), ALU.mult)
                    snb = sn.w(sn.ap.unsqueeze(1).to_broadcast([128, 2, 8]))
                    tt(rB1.r("p (h f) -> p h f", h=2), xk.w(x4[:, :, 8:16]), snb, ALU.mult)
                    tt(rB2.r("p (h f) -> p h f", h=2), xk.w(x4[:, :, 0:8]), snb, ALU.mult)
                    k3 = kab.ap.rearrange("p (h d) -> p h d", h=2)
                    rA4 = rA.ap.rearrange("p (h a f) -> p h a f", h=2, a=2)
                    tt(kab.w(k3[:, :, 0:8]), rA.w(rA4[:, :, 0, :]), rB1.r("p (h f) -> p h f", h=2), ALU.subtract)
                    tt(kab.w(k3[:, :, 8:16]), rA.w(rA4[:, :, 1, :]), rB2.r("p (h f) -> p h f", h=2), ALU.add)
                    cp(kab.w(k3[:, :, 16:64]), xk.w(x4[:, :, 16:64]))
                    van = Va.c(n * 130, (n + 1) * 130)
                    cp(van.w(van.ap.rearrange("p (g d) -> p g d", g=2)[:, :, 0:64]),
                       xa.c(128, 256).r("p (g d) -> p g d", g=2), eng="act")
                    b2 = banks[5].new()
                    tr(bfv(b2.v.c(0, 64)), kab, ident_b)
                    cp(KaT.c(n * 128, (n + 1) * 128), bfv(b2.v.c(0, 64)))
                items = [(hp, n) for hp in range(4) for n in reversed(chunks)]
                units = {}

                def p1A(i, it, t=t, chunks=chunks):
                    hp, n = it
                    if n == chunks[-1]:
                        units[hp] = ws.use(w_in_d[l, :, HU(hp) + 128:HU(hp) + 512], 8, 384, live=1)
                    wu = units[hp]
                    nl = n - chunks[0]
                    b = banks[i % 2].new()
                    for c in range(8):
                        mm(b.v.c(0, 384), uT[c].c(nl * 128, (nl + 1) * 128), wu.c(c * 384, (c + 1) * 384),
                           start=b.st(), stop=(c == 7))

                def p1set(i):
                    base = TB + (i % 4) * 3584
                    return dict(x=sub(RB, base, base + 512, F32), A=sub(RB, base + 512, base + 1024, F32),
                                B1=sub(RB, base + 1024, base + 1280, F32), B2=sub(RB, base + 1280, base + 1536, F32),
                                kr=sub(RB, base + 1536, base + 2048, F32), Kd=sub(RB, base + 2048, base + 2560),
                                Vb=sub(RB, base + 2560, base + 3072))

                def p1B(i, it):
                    hp, n = it
                    S = p1set(i)
                    b = banks[i % 2]
                    cp(S["x"], b.v.c(0, 128), eng="act")
                    cp(S["Vb"], b.v.c(128, 384), eng="act")
                    rope64(S["kr"], S["x"], n, 2, S["A"], S["B1"], S["B2"])
                    kr = S["kr"]
                    kb = kr.w(kr.ap.rearrange("p (h d) -> p h d", h=2).unsqueeze(2).to_broadcast([128, 2, 2, 64]))
                    dc = kdbc.c(2 * hp, 2 * hp + 2)
                    tt(S["Kd"].r("p (h a d) -> p h a d", h=2, a=2), kb,
                       dc.w(dc.ap.unsqueeze(2).unsqueeze(3).to_broadcast([128, 2, 2, 64])), ALU.mult)

                def p1D(i, it):
                    S = p1set(i)
                    b = banks[2 + i % 2].new()
                    for j in range(2):
                        mm(b.v.c(j * 128, (j + 1) * 128), S["Kd"].c(j * 128, (j + 1) * 128),
                           S["Vb"].c(j * 128, (j + 1) * 128), start=b.st(), stop=True)

                def p1E(i, it):
                    hp, n = it
                    pp = n % 2
                    b = banks[2 + i % 2]
                    sr = Srun.c(hp * 256, (hp + 1) * 256)
                    slot = SB.c((n // 2) * 1024 + hp * 256, (n // 2) * 1024 + (hp + 1) * 256)
                    cp(slot.p(pp * 64, (pp + 1) * 64), sr.p(pp * 64, (pp + 1) * 64), eng="act")
                    dc = cdb.c(2 * hp, 2 * hp + 2)
                    sr3 = sr.r("p (h e) -> p h e", h=2)
                    tt(sr3, sr3, dc.w(dc.ap.unsqueeze(2).to_broadcast([128, 2, 128])), ALU.mult)
                    tt(sr, sr, b.v.c(0, 256), ALU.add)

                pipeline(items, [p1A, p1B, p1D, p1E], [0, 0, 2, 2], first=2, junk=int(os.environ.get('JP', '0')))
            chk("pass1")

            memset(Srun, 0.0)
            for t in range(4):
                chunks = TILES[t]
                prenorm(t, l, 0)
                items = [(hp, n) for hp in range(4) for n in chunks]
                unitsA, unitsB = {}, {}
                junk = sub(SC, 6272, 7296, F32)

                def rset(i):
                    o = {}
                    eb = (i % 2) * 5120
                    lb = 10240 + (i % 4) * 5696
                    off = [0, 0]

                    def take(name, nb, dt=BF16, late=False):
                        k = 1 if late else 0
                        base = lb if late else eb
                        o[name] = sub(RB, base + off[k], base + off[k] + nb, dt)
                        off[k] += nb
                    take("x", 1024, F32)
                    take("A", 1024, F32)
                    take("B1", 512, F32)
                    take("B2", 512, F32)
                    take("qk", 1024, F32)
                    take("QQ", 512)
                    take("Kdup", 512)
                    take("G", 1024, F32, True)
                    take("Kd", 512, BF16, True)
                    take("Vb", 512, BF16, True)
                    take("QKT", 1024, BF16, True)
                    take("PT", 512, BF16, True)
                    take("Sfb", 512, BF16, True)
                    take("o32", 1024, F32, True)
                    take("og", 512, BF16, True)
                    take("sm", 64, F32, True)
                    assert off[0] == 5120 and off[1] == 5696
                    return o

                def rA_(i, it, chunks=chunks):
                    hp, n = it
                    if n == chunks[0]:
                        unitsA[hp] = ws.use(w_in_d[l, :, HU(hp):HU(hp) + 512], 8, 512)
                        unitsB[hp] = ws.use(w_in_d[l, :, HU(hp) + 512:HU(hp) + 768], 8, 256)
                    wa, wb = unitsA[hp], unitsB[hp]
                    nl = n - chunks[0]
                    b = banks[0].new()
                    for c in range(8):
                        mm(b.v.c(0, 512), uT[c].c(nl * 128, (nl + 1) * 128), wa.c(c * 512, (c + 1) * 512),
                           start=b.st(), stop=(c == 7))
                    b = banks[1].new()
                    for c in range(8):
                        mm(b.v.c(0, 256), uT[c].c(nl * 128, (nl + 1) * 128), wb.c(c * 256, (c + 1) * 256),
                           start=b.st(), stop=(c == 7))

                def hb(col, hp, d):
                    dc = col.c(2 * hp, 2 * hp + 2)
                    return dc.w(dc.ap.unsqueeze(2).to_broadcast([128, 2, d]))

                def rB_(i, it):
                    hp, n = it
                    pp = n % 2
                    S = rset(i)
                    cp(S["x"], banks[0].v.c(0, 256), eng="act")
                    cp(S["Vb"], banks[0].v.c(256, 512), eng="act")
                    act(S["G"], banks[1].v.c(0, 256), AF.Silu)
                    rope64(S["qk"], S["x"], n, 4, S["A"], S["B1"], S["B2"])
                    q = S["qk"].c(0, 128).r("p (h d) -> p h d", h=2)
                    k = S["qk"].c(128, 256)
                    QQ3 = S["QQ"].ap.rearrange("p (h a d) -> p h a d", h=2, a=2)
                    tt(S["QQ"].w(QQ3[:, :, pp, :]), q, hb(qbc, hp, 64), ALU.mult)
                    tt(S["QQ"].w(QQ3[:, :, 1 - pp, :]), q, hb(qfc, hp, 64), ALU.mult)
                    kb = k.w(k.ap.rearrange("p (h d) -> p h d", h=2).unsqueeze(2).to_broadcast([128, 2, 2, 64]))
                    dc = kdfc.c(2 * hp, 2 * hp + 2)
                    tt(S["Kd"].r("p (h a d) -> p h a d", h=2, a=2), kb,
                       dc.w(dc.ap.unsqueeze(2).unsqueeze(3).to_broadcast([128, 2, 2, 64])), ALU.mult, eng="pool")
                    cp(S["Kdup"].r("p (h a d) -> p h a d", h=2, a=2), kb, eng="pool")

                def rC_(i, it):
                    S = rset(i)
                    b = banks[2].new()
                    for j in range(2):
                        tr(bfv(b.v.c(j * 64, (j + 1) * 64)), S["QQ"].c(j * 128, (j + 1) * 128), ident_b)
                    for j in range(2):
                        tr(bfv(b.v.c(128 + j * 64, 128 + (j + 1) * 64)), S["Kdup"].c(j * 128, (j + 1) * 128), ident_b)

                def rC2_(i, it):
                    S = rset(i)
                    cp(S["QKT"], bfv(banks[2].v.c(0, 256)))

                def rD_(i, it):
                    hp, n = it
                    pp = n % 2
                    S = rset(i)
                    hf = 1 - pp
                    b = banks[3].new()
                    QKT = S["QKT"]
                    for j in range(2):
                        mm(b.v.c(256 + j * 128, 256 + (j + 1) * 128), S["Kd"].c(j * 128, (j + 1) * 128),
                           S["Vb"].c(j * 128, (j + 1) * 128), start=b.st(), stop=True)
                    for j in range(2):
                        mm(b.v.c(j * 128, (j + 1) * 128), QKT.p(hf * 64, (hf + 1) * 64).c(256 + j * 128, 256 + (j + 1) * 128),
                           QKT.p(hf * 64, (hf + 1) * 64).c(j * 128, (j + 1) * 128), start=b.st(), stop=True)

                def rE_(i, it):
                    hp, n = it
                    pp = n % 2
                    hf = 1 - pp
                    S = rset(i)
                    b = banks[3]
                    tt(S["PT"], b.v.c(0, 256), DT.c(hp * 256, (hp + 1) * 256), ALU.mult)
                    sr = Srun.c(hp * 256, (hp + 1) * 256)
                    cp(S["Sfb"].p(hf * 64, (hf + 1) * 64), sr.p(hf * 64, (hf + 1) * 64), eng="pool")
                    sr3 = sr.r("p (h e) -> p h e", h=2)
                    tt(sr3, sr3, hb(cdf, hp, 128), ALU.mult)
                    tt(sr, sr, b.v.c(256, 512), ALU.add)

                def rF_(i, it):
                    hp, n = it
                    pp = n % 2
                    hf = 1 - pp
                    S = rset(i)
                    b = banks[4].new()
                    QKT = S["QKT"]
                    slot = SB.c((n // 2) * 1024 + hp * 256, (n // 2) * 1024 + (hp + 1) * 256)
                    for j in range(2):
                        mm(b.v.c(j * 128, (j + 1) * 128), S["PT"].c(j * 128, (j + 1) * 128), S["Vb"].c(j * 128, (j + 1) * 128),
                           start=b.st(), stop=False)
                    for j in range(2):
                        mm(b.v.c(j * 128, (j + 1) * 128), QKT.p(pp * 64, (pp + 1) * 64).c(j * 128, (j + 1) * 128),
                           slot.p(pp * 64, (pp + 1) * 64).c(j * 128, (j + 1) * 128), start=b.st(), stop=False)
                    for j in range(2):
                        mm(b.v.c(j * 128, (j + 1) * 128), QKT.p(hf * 64, (hf + 1) * 64).c(j * 128, (j + 1) * 128),
                           S["Sfb"].p(hf * 64, (hf + 1) * 64).c(j * 128, (j + 1) * 128), start=b.st(), stop=True)

                def rG_(i, it):
                    S = rset(i)
                    b = banks[4]
                    sm = S["sm"]
                    s1, s2, mean, msq, var, rs = [sm.c(2 * j, 2 * j + 2) for j in range(6)]
                    for j in range(2):
                        act(S["o32"].c(j * 128, (j + 1) * 128), b.v.c(j * 128, (j + 1) * 128), AF.Copy, accum=s1.c(j, j + 1))
                    for j in range(2):
                        act(junk.c(0, 128), S["o32"].c(j * 128, (j + 1) * 128), AF.Square, accum=s2.c(j, j + 1))
                    ts(mean, s1, 1.0 / 128, None, ALU.mult)
                    tt(msq, mean, mean, ALU.mult)
                    ts(var, s2, 1.0 / 128, EPS, ALU.mult, ALU.add)
                    tt(var, var, msq, ALU.subtract)
                    tt(rs, var, nhalf.w(nhalf.ap.to_broadcast([128, 2])), ALU.pow, eng="pool")
                    o3 = S["o32"].r("p (h e) -> p h e", h=2)
                    tt(o3, o3, mean.w(mean.ap.unsqueeze(2).to_broadcast([128, 2, 128])), ALU.subtract)
                    tt(o3, o3, rs.w(rs.ap.unsqueeze(2).to_broadcast([128, 2, 128])), ALU.mult)
                    tt(S["og"], S["o32"], S["G"], ALU.mult, eng="pool")

                def rH_(i, it):
                    S = rset(i)
                    b = banks[5].new()
                    for j in range(2):
                        tr(bfv(b.v.c(j * 64, (j + 1) * 64)), S["og"].c(j * 128, (j + 1) * 128), ident_b)

                def rH2_(i, it, chunks=chunks):
                    hp, n = it
                    nl = n - chunks[0]
                    o = span(orT[2 * hp + j].c(nl * 128, (nl + 1) * 128) for j in range(2))
                    o.ap = sub(RA, 0, 10240).ap.rearrange("p (k t) -> p k t", k=8)[:, 2 * hp:2 * hp + 2, nl * 128:(nl + 1) * 128]
                    cp(o, bfv(banks[5].v.c(0, 128)).r("p (j t) -> p j t", j=2), eng="act")

                pipeline(items, [rA_, rB_, rC_, rC2_, rD_, rE_, rF_, rG_, rH_, rH2_], [0, 0, 1, 1, 2, 2, 3, 3, 4, 4], first=2, junk=int(os.environ.get('JR', '0')))
                chk("ret%d" % t)
                wqa = ws.use(w_in_d[l, :, QA:QA + 512], 8, 512, live=1)

                def aset(n):
                    base = TB + (n % 2) * 11264
                    return dict(
                        xq=sub(RB, base, base + 2048, F32), rA=sub(RB, base + 2048, base + 2560, F32),
                        rB1=sub(RB, base + 2560, base + 2816, F32), rB2=sub(RB, base + 2816, base + 3072, F32),
                        qab=sub(RB, base + 3072, base + 4096), QaT=sub(RB, base + 4096, base + 5120),
                        Pa=[[sub(RB, base + 5120 + (g * 2 + j) * 1024, base + 5120 + (g * 2 + j + 1) * 1024)
                             for j in range(2)] for g in range(2)],
                        oat=sub(RB, base + 9216, base + 10240), den=sub(RB, base + 10240, base + 10272, F32))

                def a0(i, n, chunks=chunks):
                    S = aset(n)
                    nl = n - chunks[0]
                    xq, rA, rB1, rB2, qab = S["xq"], S["rA"], S["rB1"], S["rB2"], S["qab"]
                    b = banks[0].new()
                    for c in range(8):
                        mm(b.v.c(0, 512), uT[c].c(nl * 128, (nl + 1) * 128), wqa.c(c * 512, (c + 1) * 512),
                           start=b.st(), stop=(c == 7))

                def a0b(i, n):
                    S = aset(n)
                    xq, rA, rB1, rB2, qab = S["xq"], S["rA"], S["rB1"], S["rB2"], S["qab"]
                    cp(xq, banks[0].v.c(0, 512), eng="act")
                    x4 = xq.ap.rearrange("p (h d) -> p h d", h=8)
                    x16 = xq.w(x4[:, :, 0:16].rearrange("p h (a f) -> p h a f", a=2))
                    cs = cosa.c(n * 8, (n + 1) * 8)
                    sn = sina.c(n * 8, (n + 1) * 8)
                    rA4 = rA.ap.rearrange("p (h a f) -> p h a f", h=8, a=2)
                    tt(rA.w(rA4), x16, cs.w(cs.ap.unsqueeze(1).unsqueeze(1).to_broadcast([128, 8, 2, 8])), ALU.mult)
                    snb = sn.w(sn.ap.unsqueeze(1).to_broadcast([128, 8, 8]))
                    tt(rB1.r("p (h f) -> p h f", h=8), xq.w(x4[:, :, 8:16]), snb, ALU.mult)
                    tt(rB2.r("p (h f) -> p h f", h=8), xq.w(x4[:, :, 0:8]), snb, ALU.mult)
                    q3 = qab.ap.rearrange("p (h d) -> p h d", h=8)
                    tt(qab.w(q3[:, :, 0:8]), rA.w(rA4[:, :, 0, :]), rB1.r("p (h f) -> p h f", h=8), ALU.subtract)
                    tt(qab.w(q3[:, :, 8:16]), rA.w(rA4[:, :, 1, :]), rB2.r("p (h f) -> p h f", h=8), ALU.add)
                    cp(qab.w(q3[:, :, 16:64]), xq.w(x4[:, :, 16:64]), eng="pool")

                def a1(i, n):
                    S = aset(n)
                    b = banks[1].new()
                    for j in range(4):
                        tr(bfv(b.v.c(j * 64, (j + 1) * 64)), S["qab"].c(j * 128, (j + 1) * 128), ident_b)
                    cp(S["QaT"], bfv(b.v.c(0, 256)))

                def a2(i, n):
                    S = aset(n)
                    QaT, Pa, oat, den = S["QaT"], S["Pa"], S["oat"], S["den"]
                    blocks = [(0, mmeta)]
                    if n >= 2:
                        blocks.append((n - 1, mprev))
                    if n >= 1:
                        blocks.append((n, None))
                    if n <= 15:
                        blocks.append((n + 1, mnext))
                    ov = [banks[4].new(), banks[5].new()]
                    steps = [(bi, m, mask, g) for bi, (m, mask) in enumerate(blocks) for g in range(2)]

                    def sc_(k):
                        bi, m, mask, g = steps[k]
                        bs = banks[2 + k % 2].new()
                        if mask is not None:
                            mm(bs.v.r("p (j q) -> p j q", j=4), ident_b,
                               mask.w(mask.ap.unsqueeze(1).to_broadcast([128, 4, 128])), start=bs.st(), stop=False)
                        for j in range(4):
                            mm(bs.v.c(j * 128, (j + 1) * 128), KaT.p(g * 64, (g + 1) * 64).c(m * 128, (m + 1) * 128),
                               QaT.p(g * 64, (g + 1) * 64).c(j * 128, (j + 1) * 128), start=bs.st(), stop=True)
                        act(Pa[g][bi % 2], bs.v.c(0, 512), AF.Exp, scale=0.125)

                    def pv_(k):
                        bi, m, mask, g = steps[k]
                        pa = Pa[g][bi % 2]
                        for j in range(4):
                            mm(ov[g].v.c(j * 65, (j + 1) * 65), pa.c(j * 128, (j + 1) * 128),
                               Va.c(m * 130 + g * 65, m * 130 + (g + 1) * 65), start=ov[g].st(),
                               stop=(bi == len(blocks) - 1))

                    for k in range(len(steps) + 1):
                        if k < len(steps):
                            sc_(k)
                        if k >= 1:
                            pv_(k - 1)
                        warm(int(os.environ.get('JA', '1')))
                    for g in range(2):
                        o3 = ov[g].v.c(0, 260)
                        o3a = o3.ap.rearrange("p (j d) -> p j d", j=4)
                        dn = den.c(g * 4, (g + 1) * 4)
                        tt(dn, o3.w(o3a[:, :, 64]), esink.c(g * 4, (g + 1) * 4), ALU.add)
                        recip(dn, dn)
                        tt(oat.c(g * 256, (g + 1) * 256).r("p (j d) -> p j d", j=4), o3.w(o3a[:, :, 0:64]),
                           dn.w(dn.ap.unsqueeze(2).to_broadcast([128, 4, 64])), ALU.mult)

                def a3(i, n, chunks=chunks):
                    S = aset(n)
                    nl = n - chunks[0]
                    b = banks[1].new()
                    for k in range(4):
                        tr(bfv(b.v.c(k * 64, (k + 1) * 64)), S["oat"].c(k * 128, (k + 1) * 128), ident_b)
                    o = span(oaT[k].c(nl * 128, (nl + 1) * 128) for k in range(4))
                    o.ap = oaT3[:, :, nl * 128:(nl + 1) * 128]
                    cp(o, bfv(b.v.c(0, 256)).r("p (k t) -> p k t", k=4), eng="act")

                pipeline(chunks, [a0, a0b, a1, a2, a3], [0, 0, 1, 2, 3], first=1, last=(1,))

                chk("att%d" % t)
                Mfirst[0] = True
                cam, cmeta = ca_main(t), (112, 128)
                vm = vo_main(t)
                nmain = vm[1] - vm[0]
                GS = [[sub(RB, TB + (k * 4 + m4) * 2048, TB + (k * 4 + m4 + 1) * 2048, F32) for m4 in range(4)]
                      for k in range(2)]
                t1 = sub(RB, TB + 16384, TB + 18432, F32)
                t2 = sub(RB, TB + 18432, TB + 20480, F32)
                for mg in range(2):
                    wgr = ws.use(w_in_d[l, :, GR + mg * 512:GR + (mg + 1) * 512], 8, 512)
                    wga = ws.use(w_in_d[l, :, GA + mg * 512:GA + (mg + 1) * 512], 8, 512)
                    for m4 in range(4):
                        m = mg * 4 + m4
                        blk = lambda wv, K: [wv.c(k * 512 + m4 * 128, k * 512 + (m4 + 1) * 128) for k in range(K)]
                        bgr, bga = rbank(), rbank()
                        fm_mm(bgr, (0 * 8 + m) * 16, blk(wgr, 8), uT, t, cam, cmeta)
                        fm_mm(bga, (1 * 8 + m) * 16, blk(wga, 8), uT, t, cam, cmeta)
                        act(GS[0][m4].c(0, nmain), bgr.v.c(0, nmain), AF.Sigmoid)
                        act(GS[1][m4].c(0, nmain), bga.v.c(0, nmain), AF.Sigmoid)
                    wro = ws.use(w_ro_d[l, :, mg * 512:(mg + 1) * 512], 8, 512)
                    wao = ws.use(w_ao_d[l, :, mg * 512:(mg + 1) * 512], 4, 512)
                    for m4 in range(4):
                        m = mg * 4 + m4
                        blk = lambda wv, K: [wv.c(k * 512 + m4 * 128, k * 512 + (m4 + 1) * 128) for k in range(K)]
                        byr, bya = rbank(), rbank()
                        fm_mm(byr, (2 * 8 + m) * 16, blk(wro, 8), orT, t, cam, cmeta)
                        fm_mm(bya, (3 * 8 + m) * 16, blk(wao, 4), oaT, t, cam, cmeta)
                        tt(t1.c(0, nmain), GS[0][m4].c(0, nmain), byr.v.c(0, nmain), ALU.mult)
                        tt(t2.c(0, nmain), GS[1][m4].c(0, nmain), bya.v.c(0, nmain), ALU.mult)
                        tt(zT[m].c(vm[0], vm[1]), t1.c(0, nmain), t2.c(0, nmain), ALU.add)
                if t == 0:
                    sg = sub(RB, TB + 20480, TB + 21504, F32)
                    pr = sub(RB, TB + 21504, TB + 22528, F32)
                    act(sg, MB.v.c(0, 256), AF.Sigmoid)
                    tt(pr, sg, MB.v.c(256, 512), ALU.mult)
                    zm = span(z.c(0, 16) for z in zT)
                    zm.ap = zT3[:, :, 0:16]
                    tt(zm, pr.c(0, 128).r("p (m t) -> p m t", m=8), pr.c(128, 256).r("p (m t) -> p m t", m=8), ALU.add)

                chk("merge%d" % t)
                Mfirst[0] = True
                SSB.new()
                for mg in range(2):
                    wmx = ws.use(w_mx_d[l, :, mg * 512:(mg + 1) * 512], 8, 512, live=1)
                    for m4 in range(4):
                        m = mg * 4 + m4
                        b = rbank()
                        fm_mm(b, m * 16, [wmx.c(k * 512 + m4 * 128, k * 512 + (m4 + 1) * 128) for k in range(8)],
                              zT, t, vm, (0, 16))
                        cp(mixT[m].c(vm[0], vm[1]), b.v.c(0, nmain), eng="act")
                        s_ = sq[m % 2].c(0, nmain)
                        act(s_, mixT[m].c(vm[0], vm[1]), AF.Square)
                        mm(SSB.v.c(0, nmain), ones_b, s_, start=SSB.st(), stop=(m == 7))
                post_residual(t, l, 1, mixT, mixT3)

                chk("mix%d" % t)
                prenorm(t, l, 2)
                Mfirst[0] = True
                for fg in range(8):
                    w1 = ws.use(w_f1_d[l, :, fg * 512:(fg + 1) * 512], 8, 512, live=1)
                    for f4 in range(4):
                        f = fg * 4 + f4
                        b = rbank()
                        fm_mm(b, f * 16, [w1.c(k * 512 + f4 * 128, k * 512 + (f4 + 1) * 128) for k in range(8)],
                              uT, t, cam, cmeta)
                        r_ = ffr[f % 2].c(0, nmain)
                        act(r_, b.v.c(0, nmain), AF.Square)
                        stt(hid[f].c(vm[0], vm[1]), b.v.c(0, nmain), 0.0, r_, ALU.is_gt, ALU.mult)
                if t == 0:
                    r_ = ffr[0]
                    act(r_, MB.v.c(0, 512), AF.Square)
                    hm = span(hh.c(0, 16) for hh in hid)
                    hm.ap = hid3[:, :, 0:16]
                    stt(hm, MB.v.c(0, 512).r("p (f t) -> p f t", f=32), 0.0, r_.r("p (f t) -> p f t", f=32),
                        ALU.is_gt, ALU.mult)
                Mfirst[0] = True
                SSB.new()
                for mp in range(4):
                    bb = [rbank(), rbank()]
                    for fq in range(4):
                        w2 = ws.use(w_f2_d[l, fq * 1024:(fq + 1) * 1024, mp * 256:(mp + 1) * 256], 8, 256, live=1)
                        for f8 in range(8):
                            f = fq * 8 + f8
                            for m2 in range(2):
                                m = mp * 2 + m2
                                L = w2.c(f8 * 256 + m2 * 128, f8 * 256 + (m2 + 1) * 128)
                                mm(bb[m2].v.c(0, nmain), L, hid[f].c(vm[0], vm[1]), start=bb[m2].st(), stop=(f == 31))
                                if t == 0:
                                    mm(MB.v.c(m * 16, (m + 1) * 16), L, hid[f].c(0, 16), start=mstart(), stop=(f == 31))
                    for m2 in range(2):
                        m = mp * 2 + m2
                        cp(mixT[m].c(vm[0], vm[1]), bb[m2].v.c(0, nmain), eng="act")
                        s_ = sq[m % 2].c(0, nmain)
                        act(s_, mixT[m].c(vm[0], vm[1]), AF.Square)
                        mm(SSB.v.c(0, nmain), ones_b, s_, start=SSB.st(), stop=(m == 7))
                post_residual(t, l, 3, mixT, mixT3)
                chk("ffn%d" % t)

        P.muted = False
        P.phase = 3
        ostage = [sub(SC, 0, 4096, F32), sub(SC, 4096, 8192, F32)]
        for n in range(1, NCH):
            st = ostage[n % 2]
            for half in range(2):
                b = rbank()
                for j in range(4):
                    c = half * 4 + j
                    tr(b.v.c(j * 128, (j + 1) * 128), hT[c].c(n * 128, (n + 1) * 128), ident_f)
                cp(st.c(half * 512, (half + 1) * 512), b.v.c(0, 512), eng=("act" if half else "dve"))
            dma("sp", y_d[(n - 1) * 128: n * 128, :], st, slot=("o", n % 2), out_tok=V(None, "d_y", n, n + 1, 1))
        P.op("sp", lambda e: e.nop(), reads=[V(None, "d_y", 0, NCH + 1, 1)])

    print('sbuf bytes remaining', nc.sbuf_bytes_remaining)
    rec = WS(None)
    P0 = P
    emit_all(rec)
    P = Prog()
    for b in banks:
        b.first = True
    ring[0] = 0
    lastpe.clear()
    ws = WS(rec.rec)
    emit_all(ws)
    P.emit(nc, es)
    es.close()
    return nc, P


def _consts():
    f32 = np.float32
    p = np.arange(128)
    pos = (np.arange(NCH)[None, :] * 128 + p[:, None] - 112).astype(f32)

    def tab(theta, rot):
        fr = np.power(f32(theta), -np.arange(0, rot, 2, dtype=f32) / f32(rot)).astype(f32)
        ang = (pos[:, :, None] * fr[None, None, :]).astype(f32)
        return (np.cos(ang).astype(f32).reshape(128, -1), np.sin(ang).astype(f32).reshape(128, -1))

    cosr, sinr = tab(10000.0, 64)
    cosa, sina = tab(500000.0, 16)
    j = p[:, None]
    i = p[None, :]
    rfp = (-(np.minimum(i, j) + 1)).astype(f32)
    rbm = np.maximum(j - i, 0).astype(f32)
    pcols = np.stack([p + 1, 128 - p, 127 - p, p], axis=1).astype(f32)
    k, q = j, i
    mprev = np.where(k >= q, 0.0, NEG).astype(f32)
    mnext = np.where(k <= q, 0.0, NEG).astype(f32)
    mmeta = np.where(k >= 112, 0.0, NEG).astype(f32) + np.zeros((128, 128), f32)
    return dict(cosr=cosr, sinr=sinr, cosa=cosa, sina=sina, rfp=rfp, rbm=rbm, pcols=pcols,
                mprev=mprev, mnext=mnext, mmeta=mmeta, ident=np.eye(128, dtype=f32))


def _perm():
    cols = []
    for hp in range(4):
        a, b = 2 * hp, 2 * hp + 1
        for h in (a, b):
            cols += list(range(h * 64, (h + 1) * 64))
        for h in (a, b):
            cols += list(range(512 + h * 64, 512 + (h + 1) * 64))
        for h in (a, b):
            cols += list(range(1024 + h * 128, 1024 + (h + 1) * 128))
        for h in (a, b):
            cols += list(range(2048 + h * 128, 2048 + (h + 1) * 128))
    for j in range(4):
        cols += list(range(3072 + j * 64, 3072 + (j + 1) * 64))
        cols += list(range(3072 + (4 + j) * 64, 3072 + (5 + j) * 64))
    cols += list(range(3584, 5888))
    return np.asarray(cols)


_CACHE = {}


def host_inputs(x, meta_tokens, w_in, w_ret_o, w_att_o, w_mix_o, w_ff1, w_ff2, norm_mix_pre, norm_mix_post,
                norm_ff_pre, norm_ff_post, ret_decay, attn_sink, ncores=8):
    f32 = np.float32
    A = lambda a: np.ascontiguousarray(np.asarray(a, dtype=f32))
    w_in_p = np.ascontiguousarray(A(w_in)[:, :, _perm()])
    norms = np.stack([A(norm_mix_pre), A(norm_mix_post), A(norm_ff_pre), A(norm_ff_post)], axis=1)
    gains = np.ascontiguousarray(norms.reshape(4, 4, 8, 128).transpose(3, 0, 1, 2).reshape(128, 128))
    shared = dict(meta=A(meta_tokens), w_in=w_in_p, w_ret_o=A(w_ret_o), w_att_o=A(w_att_o), w_mix_o=A(w_mix_o),
                  w_ff1=A(w_ff1), w_ff2=A(w_ff2), gains=gains, ret_decay=A(ret_decay).reshape(64),
                  attn_sink=A(attn_sink).reshape(32))
    shared.update(_consts())
    xs = A(x)
    return [dict(shared, x=xs[b]) for b in range(ncores)]


def kernel(**inputs):
    if "nc" not in _CACHE:
        _CACHE["nc"] = build()[0]
    nc = _CACHE["nc"]
    in_maps = host_inputs(**inputs)
    res = run_bass_kernel_spmd(nc, in_maps, core_ids=list(range(8)))
    return np.stack([np.asarray(r["y"], dtype=np.float32) for r in res.results], axis=0)
```

```python
import math
import os
from contextlib import ExitStack
import numpy as np
import concourse.bass as bass
import concourse.mybir as mybir
from concourse.bass_utils import run_bass_kernel_spmd

F32 = mybir.dt.float32
BF16 = mybir.dt.bfloat16
AF = mybir.ActivationFunctionType
ALU = mybir.AluOpType

NCH = 17
T = NCH * 128
DEPTH = 4
EPS = 1e-6
NEG = -30000.0
TILES = [[0, 1, 2, 3, 4], [5, 6, 7, 8], [9, 10, 11, 12], [13, 14, 15, 16]]
HU = lambda hp: hp * 768
QA, KV, GR, GA = 3072, 3584, 3840, 4864


class V:
    __slots__ = ("ap", "reg", "lo", "hi", "esz")

    def __init__(s, ap, reg, lo, hi, esz):
        s.ap, s.reg, s.lo, s.hi, s.esz = ap, reg, lo, hi, esz

    def c(s, a, b):
        if s.reg.startswith("ps"):
            return V(s.ap[:, a:b], s.reg, s.lo, s.hi, s.esz)
        return V(s.ap[:, a:b], s.reg, s.lo + a * s.esz, s.lo + b * s.esz, s.esz)

    def p(s, a, b):
        return V(s.ap[a:b], s.reg, s.lo, s.hi, s.esz)

    def w(s, ap):
        return V(ap, s.reg, s.lo, s.hi, s.esz)

    def r(s, pat, **kw):
        return V(s.ap.rearrange(pat, **kw), s.reg, s.lo, s.hi, s.esz)


def span(vs):
    vs = list(vs)
    return V(None, vs[0].reg, min(v.lo for v in vs), max(v.hi for v in vs), vs[0].esz)


class Op:
    __slots__ = ("idx", "eng", "fn", "dma", "slot", "ndma", "phase", "deps", "sig", "signo", "semkey")


class Prog:
    ENGS = ("pe", "act", "dve", "pool", "sp")

    def __init__(s):
        s.ops = []
        s.segs = {}
        s.phase = 0
        s.muted = False
        s.marks = []

    def _access(s, v, idx, ekey, write, deps):
        write = write or v.reg.startswith("ps")
        segs = s.segs.setdefault(v.reg, [])
        lo, hi = v.lo, v.hi
        for x in (lo, hi):
            for i, sg in enumerate(segs):
                if sg[0] < x < sg[1]:
                    segs.insert(i + 1, [x, sg[1], sg[2], dict(sg[3])])
                    sg[1] = x
                    break
        cov = [sg for sg in segs if sg[0] >= lo and sg[1] <= hi]
        cur = lo
        new = []
        for sg in cov:
            if sg[0] > cur:
                new.append([cur, sg[0], None, {}])
            cur = sg[1]
        if cur < hi:
            new.append([cur, hi, None, {}])
        if new:
            segs.extend(new)
            segs.sort(key=lambda g: g[0])
        for sg in cov + new:
            if sg[2] is not None:
                deps.add(sg[2])
            if write:
                deps.update(sg[3].values())
                sg[2] = idx
                sg[3] = {}
            else:
                sg[3][ekey] = idx

    def op(s, eng, fn, reads=(), writes=(), dma=False, slot=None, ndma=1, force=()):
        if s.muted:
            return None
        o = Op()
        o.idx = len(s.ops)
        o.eng, o.fn, o.dma, o.slot, o.ndma, o.phase = eng, fn, dma, slot, ndma, s.phase
        o.sig = dma
        o.signo = 0
        ekey = ("dma", o.idx) if dma else eng
        deps = set()
        for v in reads:
            s._access(v, o.idx, ekey, False, deps)
        for v in writes:
            s._access(v, o.idx, ekey, True, deps)
        deps.discard(o.idx)
        best = {}
        keep = []
        for d in deps:
            dop = s.ops[d]
            if dop.dma:
                keep.append(d)
            elif dop.eng == "pe" and eng == "pe" and not dma:
                continue
            elif best.get(dop.eng, -1) < d:
                best[dop.eng] = d
        keep.extend(best.values())
        for d in force:
            if d is not None and d not in keep:
                keep.append(d)
        for d in keep:
            s.ops[d].sig = True
        o.deps = keep
        s.ops.append(o)
        return o

    def emit(s, nc, es):
        cnt = {}
        for o in s.ops:
            if o.dma:
                o.semkey = ("dma", o.slot)
                cnt[o.semkey] = cnt.get(o.semkey, 0) + 16 * o.ndma
                o.signo = cnt[o.semkey]
            elif o.sig:
                o.semkey = (o.eng, o.phase)
                cnt[o.semkey] = cnt.get(o.semkey, 0) + 1
                o.signo = cnt[o.semkey]
        sems = {k: es.enter_context(nc.semaphore("s%d" % i)) for i, k in enumerate(cnt)}
        s.nsems = len(sems)
        streams = {e: [o for o in s.ops if o.eng == e] for e in s.ENGS}
        ops = s.ops

        def run(en):
            def f(e):
                lastw = {}
                for o in streams[en]:
                    for d in o.deps:
                        dop = ops[d]
                        if lastw.get(dop.semkey, 0) < dop.signo:
                            lastw[dop.semkey] = dop.signo
                            e.wait_ge(sems[dop.semkey], dop.signo)
                    r = o.fn(e)
                    if o.sig:
                        if o.dma:
                            rs = r if isinstance(r, (list, tuple)) else [r]
                            assert len(rs) == o.ndma
                            for ri in rs:
                                ri.then_inc(sems[o.semkey], 16)
                        else:
                            r.then_inc(sems[o.semkey], 1)
            return f

        block = es.enter_context(nc.Block())
        block.tensor(run("pe"))
        block.scalar(run("act"))
        block.vector(run("dve"))
        block.gpsimd(run("pool"))
        block.sync(run("sp"))


class Bank:
    def __init__(s, v):
        s.v = v
        s.first = True

    def new(s):
        s.first = True
        return s

    def st(s):
        f = s.first
        s.first = False
        return f


def build(layers=(0, 1, 2, 3), stop=None):
    nc = bass.Bass("TRN2", target_bir_lowering=False)
    es = ExitStack()
    P = Prog()

    def din(name, shape):
        return nc.dram_tensor(name, list(shape), F32, kind="ExternalInput").ap()

    x_d = din("x", [2048, 1024])
    meta_d = din("meta", [16, 1024])
    w_in_d = din("w_in", [4, 1024, 5888])
    w_ro_d = din("w_ret_o", [4, 1024, 1024])
    w_ao_d = din("w_att_o", [4, 512, 1024])
    w_mx_d = din("w_mix_o", [4, 1024, 1024])
    w_f1_d = din("w_ff1", [4, 1024, 4096])
    w_f2_d = din("w_ff2", [4, 4096, 1024])
    gains_d = din("gains", [128, 128])
    rdec_d = din("ret_decay", [64])
    sink_d = din("attn_sink", [32])
    cosr_d = din("cosr", [128, NCH * 32])
    sinr_d = din("sinr", [128, NCH * 32])
    cosa_d = din("cosa", [128, NCH * 8])
    sina_d = din("sina", [128, NCH * 8])
    rfp_d = din("rfp", [128, 128])
    rb_d = din("rbm", [128, 128])
    pcols_d = din("pcols", [128, 4])
    mprev_d = din("mprev", [128, 128])
    mnext_d = din("mnext", [128, 128])
    mmeta_d = din("mmeta", [128, 128])
    ident_d = din("ident", [128, 128])
    y_d = nc.dram_tensor("y", [2048, 1024], F32, kind="ExternalOutput").ap()

    def DR(name):
        return V(None, "d_" + name, 0, 1, 1)

    def sb(name, ncols, dt):
        t = es.enter_context(nc.sbuf_tensor("s_" + name, [128, ncols], dt))
        esz = 4 if dt == F32 else 2
        return V(t[:], name, 0, ncols * esz, esz)

    def sub(rv, blo, bhi, dt=BF16):
        ap = rv.ap[:, blo // 2: bhi // 2]
        if dt == F32:
            ap = ap.bitcast(F32)
        return V(ap, rv.reg, rv.lo + blo, rv.lo + bhi, 4 if dt == F32 else 2)

    hT_all = sb("hT", 8 * T, F32)
    hT = [hT_all.c(c * T, (c + 1) * T) for c in range(8)]
    hT3 = hT_all.ap.rearrange("p (c t) -> p c t", c=8)
    cosr = sb("cosr", NCH * 32, F32)
    sinr = sb("sinr", NCH * 32, F32)
    cosa = sb("cosa", NCH * 8, F32)
    sina = sb("sina", NCH * 8, F32)
    rfp = sb("rfp", 128, F32)
    rbm = sb("rbm", 128, F32)
    pcols = sb("pcols", 4, F32)
    DT = sb("DT", 8 * 128, F32)
    ident_f = sb("identf", 128, F32)
    ident_b = sb("identb", 128, BF16)
    ones_b = sb("onesb", 128, BF16)
    mprev = sb("mprev", 128, BF16)
    mnext = sb("mnext", 128, BF16)
    mmeta = sb("mmeta", 128, BF16)
    gains = sb("gains", 128, F32)
    rdec = sb("rdec", 64, F32)
    sinkb = sb("sinkb", 32, F32)
    lg = sb("lg", 16, F32)
    nhalf = sb("nhalf", 1, F32)
    dcol = sb("dcol", 56, F32)
    qfc, qbc, kdfc, kdbc, cdf, cdb, esink = [dcol.c(i * 8, (i + 1) * 8) for i in range(7)]
    KaT = sb("KaT", T, BF16)
    Va = sb("Va", NCH * 130, BF16)
    SB = sb("SB", 9 * 1024, BF16)
    Srun = sb("Srun", 1024, F32)
    U_all = sb("U", 8 * 640, BF16)
    uT = [U_all.c(c * 640, (c + 1) * 640) for c in range(8)]
    RA = sb("RA", 16896 // 2, BF16)
    RB = sb("RB", 33792 // 2, BF16)
    SC = sb("SC", 8192 // 2, BF16)
    NW = 3
    wslots = [sb("w%d" % i, 4096, BF16) for i in range(NW)]

    orT = [sub(RA, k * 1280, (k + 1) * 1280) for k in range(8)]
    oaT = [sub(RA, 10240 + k * 1280, 10240 + (k + 1) * 1280) for k in range(4)]
    oaT3 = sub(RA, 10240, 15360).ap.rearrange("p (k t) -> p k t", k=4)
    mixT_all = sub(RA, 0, 16896, F32)
    mixT = [mixT_all.c(m * 528, (m + 1) * 528) for m in range(8)]
    mixT3 = mixT_all.ap.rearrange("p (m t) -> p m t", m=8)
    zT_all = sub(RB, 0, 8448)
    zT = [zT_all.c(m * 528, (m + 1) * 528) for m in range(8)]
    zT3 = zT_all.ap.rearrange("p (m t) -> p m t", m=8)
    hid_all = sub(RB, 0, 33792)
    hid = [hid_all.c(f * 528, (f + 1) * 528) for f in range(32)]
    hid3 = hid_all.ap.rearrange("p (f t) -> p f t", f=32)
    TB = 8448
    stage = [sub(RB, 0, 4096, F32), sub(RB, 4096, 8192, F32)]
    sq = [sub(SC, 0, 1024), sub(SC, 1024, 2048)]
    rt = sub(SC, 2048, 4160, F32)
    rstd = sub(SC, 4160, 6272, F32)
    scx = sub(SC, 6272, 8192, F32)
    ffr = [sub(SC, 0, 2048, F32), sub(SC, 2048, 4096, F32)]

    banks = []
    for i in range(8):
        t = es.enter_context(nc.psum_tensor("ps%d" % i, [128, 512], F32))
        banks.append(Bank(V(t[:], "ps%d" % i, 0, 2048, 4)))
    SSB, MB = banks[6], banks[7]
    ring = [0]

    def rbank():
        b = banks[ring[0] % 6]
        ring[0] += 1
        return b.new()

    def bfv(v):
        return V(v.ap.bitcast(BF16), v.reg, v.lo, v.hi, 2)

    def rw(x):
        return [x] if isinstance(x, V) else list(x)

    lastpe = {}

    def mm(out, lhsT, rhs, start, stop):
        sg_ = (lhsT.ap.base_partition(), lhsT.ap.shape[0])
        prev = lastpe.get(out.reg)
        force = [prev[1]] if (prev is not None and prev[0] != sg_) else []
        o = P.op("pe", lambda e: e.matmul(out.ap, lhsT.ap, rhs.ap, start=start, stop=stop, skip_group_check=True),
                 reads=[lhsT, rhs], writes=[out], force=force)
        if o is not None:
            lastpe[out.reg] = (sg_, o.idx)

    def tr(out, in_, idn):
        sg_ = (0, 128)
        prev = lastpe.get(out.reg)
        force = [prev[1]] if (prev is not None and prev[0] != sg_) else []
        o = P.op("pe", lambda e: e.transpose(out.ap, in_.ap, idn.ap), reads=[in_, idn], writes=[out], force=force)
        if o is not None:
            lastpe[out.reg] = (sg_, o.idx)

    def act(out, in_, func, scale=None, bias=None, accum=None, eng="act"):
        kw = {}
        rd = [in_]
        wr = [out]
        if scale is not None:
            if isinstance(scale, V):
                kw["scale"] = scale.ap
                rd.append(scale)
            else:
                kw["scale"] = float(scale)
        if bias is not None:
            if isinstance(bias, V):
                kw["bias"] = bias.ap
                rd.append(bias)
            else:
                kw["bias"] = float(bias)
        if accum is not None:
            kw["accum_out"] = accum.ap
            wr.append(accum)
        P.op("act", lambda e: e.activation(out.ap, in_.ap, func, **kw), reads=rd, writes=wr)

    def tt(out, a, b, op, eng="dve"):
        P.op(eng, lambda e: e.tensor_tensor(out.ap, a.ap, b.ap, op), reads=[a, b], writes=[out])

    def ts(out, a, s1, s2, op0, op1=None, eng="dve"):
        rd = [a]
        a1 = s1.ap if isinstance(s1, V) else s1
        a2 = s2.ap if isinstance(s2, V) else s2
        if isinstance(s1, V):
            rd.append(s1)
        if isinstance(s2, V):
            rd.append(s2)
        if op1 is None:
            P.op(eng, lambda e: e.tensor_scalar(out.ap, a.ap, a1, None, op0), reads=rd, writes=[out])
        else:
            P.op(eng, lambda e: e.tensor_scalar(out.ap, a.ap, a1, a2, op0, op1), reads=rd, writes=[out])

    def stt(out, a, sc, b, op0, op1, eng="dve"):
        rd = [a, b]
        a1 = sc.ap if isinstance(sc, V) else sc
        if isinstance(sc, V):
            rd.append(sc)
        P.op(eng, lambda e: e.scalar_tensor_tensor(out.ap, a.ap, a1, b.ap, op0, op1), reads=rd, writes=[out])

    def cp(out, in_, eng="dve"):
        if eng == "act":
            act(out, in_, AF.Copy)
        else:
            P.op(eng, lambda e: e.tensor_copy(out.ap, in_.ap), reads=[in_], writes=[out])

    def recip(out, in_):
        P.op("dve", lambda e: e.reciprocal(out.ap, in_.ap), reads=[in_], writes=[out])

    def rstd_from(out, ssq, n_scale, tmp):
        act(tmp, ssq, AF.Sqrt, scale=n_scale, bias=EPS)
        recip(out, tmp)

    def memset(v, val, eng="dve"):
        P.op(eng, lambda e: e.memset(v.ap, val), writes=[v])

    def dma(q, out, in_ap_or_v, slot, in_tok=None, out_tok=None):
        if isinstance(out, V):
            o_ap, wr = out.ap, [out]
        else:
            o_ap, wr = out, [out_tok]
        if isinstance(in_ap_or_v, V):
            i_ap, rd = in_ap_or_v.ap, [in_ap_or_v]
        else:
            i_ap, rd = in_ap_or_v, ([in_tok] if in_tok else [])
        P.op(q, lambda e: e.dma_start(out=o_ap, in_=i_ap), reads=rd, writes=wr, dma=True, slot=slot)

    class WS:
        def __init__(s, plan=None):
            s.plan = plan
            s.rec = []
            s.i = 0
            s.issued = 0

        def use(s, src, nk, ncols, live=2):
            if s.plan is None:
                s.rec.append((src, nk, ncols))
                return wslots[0]
            i = s.i
            s.i += 1
            if P.muted:
                return wslots[0]
            while s.issued < min(len(s.plan), i + NW + 1 - live):
                j = s.issued
                sr, k2, n2 = s.plan[j]
                sl = wslots[j % NW]
                dma("pool", sl.c(0, k2 * n2).w(sl.ap[:, 0:k2 * n2].rearrange("p (k n) -> p k n", k=k2)),
                    sr.rearrange("(k p) n -> p k n", p=128), slot=("w", j % NW))
                s.issued += 1
            return wslots[i % NW]

    def tc0(t):
        return TILES[t][0] * 128

    def g_main(t):
        return (128, 640) if t == 0 else (tc0(t), tc0(t) + 512)

    def ca_main(t):
        return (128, 640) if t == 0 else (0, 512)

    def vo_main(t):
        return (16, 528) if t == 0 else (0, 512)

    def gidx(l, j):
        return (l * 4 + j) * 8

    def emit_all(ws):
        Mfirst = [True]

        def mstart():
            f = Mfirst[0]
            Mfirst[0] = False
            return f

        P.phase = 0
        for (v, d, q) in ((cosr, cosr_d, "sp"), (sinr, sinr_d, "sp"), (cosa, cosa_d, "sp"), (sina, sina_d, "sp"),
                          (rfp, rfp_d, "sp"), (rbm, rb_d, "sp"), (pcols, pcols_d, "sp"), (ident_f, ident_d, "sp"),
                          (gains, gains_d, "sp"), (ident_b, ident_d, "pool"), (mprev, mprev_d, "pool"),
                          (mnext, mnext_d, "pool"), (mmeta, mmeta_d, "pool")):
            dma(q, v, d, slot=("c", v.reg))
        dma("sp", rdec, rdec_d.partition_broadcast(128), slot=("c", "rdec"))
        dma("sp", sinkb, sink_d.partition_broadcast(128), slot=("c", "sink"))
        memset(ones_b, 1.0)
        memset(nhalf, -0.5)
        memset(Va, 1.0)
        memset(U_all, 0.0)

        for n in range(NCH):
            st = stage[n % 2]
            if n == 0:
                memset(st, 0.0)
                dma("sp", st.p(112, 128), meta_d, slot=("st", 0))
            else:
                dma("sp", st, x_d[(n - 1) * 128: n * 128, :], slot=("st", n % 2))
            for half in range(2):
                b = rbank()
                for j in range(4):
                    c = half * 4 + j
                    tr(b.v.c(j * 128, (j + 1) * 128), st.c(c * 128, (c + 1) * 128), ident_f)
                o = span(hT[half * 4 + j].c(n * 128, (n + 1) * 128) for j in range(4))
                o.ap = hT3[:, half * 4:(half + 1) * 4, n * 128:(n + 1) * 128]
                cp(o, b.v.r("p (j t) -> p j t", j=4), eng=("act" if half else "dve"))

        def prenorm(t, l, j):
            pieces = [g_main(t)] + ([(112, 128)] if t == 0 else [])
            if t == 0:
                z = span(u.c(0, 112) for u in uT)
                z.ap = U_all.ap.rearrange("p (c t) -> p c t", c=8)[:, :, 0:112]
                memset(z, 0.0)
            for (c0, c1) in pieces:
                n = c1 - c0
                ssb = SSB.new() if n > 16 else rbank()
                for c in range(8):
                    s_ = sq[c % 2].c(0, n)
                    act(s_, hT[c].c(c0, c1), AF.Square)
                    mm(ssb.v.c(0, n), ones_b, s_, start=ssb.st(), stop=(c == 7))
                rstd_from(rstd.c(0, n), ssb.v.c(0, n), 1.0 / 1024, rt.c(0, n))
                for c in range(8):
                    g = gains.c(gidx(l, j) + c, gidx(l, j) + c + 1)
                    stt(uT[c].c(c0 - tc0(t), c1 - tc0(t)), hT[c].c(c0, c1), g, rstd.c(0, n), ALU.mult, ALU.mult)

        def post_residual(t, l, j, outT, outT3):
            a, b = vo_main(t)
            n = b - a
            ga, gb = g_main(t)
            rstd_from(rstd.c(0, n), SSB.v.c(0, n), 1.0 / 1024, rt.c(0, n))
            for m in range(8):
                g = gains.c(gidx(l, j) + m, gidx(l, j) + m + 1)
                tt(outT[m].c(a, b), outT[m].c(a, b), rstd.c(0, n), ALU.mult)
                stt(hT[m].c(ga, gb), outT[m].c(a, b), g, hT[m].c(ga, gb), ALU.mult, ALU.add)
            if t == 0:
                om = span(o.c(0, 16) for o in outT)
                om.ap = outT3[:, :, 0:16]
                cp(om, MB.v.c(0, 128).r("p (m t) -> p m t", m=8), eng="act")
                sqm = sub(SC, 0, 256)
                act(sqm, MB.v.c(0, 128), AF.Square)
                b2 = rbank()
                for m in range(8):
                    mm(b2.v.c(0, 16), ones_b, sqm.c(m * 16, (m + 1) * 16), start=b2.st(), stop=(m == 7))
                rtm = scx.c(0, 16)
                rsm = scx.c(16, 32)
                rstd_from(rsm, b2.v.c(0, 16), 1.0 / 1024, rtm)
                t1 = scx.c(32, 160)
                t13 = t1.r("p (m t) -> p m t", m=8)
                tt(t13, om, rsm.w(rsm.ap.unsqueeze(1).to_broadcast([128, 8, 16])), ALU.mult)
                gv = gains.c(gidx(l, j), gidx(l, j) + 8)
                tt(t13, t13, gv.w(gv.ap.unsqueeze(2).to_broadcast([128, 8, 16])), ALU.mult)
                hm = span(h.c(112, 128) for h in hT)
                hm.ap = hT3[:, :, 112:128]
                tt(hm, hm, t13, ALU.add)

        def fm_mm(bank, mcol, lhs_list, rhs_list, t, rng_main, rng_meta, last=True):
            K = len(lhs_list)
            a, b = rng_main
            for k in range(K):
                mm(bank.v.c(0, b - a), lhs_list[k], rhs_list[k].c(a, b), start=bank.st(), stop=(last and k == K - 1))
            if t == 0 and mcol is not None:
                ma, mb_ = rng_meta
                for k in range(K):
                    mm(MB.v.c(mcol, mcol + 16), lhs_list[k], rhs_list[k].c(ma, mb_), start=mstart(),
                       stop=(last and k == K - 1))

        def rope64(dst, src, n, nq, A, B1, B2, eng="dve"):
            cs = cosr.c(n * 32, (n + 1) * 32)
            sn = sinr.c(n * 32, (n + 1) * 32)
            s4 = src.ap.rearrange("p (q a f) -> p q a f", q=nq, a=2)
            d4 = dst.ap.rearrange("p (q a f) -> p q a f", q=nq, a=2)
            A4 = A.ap.rearrange("p (q a f) -> p q a f", q=nq, a=2)
            tt(A.w(A4), src.w(s4), cs.w(cs.ap.unsqueeze(1).unsqueeze(1).to_broadcast([128, nq, 2, 32])), ALU.mult, eng=eng)
            snb = sn.w(sn.ap.unsqueeze(1).to_broadcast([128, nq, 32]))
            b1 = B1.r("p (q f) -> p q f", q=nq)
            b2 = B2.r("p (q f) -> p q f", q=nq)
            tt(b1, src.w(s4[:, :, 1, :]), snb, ALU.mult, eng=eng)
            tt(b2, src.w(s4[:, :, 0, :]), snb, ALU.mult, eng=eng)
            tt(dst.w(d4[:, :, 0, :]), A.w(A4[:, :, 0, :]), b1, ALU.subtract, eng=eng)
            tt(dst.w(d4[:, :, 1, :]), A.w(A4[:, :, 1, :]), b2, ALU.add, eng=eng)

        def warm(k):
            for _ in range(k):
                mm(MB.v.c(0, 512), ones_b, U_all.c(0, 512), start=True, stop=True)

        def pipeline(items, stages, skews, first=0, junk=0):
            n = len(items)
            order = list(range(first)) + sorted(range(first, len(stages)), key=lambda j: (-skews[j], j))
            for s_ in range(n + max(skews)):
                for j in order:
                    fn, sk = stages[j], skews[j]
                    i = s_ - sk
                    if 0 <= i < n:
                        fn(i, items[i])
                warm(junk)

        def chk(name):
            P.marks.append((name, sum(1 for o in P.ops if o.eng == "pe")))
            if stop == name:
                P.muted = True

        for l in layers:
            P.phase = l
            act(lg, rdec.c(l * 16, (l + 1) * 16), AF.Exp)
            ts(lg, lg, -1.0, None, ALU.mult)
            tA = sub(RB, TB, TB + 512, F32)
            tB = sub(RB, TB + 512, TB + 1024, F32)
            for h in range(8):
                ts(tA, rfp, lg.c(h, h + 1), None, ALU.mult)
                stt(tB, rbm, lg.c(8 + h, 9 + h), tA, ALU.mult, ALU.add)
                act(DT.c(h * 128, (h + 1) * 128), tB, AF.Exp)
            l8 = math.log(0.125)
            act(qfc, lg.c(0, 8), AF.Exp, scale=pcols.c(0, 1), bias=l8)
            act(qbc, lg.c(8, 16), AF.Exp, scale=pcols.c(1, 2), bias=l8)
            act(kdfc, lg.c(0, 8), AF.Exp, scale=pcols.c(2, 3))
            act(kdbc, lg.c(8, 16), AF.Exp, scale=pcols.c(3, 4))
            act(cdf, lg.c(0, 8), AF.Exp, scale=128.0)
            act(cdb, lg.c(8, 16), AF.Exp, scale=128.0)
            act(esink, sinkb.c(l * 8, (l + 1) * 8), AF.Exp)

            chk("L%d_start" % l)
            memset(Srun, 0.0)
            for t in (3, 2, 1, 0):
                prenorm(t, l, 0)
                chunks = TILES[t]
                wkv = ws.use(w_in_d[l, :, KV:KV + 256], 8, 256, live=1)
                for n in chunks:
                    nl = n - chunks[0]
                    base = TB + (n % 2) * 2048
                    xa = sub(RB, base, base + 1024, F32)
                    rA = sub(RB, base + 1024, base + 1152, F32)
                    rB1 = sub(RB, base + 1152, base + 1216, F32)
                    rB2 = sub(RB, base + 1216, base + 1280, F32)
                    kab = sub(RB, base + 1280, base + 1536)
                    b = banks[4].new()
                    for c in range(8):
                        mm(b.v.c(0, 256), uT[c].c(nl * 128, (nl + 1) * 128), wkv.c(c * 256, (c + 1) * 256),
                           start=b.st(), stop=(c == 7))
                    cp(xa, b.v.c(0, 256), eng="act")
                    xk = xa.c(0, 128)
                    x4 = xk.ap.rearrange("p (h d) -> p h d", h=2)
                    x16 = xk.w(x4[:, :, 0:16].rearrange("p h (a f) -> p h a f", a=2))
                    cs = cosa.c(n * 8, (n + 1) * 8)
                    sn = sina.c(n * 8, (n + 1) * 8)
                    tt(rA.r("p (h a f) -> p h a f", h=2, a=2), x16,
                       cs.w(cs.ap.unsqueeze(1).unsqueeze(1).to_broadcast([128, 2, 2, 8])), ALU.mult)
                    snb = sn.w(sn.ap.unsqueeze(1).to_broadcast([128, 2, 8]))
                    tt(rB1.r("p (h f) -> p h f", h=2), xk.w(x4[:, :, 8:16]), snb, ALU.mult)
                    tt(rB2.r("p (h f) -> p h f", h=2), xk.w(x4[:, :, 0:8]), snb, ALU.mult)
                    k3 = kab.ap.rearrange("p (h d) -> p h d", h=2)
                    rA4 = rA.ap.rearrange("p (h a f) -> p h a f", h=2, a=2)
                    tt(kab.w(k3[:, :, 0:8]), rA.w(rA4[:, :, 0, :]), rB1.r("p (h f) -> p h f", h=2), ALU.subtract)
                    tt(kab.w(k3[:, :, 8:16]), rA.w(rA4[:, :, 1, :]), rB2.r("p (h f) -> p h f", h=2), ALU.add)
                    cp(kab.w(k3[:, :, 16:64]), xk.w(x4[:, :, 16:64]))
                    van = Va.c(n * 130, (n + 1) * 130)
                    cp(van.w(van.ap.rearrange("p (g d) -> p g d", g=2)[:, :, 0:64]),
                       xa.c(128, 256).r("p (g d) -> p g d", g=2), eng="act")
                    b2 = banks[5].new()
                    tr(bfv(b2.v.c(0, 64)), kab, ident_b)
                    cp(KaT.c(n * 128, (n + 1) * 128), bfv(b2.v.c(0, 64)))
                items = [(hp, n) for hp in range(4) for n in reversed(chunks)]
                units = {}

                def p1A(i, it, t=t, chunks=chunks):
                    hp, n = it
                    if n == chunks[-1]:
                        units[hp] = ws.use(w_in_d[l, :, HU(hp) + 128:HU(hp) + 512], 8, 384, live=1)
                    wu = units[hp]
                    nl = n - chunks[0]
                    b = banks[i % 2].new()
                    for c in range(8):
                        mm(b.v.c(0, 384), uT[c].c(nl * 128, (nl + 1) * 128), wu.c(c * 384, (c + 1) * 384),
                           start=b.st(), stop=(c == 7))

                def p1set(i):
                    base = TB + (i % 4) * 3584
                    return dict(x=sub(RB, base, base + 512, F32), A=sub(RB, base + 512, base + 1024, F32),
                                B1=sub(RB, base + 1024, base + 1280, F32), B2=sub(RB, base + 1280, base + 1536, F32),
                                kr=sub(RB, base + 1536, base + 2048, F32), Kd=sub(RB, base + 2048, base + 2560),
                                Vb=sub(RB, base + 2560, base + 3072))

                def p1B(i, it):
                    hp, n = it
                    S = p1set(i)
                    b = banks[i % 2]
                    cp(S["x"], b.v.c(0, 128), eng="act")
                    cp(S["Vb"], b.v.c(128, 384), eng="act")
                    rope64(S["kr"], S["x"], n, 2, S["A"], S["B1"], S["B2"])
                    kr = S["kr"]
                    kb = kr.w(kr.ap.rearrange("p (h d) -> p h d", h=2).unsqueeze(2).to_broadcast([128, 2, 2, 64]))
                    dc = kdbc.c(2 * hp, 2 * hp + 2)
                    tt(S["Kd"].r("p (h a d) -> p h a d", h=2, a=2), kb,
                       dc.w(dc.ap.unsqueeze(2).unsqueeze(3).to_broadcast([128, 2, 2, 64])), ALU.mult)

                def p1D(i, it):
                    S = p1set(i)
                    b = banks[2 + i % 2].new()
                    for j in range(2):
                        mm(b.v.c(j * 128, (j + 1) * 128), S["Kd"].c(j * 128, (j + 1) * 128),
                           S["Vb"].c(j * 128, (j + 1) * 128), start=b.st(), stop=True)

                def p1E(i, it):
                    hp, n = it
                    pp = n % 2
                    b = banks[2 + i % 2]
                    sr = Srun.c(hp * 256, (hp + 1) * 256)
                    slot = SB.c((n // 2) * 1024 + hp * 256, (n // 2) * 1024 + (hp + 1) * 256)
                    cp(slot.p(pp * 64, (pp + 1) * 64), sr.p(pp * 64, (pp + 1) * 64), eng="act")
                    dc = cdb.c(2 * hp, 2 * hp + 2)
                    sr3 = sr.r("p (h e) -> p h e", h=2)
                    tt(sr3, sr3, dc.w(dc.ap.unsqueeze(2).to_broadcast([128, 2, 128])), ALU.mult)
                    tt(sr, sr, b.v.c(0, 256), ALU.add)

                pipeline(items, [p1A, p1B, p1D, p1E], [0, 0, 2, 2], first=2, junk=int(os.environ.get('JP', '0')))
            chk("pass1")

            memset(Srun, 0.0)
            for t in range(4):
                chunks = TILES[t]
                prenorm(t, l, 0)
                items = [(hp, n) for hp in range(4) for n in chunks]
                unitsA, unitsB = {}, {}
                junk = sub(SC, 6272, 7296, F32)

                def rset(i):
                    o = {}
                    eb = (i % 2) * 5120
                    lb = 10240 + (i % 4) * 5696
                    off = [0, 0]

                    def take(name, nb, dt=BF16, late=False):
                        k = 1 if late else 0
                        base = lb if late else eb
                        o[name] = sub(RB, base + off[k], base + off[k] + nb, dt)
                        off[k] += nb
                    take("x", 1024, F32)
                    take("A", 1024, F32)
                    take("B1", 512, F32)
                    take("B2", 512, F32)
                    take("qk", 1024, F32)
                    take("QQ", 512)
                    take("Kdup", 512)
                    take("G", 1024, F32, True)
                    take("Kd", 512, BF16, True)
                    take("Vb", 512, BF16, True)
                    take("QKT", 1024, BF16, True)
                    take("PT", 512, BF16, True)
                    take("Sfb", 512, BF16, True)
                    take("o32", 1024, F32, True)
                    take("og", 512, BF16, True)
                    take("sm", 64, F32, True)
                    assert off[0] == 5120 and off[1] == 5696
                    return o

                def rA_(i, it, chunks=chunks):
                    hp, n = it
                    if n == chunks[0]:
                        unitsA[hp] = ws.use(w_in_d[l, :, HU(hp):HU(hp) + 512], 8, 512)
                        unitsB[hp] = ws.use(w_in_d[l, :, HU(hp) + 512:HU(hp) + 768], 8, 256)
                    wa, wb = unitsA[hp], unitsB[hp]
                    nl = n - chunks[0]
                    b = banks[0].new()
                    for c in range(8):
                        mm(b.v.c(0, 512), uT[c].c(nl * 128, (nl + 1) * 128), wa.c(c * 512, (c + 1) * 512),
                           start=b.st(), stop=(c == 7))
                    b = banks[1].new()
                    for c in range(8):
                        mm(b.v.c(0, 256), uT[c].c(nl * 128, (nl + 1) * 128), wb.c(c * 256, (c + 1) * 256),
                           start=b.st(), stop=(c == 7))

                def hb(col, hp, d):
                    dc = col.c(2 * hp, 2 * hp + 2)
                    return dc.w(dc.ap.unsqueeze(2).to_broadcast([128, 2, d]))

                def rB_(i, it):
                    hp, n = it
                    pp = n % 2
                    S = rset(i)
                    cp(S["x"], banks[0].v.c(0, 256), eng="act")
                    cp(S["Vb"], banks[0].v.c(256, 512), eng="act")
                    act(S["G"], banks[1].v.c(0, 256), AF.Silu)
                    rope64(S["qk"], S["x"], n, 4, S["A"], S["B1"], S["B2"])
                    q = S["qk"].c(0, 128).r("p (h d) -> p h d", h=2)
                    k = S["qk"].c(128, 256)
                    QQ3 = S["QQ"].ap.rearrange("p (h a d) -> p h a d", h=2, a=2)
                    tt(S["QQ"].w(QQ3[:, :, pp, :]), q, hb(qbc, hp, 64), ALU.mult)
                    tt(S["QQ"].w(QQ3[:, :, 1 - pp, :]), q, hb(qfc, hp, 64), ALU.mult)
                    kb = k.w(k.ap.rearrange("p (h d) -> p h d", h=2).unsqueeze(2).to_broadcast([128, 2, 2, 64]))
                    dc = kdfc.c(2 * hp, 2 * hp + 2)
                    tt(S["Kd"].r("p (h a d) -> p h a d", h=2, a=2), kb,
                       dc.w(dc.ap.unsqueeze(2).unsqueeze(3).to_broadcast([128, 2, 2, 64])), ALU.mult, eng="pool")
                    cp(S["Kdup"].r("p (h a d) -> p h a d", h=2, a=2), kb)

                def rC_(i, it):
                    S = rset(i)
                    b = banks[2].new()
                    for j in range(2):
                        tr(bfv(b.v.c(j * 64, (j + 1) * 64)), S["QQ"].c(j * 128, (j + 1) * 128), ident_b)
                    for j in range(2):
                        tr(bfv(b.v.c(128 + j * 64, 128 + (j + 1) * 64)), S["Kdup"].c(j * 128, (j + 1) * 128), ident_b)

                def rC2_(i, it):
                    S = rset(i)
                    cp(S["QKT"], bfv(banks[2].v.c(0, 256)))

                def rD_(i, it):
                    hp, n = it
                    pp = n % 2
                    S = rset(i)
                    hf = 1 - pp
                    b = banks[3].new()
                    QKT = S["QKT"]
                    for j in range(2):
                        mm(b.v.c(256 + j * 128, 256 + (j + 1) * 128), S["Kd"].c(j * 128, (j + 1) * 128),
                           S["Vb"].c(j * 128, (j + 1) * 128), start=b.st(), stop=True)
                    for j in range(2):
                        mm(b.v.c(j * 128, (j + 1) * 128), QKT.p(hf * 64, (hf + 1) * 64).c(256 + j * 128, 256 + (j + 1) * 128),
                           QKT.p(hf * 64, (hf + 1) * 64).c(j * 128, (j + 1) * 128), start=b.st(), stop=True)

                def rE_(i, it):
                    hp, n = it
                    pp = n % 2
                    hf = 1 - pp
                    S = rset(i)
                    b = banks[3]
                    tt(S["PT"], b.v.c(0, 256), DT.c(hp * 256, (hp + 1) * 256), ALU.mult)
                    sr = Srun.c(hp * 256, (hp + 1) * 256)
                    cp(S["Sfb"].p(hf * 64, (hf + 1) * 64), sr.p(hf * 64, (hf + 1) * 64), eng="pool")
                    slot = SB.c((n // 2) * 1024 + hp * 256, (n // 2) * 1024 + (hp + 1) * 256)
                    cp(S["Sfb"].p(pp * 64, (pp + 1) * 64), slot.p(pp * 64, (pp + 1) * 64), eng="act")
                    sr3 = sr.r("p (h e) -> p h e", h=2)
                    tt(sr3, sr3, hb(cdf, hp, 128), ALU.mult)
                    tt(sr, sr, b.v.c(256, 512), ALU.add)

                def rF_(i, it):
                    hp, n = it
                    S = rset(i)
                    b = banks[4].new()
                    QKT = S["QKT"]
                    for j in range(2):
                        mm(b.v.c(j * 128, (j + 1) * 128), S["PT"].c(j * 128, (j + 1) * 128), S["Vb"].c(j * 128, (j + 1) * 128),
                           start=b.st(), stop=False)
                    for j in range(2):
                        mm(b.v.c(j * 128, (j + 1) * 128), QKT.c(j * 128, (j + 1) * 128), S["Sfb"].c(j * 128, (j + 1) * 128),
                           start=b.st(), stop=True)

                def rG_(i, it):
                    S = rset(i)
                    b = banks[4]
                    sm = S["sm"]
                    s1, s2, mean, msq, var, rs = [sm.c(2 * j, 2 * j + 2) for j in range(6)]
                    for j in range(2):
                        act(S["o32"].c(j * 128, (j + 1) * 128), b.v.c(j * 128, (j + 1) * 128), AF.Copy, accum=s1.c(j, j + 1))
                    for j in range(2):
                        act(junk.c(0, 128), S["o32"].c(j * 128, (j + 1) * 128), AF.Square, accum=s2.c(j, j + 1))
                    ts(mean, s1, 1.0 / 128, None, ALU.mult)
                    tt(msq, mean, mean, ALU.mult)
                    ts(var, s2, 1.0 / 128, EPS, ALU.mult, ALU.add)
                    tt(var, var, msq, ALU.subtract)
                    tt(rs, var, nhalf.w(nhalf.ap.to_broadcast([128, 2])), ALU.pow, eng="pool")
                    o3 = S["o32"].r("p (h e) -> p h e", h=2)
                    tt(o3, o3, mean.w(mean.ap.unsqueeze(2).to_broadcast([128, 2, 128])), ALU.subtract)
                    tt(o3, o3, rs.w(rs.ap.unsqueeze(2).to_broadcast([128, 2, 128])), ALU.mult)
                    tt(S["og"], S["o32"], S["G"], ALU.mult, eng="pool")

                def rH_(i, it):
                    S = rset(i)
                    b = banks[5].new()
                    for j in range(2):
                        tr(bfv(b.v.c(j * 64, (j + 1) * 64)), S["og"].c(j * 128, (j + 1) * 128), ident_b)

                def rH2_(i, it, chunks=chunks):
                    hp, n = it
                    nl = n - chunks[0]
                    o = span(orT[2 * hp + j].c(nl * 128, (nl + 1) * 128) for j in range(2))
                    o.ap = sub(RA, 0, 10240).ap.rearrange("p (k t) -> p k t", k=8)[:, 2 * hp:2 * hp + 2, nl * 128:(nl + 1) * 128]
                    cp(o, bfv(banks[5].v.c(0, 128)).r("p (j t) -> p j t", j=2), eng="act")

                pipeline(items, [rA_, rB_, rC_, rC2_, rD_, rE_, rF_, rG_, rH_, rH2_], [0, 0, 1, 1, 2, 2, 3, 3, 4, 4], first=2, junk=int(os.environ.get('JR', '0')))
                chk("ret%d" % t)
                wqa = ws.use(w_in_d[l, :, QA:QA + 512], 8, 512, live=1)

                def aset(n):
                    base = TB + (n % 2) * 11264
                    return dict(
                        xq=sub(RB, base, base + 2048, F32), rA=sub(RB, base + 2048, base + 2560, F32),
                        rB1=sub(RB, base + 2560, base + 2816, F32), rB2=sub(RB, base + 2816, base + 3072, F32),
                        qab=sub(RB, base + 3072, base + 4096), QaT=sub(RB, base + 4096, base + 5120),
                        Pa=[[sub(RB, base + 5120 + (g * 2 + j) * 1024, base + 5120 + (g * 2 + j + 1) * 1024)
                             for j in range(2)] for g in range(2)],
                        oat=sub(RB, base + 9216, base + 10240), den=sub(RB, base + 10240, base + 10272, F32))

                def a0(i, n, chunks=chunks):
                    S = aset(n)
                    nl = n - chunks[0]
                    xq, rA, rB1, rB2, qab = S["xq"], S["rA"], S["rB1"], S["rB2"], S["qab"]
                    b = banks[0].new()
                    for c in range(8):
                        mm(b.v.c(0, 512), uT[c].c(nl * 128, (nl + 1) * 128), wqa.c(c * 512, (c + 1) * 512),
                           start=b.st(), stop=(c == 7))
                    cp(xq, b.v.c(0, 512), eng="act")
                    x4 = xq.ap.rearrange("p (h d) -> p h d", h=8)
                    x16 = xq.w(x4[:, :, 0:16].rearrange("p h (a f) -> p h a f", a=2))
                    cs = cosa.c(n * 8, (n + 1) * 8)
                    sn = sina.c(n * 8, (n + 1) * 8)
                    rA4 = rA.ap.rearrange("p (h a f) -> p h a f", h=8, a=2)
                    tt(rA.w(rA4), x16, cs.w(cs.ap.unsqueeze(1).unsqueeze(1).to_broadcast([128, 8, 2, 8])), ALU.mult)
                    snb = sn.w(sn.ap.unsqueeze(1).to_broadcast([128, 8, 8]))
                    tt(rB1.r("p (h f) -> p h f", h=8), xq.w(x4[:, :, 8:16]), snb, ALU.mult)
                    tt(rB2.r("p (h f) -> p h f", h=8), xq.w(x4[:, :, 0:8]), snb, ALU.mult)
                    q3 = qab.ap.rearrange("p (h d) -> p h d", h=8)
                    tt(qab.w(q3[:, :, 0:8]), rA.w(rA4[:, :, 0, :]), rB1.r("p (h f) -> p h f", h=8), ALU.subtract)
                    tt(qab.w(q3[:, :, 8:16]), rA.w(rA4[:, :, 1, :]), rB2.r("p (h f) -> p h f", h=8), ALU.add)
                    cp(qab.w(q3[:, :, 16:64]), xq.w(x4[:, :, 16:64]), eng="pool")

                def a1(i, n):
                    S = aset(n)
                    b = banks[1].new()
                    for j in range(4):
                        tr(bfv(b.v.c(j * 64, (j + 1) * 64)), S["qab"].c(j * 128, (j + 1) * 128), ident_b)
                    cp(S["QaT"], bfv(b.v.c(0, 256)))

                def a2(i, n):
                    S = aset(n)
                    QaT, Pa, oat, den = S["QaT"], S["Pa"], S["oat"], S["den"]
                    blocks = [(0, mmeta)]
                    if n >= 2:
                        blocks.append((n - 1, mprev))
                    if n >= 1:
                        blocks.append((n, None))
                    if n <= 15:
                        blocks.append((n + 1, mnext))
                    ov = [banks[4].new(), banks[5].new()]
                    steps = [(bi, m, mask, g) for bi, (m, mask) in enumerate(blocks) for g in range(2)]

                    def sc_(k):
                        bi, m, mask, g = steps[k]
                        bs = banks[2 + k % 2].new()
                        if mask is not None:
                            mm(bs.v.r("p (j q) -> p j q", j=4), ident_b,
                               mask.w(mask.ap.unsqueeze(1).to_broadcast([128, 4, 128])), start=bs.st(), stop=False)
                        for j in range(4):
                            mm(bs.v.c(j * 128, (j + 1) * 128), KaT.p(g * 64, (g + 1) * 64).c(m * 128, (m + 1) * 128),
                               QaT.p(g * 64, (g + 1) * 64).c(j * 128, (j + 1) * 128), start=bs.st(), stop=True)
                        act(Pa[g][bi % 2], bs.v.c(0, 512), AF.Exp, scale=0.125)

                    def pv_(k):
                        bi, m, mask, g = steps[k]
                        pa = Pa[g][bi % 2]
                        for j in range(4):
                            mm(ov[g].v.c(j * 65, (j + 1) * 65), pa.c(j * 128, (j + 1) * 128),
                               Va.c(m * 130 + g * 65, m * 130 + (g + 1) * 65), start=ov[g].st(),
                               stop=(bi == len(blocks) - 1))

                    for k in range(len(steps) + 1):
                        if k < len(steps):
                            sc_(k)
                        if k >= 1:
                            pv_(k - 1)
                        warm(int(os.environ.get('JA', '1')))
                    for g in range(2):
                        o3 = ov[g].v.c(0, 260)
                        o3a = o3.ap.rearrange("p (j d) -> p j d", j=4)
                        dn = den.c(g * 4, (g + 1) * 4)
                        tt(dn, o3.w(o3a[:, :, 64]), esink.c(g * 4, (g + 1) * 4), ALU.add)
                        recip(dn, dn)
                        tt(oat.c(g * 256, (g + 1) * 256).r("p (j d) -> p j d", j=4), o3.w(o3a[:, :, 0:64]),
                           dn.w(dn.ap.unsqueeze(2).to_broadcast([128, 4, 64])), ALU.mult)

                def a3(i, n, chunks=chunks):
                    S = aset(n)
                    nl = n - chunks[0]
                    b = banks[0].new()
                    for k in range(4):
                        tr(bfv(b.v.c(k * 64, (k + 1) * 64)), S["oat"].c(k * 128, (k + 1) * 128), ident_b)
                    o = span(oaT[k].c(nl * 128, (nl + 1) * 128) for k in range(4))
                    o.ap = oaT3[:, :, nl * 128:(nl + 1) * 128]
                    cp(o, bfv(b.v.c(0, 256)).r("p (k t) -> p k t", k=4), eng="act")

                pipeline(chunks, [a0, a1, a2, a3], [0, 1, 2, 3], first=1)

                chk("att%d" % t)
                Mfirst[0] = True
                cam, cmeta = ca_main(t), (112, 128)
                vm = vo_main(t)
                nmain = vm[1] - vm[0]
                GS = [[sub(RB, TB + (k * 4 + m4) * 2048, TB + (k * 4 + m4 + 1) * 2048, F32) for m4 in range(4)]
                      for k in range(2)]
                t1 = sub(RB, TB + 16384, TB + 18432, F32)
                t2 = sub(RB, TB + 18432, TB + 20480, F32)
                for mg in range(2):
                    wgr = ws.use(w_in_d[l, :, GR + mg * 512:GR + (mg + 1) * 512], 8, 512)
                    wga = ws.use(w_in_d[l, :, GA + mg * 512:GA + (mg + 1) * 512], 8, 512)
                    for m4 in range(4):
                        m = mg * 4 + m4
                        blk = lambda wv, K: [wv.c(k * 512 + m4 * 128, k * 512 + (m4 + 1) * 128) for k in range(K)]
                        bgr, bga = rbank(), rbank()
                        fm_mm(bgr, (0 * 8 + m) * 16, blk(wgr, 8), uT, t, cam, cmeta)
                        fm_mm(bga, (1 * 8 + m) * 16, blk(wga, 8), uT, t, cam, cmeta)
                        act(GS[0][m4].c(0, nmain), bgr.v.c(0, nmain), AF.Sigmoid)
                        act(GS[1][m4].c(0, nmain), bga.v.c(0, nmain), AF.Sigmoid)
                    wro = ws.use(w_ro_d[l, :, mg * 512:(mg + 1) * 512], 8, 512)
                    wao = ws.use(w_ao_d[l, :, mg * 512:(mg + 1) * 512], 4, 512)
                    for m4 in range(4):
                        m = mg * 4 + m4
                        blk = lambda wv, K: [wv.c(k * 512 + m4 * 128, k * 512 + (m4 + 1) * 128) for k in range(K)]
                        byr, bya = rbank(), rbank()
                        fm_mm(byr, (2 * 8 + m) * 16, blk(wro, 8), orT, t, cam, cmeta)
                        fm_mm(bya, (3 * 8 + m) * 16, blk(wao, 4), oaT, t, cam, cmeta)
                        tt(t1.c(0, nmain), GS[0][m4].c(0, nmain), byr.v.c(0, nmain), ALU.mult)
                        tt(t2.c(0, nmain), GS[1][m4].c(0, nmain), bya.v.c(0, nmain), ALU.mult)
                        tt(zT[m].c(vm[0], vm[1]), t1.c(0, nmain), t2.c(0, nmain), ALU.add)
                if t == 0:
                    sg = sub(RB, TB + 20480, TB + 21504, F32)
                    pr = sub(RB, TB + 21504, TB + 22528, F32)
                    act(sg, MB.v.c(0, 256), AF.Sigmoid)
                    tt(pr, sg, MB.v.c(256, 512), ALU.mult)
                    zm = span(z.c(0, 16) for z in zT)
                    zm.ap = zT3[:, :, 0:16]
                    tt(zm, pr.c(0, 128).r("p (m t) -> p m t", m=8), pr.c(128, 256).r("p (m t) -> p m t", m=8), ALU.add)

                chk("merge%d" % t)
                Mfirst[0] = True
                SSB.new()
                for mg in range(2):
                    wmx = ws.use(w_mx_d[l, :, mg * 512:(mg + 1) * 512], 8, 512, live=1)
                    for m4 in range(4):
                        m = mg * 4 + m4
                        b = rbank()
                        fm_mm(b, m * 16, [wmx.c(k * 512 + m4 * 128, k * 512 + (m4 + 1) * 128) for k in range(8)],
                              zT, t, vm, (0, 16))
                        cp(mixT[m].c(vm[0], vm[1]), b.v.c(0, nmain), eng="act")
                        s_ = sq[m % 2].c(0, nmain)
                        act(s_, mixT[m].c(vm[0], vm[1]), AF.Square)
                        mm(SSB.v.c(0, nmain), ones_b, s_, start=SSB.st(), stop=(m == 7))
                post_residual(t, l, 1, mixT, mixT3)

                chk("mix%d" % t)
                prenorm(t, l, 2)
                Mfirst[0] = True
                for fg in range(8):
                    w1 = ws.use(w_f1_d[l, :, fg * 512:(fg + 1) * 512], 8, 512, live=1)
                    for f4 in range(4):
                        f = fg * 4 + f4
                        b = rbank()
                        fm_mm(b, f * 16, [w1.c(k * 512 + f4 * 128, k * 512 + (f4 + 1) * 128) for k in range(8)],
                              uT, t, cam, cmeta)
                        r_ = ffr[f % 2].c(0, nmain)
                        act(r_, b.v.c(0, nmain), AF.Square)
                        stt(hid[f].c(vm[0], vm[1]), b.v.c(0, nmain), 0.0, r_, ALU.is_gt, ALU.mult)
                if t == 0:
                    r_ = ffr[0]
                    act(r_, MB.v.c(0, 512), AF.Square)
                    hm = span(hh.c(0, 16) for hh in hid)
                    hm.ap = hid3[:, :, 0:16]
                    stt(hm, MB.v.c(0, 512).r("p (f t) -> p f t", f=32), 0.0, r_.r("p (f t) -> p f t", f=32),
                        ALU.is_gt, ALU.mult)
                Mfirst[0] = True
                SSB.new()
                for mp in range(4):
                    bb = [rbank(), rbank()]
                    for fq in range(4):
                        w2 = ws.use(w_f2_d[l, fq * 1024:(fq + 1) * 1024, mp * 256:(mp + 1) * 256], 8, 256, live=1)
                        for f8 in range(8):
                            f = fq * 8 + f8
                            for m2 in range(2):
                                m = mp * 2 + m2
                                L = w2.c(f8 * 256 + m2 * 128, f8 * 256 + (m2 + 1) * 128)
                                mm(bb[m2].v.c(0, nmain), L, hid[f].c(vm[0], vm[1]), start=bb[m2].st(), stop=(f == 31))
                                if t == 0:
                                    mm(MB.v.c(m * 16, (m + 1) * 16), L, hid[f].c(0, 16), start=mstart(), stop=(f == 31))
                    for m2 in range(2):
                        m = mp * 2 + m2
                        cp(mixT[m].c(vm[0], vm[1]), bb[m2].v.c(0, nmain), eng="act")
                        s_ = sq[m % 2].c(0, nmain)
                        act(s_, mixT[m].c(vm[0], vm[1]), AF.Square)
                        mm(SSB.v.c(0, nmain), ones_b, s_, start=SSB.st(), stop=(m == 7))
                post_residual(t, l, 3, mixT, mixT3)
                chk("ffn%d" % t)

        P.muted = False
        P.phase = 3
        ostage = [sub(SC, 0, 4096, F32), sub(SC, 4096, 8192, F32)]
        for n in range(1, NCH):
            st = ostage[n % 2]
            for half in range(2):
                b = rbank()
                for j in range(4):
                    c = half * 4 + j
                    tr(b.v.c(j * 128, (j + 1) * 128), hT[c].c(n * 128, (n + 1) * 128), ident_f)
                cp(st.c(half * 512, (half + 1) * 512), b.v.c(0, 512), eng=("act" if half else "dve"))
            dma("sp", y_d[(n - 1) * 128: n * 128, :], st, slot=("o", n % 2), out_tok=V(None, "d_y", n, n + 1, 1))
        P.op("sp", lambda e: e.nop(), reads=[V(None, "d_y", 0, NCH + 1, 1)])

    print('sbuf bytes remaining', nc.sbuf_bytes_remaining)
    rec = WS(None)
    P0 = P
    emit_all(rec)
    P = Prog()
    for b in banks:
        b.first = True
    ring[0] = 0
    lastpe.clear()
    ws = WS(rec.rec)
    emit_all(ws)
    P.emit(nc, es)
    es.close()
    return nc, P


def _consts():
    f32 = np.float32
    p = np.arange(128)
    pos = (np.arange(NCH)[None, :] * 128 + p[:, None] - 112).astype(f32)

    def tab(theta, rot):
        fr = np.power(f32(theta), -np.arange(0, rot, 2, dtype=f32) / f32(rot)).astype(f32)
        ang = (pos[:, :, None] * fr[None, None, :]).astype(f32)
        return (np.cos(ang).astype(f32).reshape(128, -1), np.sin(ang).astype(f32).reshape(128, -1))

    cosr, sinr = tab(10000.0, 64)
    cosa, sina = tab(500000.0, 16)
    j = p[:, None]
    i = p[None, :]
    rfp = (-(np.minimum(i, j) + 1)).astype(f32)
    rbm = np.maximum(j - i, 0).astype(f32)
    pcols = np.stack([p + 1, 128 - p, 127 - p, p], axis=1).astype(f32)
    k, q = j, i
    mprev = np.where(k >= q, 0.0, NEG).astype(f32)
    mnext = np.where(k <= q, 0.0, NEG).astype(f32)
    mmeta = np.where(k >= 112, 0.0, NEG).astype(f32) + np.zeros((128, 128), f32)
    return dict(cosr=cosr, sinr=sinr, cosa=cosa, sina=sina, rfp=rfp, rbm=rbm, pcols=pcols,
                mprev=mprev, mnext=mnext, mmeta=mmeta, ident=np.eye(128, dtype=f32))


def _perm():
    cols = []
    for hp in range(4):
        a, b = 2 * hp, 2 * hp + 1
        for h in (a, b):
            cols += list(range(h * 64, (h + 1) * 64))
        for h in (a, b):
            cols += list(range(512 + h * 64, 512 + (h + 1) * 64))
        for h in (a, b):
            cols += list(range(1024 + h * 128, 1024 + (h + 1) * 128))
        for h in (a, b):
            cols += list(range(2048 + h * 128, 2048 + (h + 1) * 128))
    for j in range(4):
        cols += list(range(3072 + j * 64, 3072 + (j + 1) * 64))
        cols += list(range(3072 + (4 + j) * 64, 3072 + (5 + j) * 64))
    cols += list(range(3584, 5888))
    return np.asarray(cols)


_CACHE = {}


def host_inputs(x, meta_tokens, w_in, w_ret_o, w_att_o, w_mix_o, w_ff1, w_ff2, norm_mix_pre, norm_mix_post,
                norm_ff_pre, norm_ff_post, ret_decay, attn_sink, ncores=8):
    f32 = np.float32
    A = lambda a: np.ascontiguousarray(np.asarray(a, dtype=f32))
    w_in_p = np.ascontiguousarray(A(w_in)[:, :, _perm()])
    norms = np.stack([A(norm_mix_pre), A(norm_mix_post), A(norm_ff_pre), A(norm_ff_post)], axis=1)
    gains = np.ascontiguousarray(norms.reshape(4, 4, 8, 128).transpose(3, 0, 1, 2).reshape(128, 128))
    shared = dict(meta=A(meta_tokens), w_in=w_in_p, w_ret_o=A(w_ret_o), w_att_o=A(w_att_o), w_mix_o=A(w_mix_o),
                  w_ff1=A(w_ff1), w_ff2=A(w_ff2), gains=gains, ret_decay=A(ret_decay).reshape(64),
                  attn_sink=A(attn_sink).reshape(32))
    shared.update(_consts())
    xs = A(x)
    return [dict(shared, x=xs[b]) for b in range(ncores)]


def kernel(**inputs):
    if "nc" not in _CACHE:
        _CACHE["nc"] = build()[0]
    nc = _CACHE["nc"]
    in_maps = host_inputs(**inputs)
    res = run_bass_kernel_spmd(nc, in_maps, core_ids=list(range(8)))
    return np.stack([np.asarray(r["y"], dtype=np.float32) for r in res.results], axis=0)
```

```python
import math
import os
from contextlib import ExitStack
import numpy as np
import concourse.bass as bass
import concourse.mybir as mybir
from concourse.bass_utils import run_bass_kernel_spmd

F32 = mybir.dt.float32
BF16 = mybir.dt.bfloat16
AF = mybir.ActivationFunctionType
ALU = mybir.AluOpType

NCH = 17
T = NCH * 128
DEPTH = 4
EPS = 1e-6
NEG = -30000.0
TILES = [[0, 1, 2, 3, 4], [5, 6, 7, 8], [9, 10, 11, 12], [13, 14, 15, 16]]
HU = lambda hp: hp * 768
QA, KV, GR, GA = 3072, 3584, 3840, 4864


class V:
    __slots__ = ("ap", "reg", "lo", "hi", "esz")

    def __init__(s, ap, reg, lo, hi, esz):
        s.ap, s.reg, s.lo, s.hi, s.esz = ap, reg, lo, hi, esz

    def c(s, a, b):
        if s.reg.startswith("ps"):
            return V(s.ap[:, a:b], s.reg, s.lo, s.hi, s.esz)
        return V(s.ap[:, a:b], s.reg, s.lo + a * s.esz, s.lo + b * s.esz, s.esz)

    def p(s, a, b):
        return V(s.ap[a:b], s.reg, s.lo, s.hi, s.esz)

    def w(s, ap):
        return V(ap, s.reg, s.lo, s.hi, s.esz)

    def r(s, pat, **kw):
        return V(s.ap.rearrange(pat, **kw), s.reg, s.lo, s.hi, s.esz)


def span(vs):
    vs = list(vs)
    return V(None, vs[0].reg, min(v.lo for v in vs), max(v.hi for v in vs), vs[0].esz)


class Op:
    __slots__ = ("idx", "eng", "fn", "dma", "slot", "ndma", "phase", "deps", "sig", "signo", "semkey")


class Prog:
    ENGS = ("pe", "act", "dve", "pool", "sp")

    def __init__(s):
        s.ops = []
        s.segs = {}
        s.phase = 0
        s.muted = False
        s.marks = []

    def _access(s, v, idx, ekey, write, deps):
        write = write or v.reg.startswith("ps")
        segs = s.segs.setdefault(v.reg, [])
        lo, hi = v.lo, v.hi
        for x in (lo, hi):
            for i, sg in enumerate(segs):
                if sg[0] < x < sg[1]:
                    segs.insert(i + 1, [x, sg[1], sg[2], dict(sg[3])])
                    sg[1] = x
                    break
        cov = [sg for sg in segs if sg[0] >= lo and sg[1] <= hi]
        cur = lo
        new = []
        for sg in cov:
            if sg[0] > cur:
                new.append([cur, sg[0], None, {}])
            cur = sg[1]
        if cur < hi:
            new.append([cur, hi, None, {}])
        if new:
            segs.extend(new)
            segs.sort(key=lambda g: g[0])
        for sg in cov + new:
            if sg[2] is not None:
                deps.add(sg[2])
            if write:
                deps.update(sg[3].values())
                sg[2] = idx
                sg[3] = {}
            else:
                sg[3][ekey] = idx

    def op(s, eng, fn, reads=(), writes=(), dma=False, slot=None, ndma=1, force=()):
        if s.muted:
            return None
        o = Op()
        o.idx = len(s.ops)
        o.eng, o.fn, o.dma, o.slot, o.ndma, o.phase = eng, fn, dma, slot, ndma, s.phase
        o.sig = dma
        o.signo = 0
        ekey = ("dma", o.idx) if dma else eng
        deps = set()
        for v in reads:
            s._access(v, o.idx, ekey, False, deps)
        for v in writes:
            s._access(v, o.idx, ekey, True, deps)
        deps.discard(o.idx)
        best = {}
        keep = []
        for d in deps:
            dop = s.ops[d]
            if dop.dma:
                keep.append(d)
            elif dop.eng == "pe" and eng == "pe" and not dma:
                continue
            elif best.get(dop.eng, -1) < d:
                best[dop.eng] = d
        keep.extend(best.values())
        for d in force:
            if d is not None and d not in keep:
                keep.append(d)
        for d in keep:
            s.ops[d].sig = True
        o.deps = keep
        s.ops.append(o)
        return o

    def emit(s, nc, es):
        cnt = {}
        for o in s.ops:
            if o.dma:
                o.semkey = ("dma", o.slot)
                cnt[o.semkey] = cnt.get(o.semkey, 0) + 16 * o.ndma
                o.signo = cnt[o.semkey]
            elif o.sig:
                o.semkey = (o.eng, o.phase)
                cnt[o.semkey] = cnt.get(o.semkey, 0) + 1
                o.signo = cnt[o.semkey]
        sems = {k: es.enter_context(nc.semaphore("s%d" % i)) for i, k in enumerate(cnt)}
        s.nsems = len(sems)
        streams = {e: [o for o in s.ops if o.eng == e] for e in s.ENGS}
        ops = s.ops

        def run(en):
            def f(e):
                lastw = {}
                for o in streams[en]:
                    for d in o.deps:
                        dop = ops[d]
                        if lastw.get(dop.semkey, 0) < dop.signo:
                            lastw[dop.semkey] = dop.signo
                            e.wait_ge(sems[dop.semkey], dop.signo)
                    r = o.fn(e)
                    if o.sig:
                        if o.dma:
                            rs = r if isinstance(r, (list, tuple)) else [r]
                            assert len(rs) == o.ndma
                            for ri in rs:
                                ri.then_inc(sems[o.semkey], 16)
                        else:
                            r.then_inc(sems[o.semkey], 1)
            return f

        block = es.enter_context(nc.Block())
        block.tensor(run("pe"))
        block.scalar(run("act"))
        block.vector(run("dve"))
        block.gpsimd(run("pool"))
        block.sync(run("sp"))


class Bank:
    def __init__(s, v):
        s.v = v
        s.first = True

    def new(s):
        s.first = True
        return s

    def st(s):
        f = s.first
        s.first = False
        return f


def build(layers=(0, 1, 2, 3), stop=None):
    nc = bass.Bass("TRN2", target_bir_lowering=False)
    es = ExitStack()
    P = Prog()

    def din(name, shape):
        return nc.dram_tensor(name, list(shape), F32, kind="ExternalInput").ap()

    x_d = din("x", [2048, 1024])
    meta_d = din("meta", [16, 1024])
    w_in_d = din("w_in", [4, 1024, 5888])
    w_ro_d = din("w_ret_o", [4, 1024, 1024])
    w_ao_d = din("w_att_o", [4, 512, 1024])
    w_mx_d = din("w_mix_o", [4, 1024, 1024])
    w_f1_d = din("w_ff1", [4, 1024, 4096])
    w_f2_d = din("w_ff2", [4, 4096, 1024])
    gains_d = din("gains", [128, 128])
    rdec_d = din("ret_decay", [64])
    sink_d = din("attn_sink", [32])
    cosr_d = din("cosr", [128, NCH * 32])
    sinr_d = din("sinr", [128, NCH * 32])
    cosa_d = din("cosa", [128, NCH * 8])
    sina_d = din("sina", [128, NCH * 8])
    rfp_d = din("rfp", [128, 128])
    rb_d = din("rbm", [128, 128])
    pcols_d = din("pcols", [128, 4])
    mprev_d = din("mprev", [128, 128])
    mnext_d = din("mnext", [128, 128])
    mmeta_d = din("mmeta", [128, 128])
    ident_d = din("ident", [128, 128])
    y_d = nc.dram_tensor("y", [2048, 1024], F32, kind="ExternalOutput").ap()

    def DR(name):
        return V(None, "d_" + name, 0, 1, 1)

    def sb(name, ncols, dt):
        t = es.enter_context(nc.sbuf_tensor("s_" + name, [128, ncols], dt))
        esz = 4 if dt == F32 else 2
        return V(t[:], name, 0, ncols * esz, esz)

    def sub(rv, blo, bhi, dt=BF16):
        ap = rv.ap[:, blo // 2: bhi // 2]
        if dt == F32:
            ap = ap.bitcast(F32)
        return V(ap, rv.reg, rv.lo + blo, rv.lo + bhi, 4 if dt == F32 else 2)

    hT_all = sb("hT", 8 * T, F32)
    hT = [hT_all.c(c * T, (c + 1) * T) for c in range(8)]
    hT3 = hT_all.ap.rearrange("p (c t) -> p c t", c=8)
    cosr = sb("cosr", NCH * 32, F32)
    sinr = sb("sinr", NCH * 32, F32)
    cosa = sb("cosa", NCH * 8, F32)
    sina = sb("sina", NCH * 8, F32)
    rfp = sb("rfp", 128, F32)
    rbm = sb("rbm", 128, F32)
    pcols = sb("pcols", 4, F32)
    DT = sb("DT", 8 * 128, F32)
    ident_f = sb("identf", 128, F32)
    ident_b = sb("identb", 128, BF16)
    ones_b = sb("onesb", 128, BF16)
    mprev = sb("mprev", 128, BF16)
    mnext = sb("mnext", 128, BF16)
    mmeta = sb("mmeta", 128, BF16)
    gains = sb("gains", 128, F32)
    rdec = sb("rdec", 64, F32)
    sinkb = sb("sinkb", 32, F32)
    lg = sb("lg", 16, F32)
    nhalf = sb("nhalf", 1, F32)
    dcol = sb("dcol", 56, F32)
    qfc, qbc, kdfc, kdbc, cdf, cdb, esink = [dcol.c(i * 8, (i + 1) * 8) for i in range(7)]
    KaT = sb("KaT", T, BF16)
    Va = sb("Va", NCH * 130, BF16)
    SB = sb("SB", 9 * 1024, BF16)
    Srun = sb("Srun", 1024, F32)
    U_all = sb("U", 8 * 640, BF16)
    uT = [U_all.c(c * 640, (c + 1) * 640) for c in range(8)]
    RA = sb("RA", 16896 // 2, BF16)
    RB = sb("RB", 33792 // 2, BF16)
    SC = sb("SC", 8192 // 2, BF16)
    NW = 3
    wslots = [sb("w%d" % i, 4096, BF16) for i in range(NW)]

    orT = [sub(RA, k * 1280, (k + 1) * 1280) for k in range(8)]
    oaT = [sub(RA, 10240 + k * 1280, 10240 + (k + 1) * 1280) for k in range(4)]
    oaT3 = sub(RA, 10240, 15360).ap.rearrange("p (k t) -> p k t", k=4)
    mixT_all = sub(RA, 0, 16896, F32)
    mixT = [mixT_all.c(m * 528, (m + 1) * 528) for m in range(8)]
    mixT3 = mixT_all.ap.rearrange("p (m t) -> p m t", m=8)
    zT_all = sub(RB, 0, 8448)
    zT = [zT_all.c(m * 528, (m + 1) * 528) for m in range(8)]
    zT3 = zT_all.ap.rearrange("p (m t) -> p m t", m=8)
    hid_all = sub(RB, 0, 33792)
    hid = [hid_all.c(f * 528, (f + 1) * 528) for f in range(32)]
    hid3 = hid_all.ap.rearrange("p (f t) -> p f t", f=32)
    TB = 8448
    stage = [sub(RB, 0, 4096, F32), sub(RB, 4096, 8192, F32)]
    sq = [sub(SC, 0, 1024), sub(SC, 1024, 2048)]
    rt = sub(SC, 2048, 4160, F32)
    rstd = sub(SC, 4160, 6272, F32)
    scx = sub(SC, 6272, 8192, F32)
    ffr = [sub(SC, 0, 2048, F32), sub(SC, 2048, 4096, F32)]

    banks = []
    for i in range(8):
        t = es.enter_context(nc.psum_tensor("ps%d" % i, [128, 512], F32))
        banks.append(Bank(V(t[:], "ps%d" % i, 0, 2048, 4)))
    SSB, MB = banks[6], banks[7]
    ring = [0]

    def rbank():
        b = banks[ring[0] % 6]
        ring[0] += 1
        return b.new()

    def bfv(v):
        return V(v.ap.bitcast(BF16), v.reg, v.lo, v.hi, 2)

    def rw(x):
        return [x] if isinstance(x, V) else list(x)

    lastpe = {}

    def mm(out, lhsT, rhs, start, stop):
        sg_ = (lhsT.ap.base_partition(), lhsT.ap.shape[0])
        prev = lastpe.get(out.reg)
        force = [prev[1]] if (prev is not None and prev[0] != sg_) else []
        o = P.op("pe", lambda e: e.matmul(out.ap, lhsT.ap, rhs.ap, start=start, stop=stop, skip_group_check=True),
                 reads=[lhsT, rhs], writes=[out], force=force)
        if o is not None:
            lastpe[out.reg] = (sg_, o.idx)

    def tr(out, in_, idn):
        sg_ = (0, 128)
        prev = lastpe.get(out.reg)
        force = [prev[1]] if (prev is not None and prev[0] != sg_) else []
        o = P.op("pe", lambda e: e.transpose(out.ap, in_.ap, idn.ap), reads=[in_, idn], writes=[out], force=force)
        if o is not None:
            lastpe[out.reg] = (sg_, o.idx)

    def act(out, in_, func, scale=None, bias=None, accum=None, eng="act"):
        kw = {}
        rd = [in_]
        wr = [out]
        if scale is not None:
            if isinstance(scale, V):
                kw["scale"] = scale.ap
                rd.append(scale)
            else:
                kw["scale"] = float(scale)
        if bias is not None:
            if isinstance(bias, V):
                kw["bias"] = bias.ap
                rd.append(bias)
            else:
                kw["bias"] = float(bias)
        if accum is not None:
            kw["accum_out"] = accum.ap
            wr.append(accum)
        P.op("act", lambda e: e.activation(out.ap, in_.ap, func, **kw), reads=rd, writes=wr)

    def tt(out, a, b, op, eng="dve"):
        P.op(eng, lambda e: e.tensor_tensor(out.ap, a.ap, b.ap, op), reads=[a, b], writes=[out])

    def ts(out, a, s1, s2, op0, op1=None, eng="dve"):
        rd = [a]
        a1 = s1.ap if isinstance(s1, V) else s1
        a2 = s2.ap if isinstance(s2, V) else s2
        if isinstance(s1, V):
            rd.append(s1)
        if isinstance(s2, V):
            rd.append(s2)
        if op1 is None:
            P.op(eng, lambda e: e.tensor_scalar(out.ap, a.ap, a1, None, op0), reads=rd, writes=[out])
        else:
            P.op(eng, lambda e: e.tensor_scalar(out.ap, a.ap, a1, a2, op0, op1), reads=rd, writes=[out])

    def stt(out, a, sc, b, op0, op1, eng="dve"):
        rd = [a, b]
        a1 = sc.ap if isinstance(sc, V) else sc
        if isinstance(sc, V):
            rd.append(sc)
        P.op(eng, lambda e: e.scalar_tensor_tensor(out.ap, a.ap, a1, b.ap, op0, op1), reads=rd, writes=[out])

    def cp(out, in_, eng="dve"):
        if eng == "act":
            act(out, in_, AF.Copy)
        else:
            P.op(eng, lambda e: e.tensor_copy(out.ap, in_.ap), reads=[in_], writes=[out])

    def recip(out, in_):
        P.op("dve", lambda e: e.reciprocal(out.ap, in_.ap), reads=[in_], writes=[out])

    def rstd_from(out, ssq, n_scale, tmp):
        act(tmp, ssq, AF.Sqrt, scale=n_scale, bias=EPS)
        recip(out, tmp)

    def memset(v, val, eng="dve"):
        P.op(eng, lambda e: e.memset(v.ap, val), writes=[v])

    def dma(q, out, in_ap_or_v, slot, in_tok=None, out_tok=None):
        if isinstance(out, V):
            o_ap, wr = out.ap, [out]
        else:
            o_ap, wr = out, [out_tok]
        if isinstance(in_ap_or_v, V):
            i_ap, rd = in_ap_or_v.ap, [in_ap_or_v]
        else:
            i_ap, rd = in_ap_or_v, ([in_tok] if in_tok else [])
        P.op(q, lambda e: e.dma_start(out=o_ap, in_=i_ap), reads=rd, writes=wr, dma=True, slot=slot)

    class WS:
        def __init__(s, plan=None):
            s.plan = plan
            s.rec = []
            s.i = 0
            s.issued = 0

        def use(s, src, nk, ncols, live=2):
            if s.plan is None:
                s.rec.append((src, nk, ncols))
                return wslots[0]
            i = s.i
            s.i += 1
            if P.muted:
                return wslots[0]
            while s.issued < min(len(s.plan), i + NW + 1 - live):
                j = s.issued
                sr, k2, n2 = s.plan[j]
                sl = wslots[j % NW]
                dma("pool", sl.c(0, k2 * n2).w(sl.ap[:, 0:k2 * n2].rearrange("p (k n) -> p k n", k=k2)),
                    sr.rearrange("(k p) n -> p k n", p=128), slot=("w", j % NW))
                s.issued += 1
            return wslots[i % NW]

    def tc0(t):
        return TILES[t][0] * 128

    def g_main(t):
        return (128, 640) if t == 0 else (tc0(t), tc0(t) + 512)

    def ca_main(t):
        return (128, 640) if t == 0 else (0, 512)

    def vo_main(t):
        return (16, 528) if t == 0 else (0, 512)

    def gidx(l, j):
        return (l * 4 + j) * 8

    def emit_all(ws):
        Mfirst = [True]

        def mstart():
            f = Mfirst[0]
            Mfirst[0] = False
            return f

        P.phase = 0
        for (v, d, q) in ((cosr, cosr_d, "sp"), (sinr, sinr_d, "sp"), (cosa, cosa_d, "sp"), (sina, sina_d, "sp"),
                          (rfp, rfp_d, "sp"), (rbm, rb_d, "sp"), (pcols, pcols_d, "sp"), (ident_f, ident_d, "sp"),
                          (gains, gains_d, "sp"), (ident_b, ident_d, "pool"), (mprev, mprev_d, "pool"),
                          (mnext, mnext_d, "pool"), (mmeta, mmeta_d, "pool")):
            dma(q, v, d, slot=("c", v.reg))
        dma("sp", rdec, rdec_d.partition_broadcast(128), slot=("c", "rdec"))
        dma("sp", sinkb, sink_d.partition_broadcast(128), slot=("c", "sink"))
        memset(ones_b, 1.0)
        memset(nhalf, -0.5)
        memset(Va, 1.0)
        memset(U_all, 0.0)

        for n in range(NCH):
            st = stage[n % 2]
            if n == 0:
                memset(st, 0.0)
                dma("sp", st.p(112, 128), meta_d, slot=("st", 0))
            else:
                dma("sp", st, x_d[(n - 1) * 128: n * 128, :], slot=("st", n % 2))
            for half in range(2):
                b = rbank()
                for j in range(4):
                    c = half * 4 + j
                    tr(b.v.c(j * 128, (j + 1) * 128), st.c(c * 128, (c + 1) * 128), ident_f)
                o = span(hT[half * 4 + j].c(n * 128, (n + 1) * 128) for j in range(4))
                o.ap = hT3[:, half * 4:(half + 1) * 4, n * 128:(n + 1) * 128]
                cp(o, b.v.r("p (j t) -> p j t", j=4), eng=("act" if half else "dve"))

        def prenorm(t, l, j):
            pieces = [g_main(t)] + ([(112, 128)] if t == 0 else [])
            if t == 0:
                z = span(u.c(0, 112) for u in uT)
                z.ap = U_all.ap.rearrange("p (c t) -> p c t", c=8)[:, :, 0:112]
                memset(z, 0.0)
            for (c0, c1) in pieces:
                n = c1 - c0
                ssb = SSB.new() if n > 16 else rbank()
                for c in range(8):
                    s_ = sq[c % 2].c(0, n)
                    act(s_, hT[c].c(c0, c1), AF.Square)
                    mm(ssb.v.c(0, n), ones_b, s_, start=ssb.st(), stop=(c == 7))
                rstd_from(rstd.c(0, n), ssb.v.c(0, n), 1.0 / 1024, rt.c(0, n))
                for c in range(8):
                    g = gains.c(gidx(l, j) + c, gidx(l, j) + c + 1)
                    stt(uT[c].c(c0 - tc0(t), c1 - tc0(t)), hT[c].c(c0, c1), g, rstd.c(0, n), ALU.mult, ALU.mult)

        def post_residual(t, l, j, outT, outT3):
            a, b = vo_main(t)
            n = b - a
            ga, gb = g_main(t)
            rstd_from(rstd.c(0, n), SSB.v.c(0, n), 1.0 / 1024, rt.c(0, n))
            for m in range(8):
                g = gains.c(gidx(l, j) + m, gidx(l, j) + m + 1)
                tt(outT[m].c(a, b), outT[m].c(a, b), rstd.c(0, n), ALU.mult)
                stt(hT[m].c(ga, gb), outT[m].c(a, b), g, hT[m].c(ga, gb), ALU.mult, ALU.add)
            if t == 0:
                om = span(o.c(0, 16) for o in outT)
                om.ap = outT3[:, :, 0:16]
                cp(om, MB.v.c(0, 128).r("p (m t) -> p m t", m=8), eng="act")
                sqm = sub(SC, 0, 256)
                act(sqm, MB.v.c(0, 128), AF.Square)
                b2 = rbank()
                for m in range(8):
                    mm(b2.v.c(0, 16), ones_b, sqm.c(m * 16, (m + 1) * 16), start=b2.st(), stop=(m == 7))
                rtm = scx.c(0, 16)
                rsm = scx.c(16, 32)
                rstd_from(rsm, b2.v.c(0, 16), 1.0 / 1024, rtm)
                t1 = scx.c(32, 160)
                t13 = t1.r("p (m t) -> p m t", m=8)
                tt(t13, om, rsm.w(rsm.ap.unsqueeze(1).to_broadcast([128, 8, 16])), ALU.mult)
                gv = gains.c(gidx(l, j), gidx(l, j) + 8)
                tt(t13, t13, gv.w(gv.ap.unsqueeze(2).to_broadcast([128, 8, 16])), ALU.mult)
                hm = span(h.c(112, 128) for h in hT)
                hm.ap = hT3[:, :, 112:128]
                tt(hm, hm, t13, ALU.add)

        def fm_mm(bank, mcol, lhs_list, rhs_list, t, rng_main, rng_meta, last=True):
            K = len(lhs_list)
            a, b = rng_main
            for k in range(K):
                mm(bank.v.c(0, b - a), lhs_list[k], rhs_list[k].c(a, b), start=bank.st(), stop=(last and k == K - 1))
            if t == 0 and mcol is not None:
                ma, mb_ = rng_meta
                for k in range(K):
                    mm(MB.v.c(mcol, mcol + 16), lhs_list[k], rhs_list[k].c(ma, mb_), start=mstart(),
                       stop=(last and k == K - 1))

        def rope64(dst, src, n, nq, A, B1, B2, eng="dve"):
            cs = cosr.c(n * 32, (n + 1) * 32)
            sn = sinr.c(n * 32, (n + 1) * 32)
            s4 = src.ap.rearrange("p (q a f) -> p q a f", q=nq, a=2)
            d4 = dst.ap.rearrange("p (q a f) -> p q a f", q=nq, a=2)
            A4 = A.ap.rearrange("p (q a f) -> p q a f", q=nq, a=2)
            tt(A.w(A4), src.w(s4), cs.w(cs.ap.unsqueeze(1).unsqueeze(1).to_broadcast([128, nq, 2, 32])), ALU.mult, eng=eng)
            snb = sn.w(sn.ap.unsqueeze(1).to_broadcast([128, nq, 32]))
            b1 = B1.r("p (q f) -> p q f", q=nq)
            b2 = B2.r("p (q f) -> p q f", q=nq)
            tt(b1, src.w(s4[:, :, 1, :]), snb, ALU.mult, eng=eng)
            tt(b2, src.w(s4[:, :, 0, :]), snb, ALU.mult, eng=eng)
            tt(dst.w(d4[:, :, 0, :]), A.w(A4[:, :, 0, :]), b1, ALU.subtract, eng=eng)
            tt(dst.w(d4[:, :, 1, :]), A.w(A4[:, :, 1, :]), b2, ALU.add, eng=eng)

        def warm(k):
            for _ in range(k):
                mm(MB.v.c(0, 512), ones_b, U_all.c(0, 512), start=True, stop=True)

        def pipeline(items, stages, skews, first=0, junk=0):
            n = len(items)
            order = list(range(first)) + sorted(range(first, len(stages)), key=lambda j: (-skews[j], j))
            for s_ in range(n + max(skews)):
                for j in order:
                    fn, sk = stages[j], skews[j]
                    i = s_ - sk
                    if 0 <= i < n:
                        fn(i, items[i])
                warm(junk)

        def chk(name):
            P.marks.append((name, sum(1 for o in P.ops if o.eng == "pe")))
            if stop == name:
                P.muted = True

        for l in layers:
            P.phase = l
            act(lg, rdec.c(l * 16, (l + 1) * 16), AF.Exp)
            ts(lg, lg, -1.0, None, ALU.mult)
            tA = sub(RB, TB, TB + 512, F32)
            tB = sub(RB, TB + 512, TB + 1024, F32)
            for h in range(8):
                ts(tA, rfp, lg.c(h, h + 1), None, ALU.mult)
                stt(tB, rbm, lg.c(8 + h, 9 + h), tA, ALU.mult, ALU.add)
                act(DT.c(h * 128, (h + 1) * 128), tB, AF.Exp)
            l8 = math.log(0.125)
            act(qfc, lg.c(0, 8), AF.Exp, scale=pcols.c(0, 1), bias=l8)
            act(qbc, lg.c(8, 16), AF.Exp, scale=pcols.c(1, 2), bias=l8)
            act(kdfc, lg.c(0, 8), AF.Exp, scale=pcols.c(2, 3))
            act(kdbc, lg.c(8, 16), AF.Exp, scale=pcols.c(3, 4))
            act(cdf, lg.c(0, 8), AF.Exp, scale=128.0)
            act(cdb, lg.c(8, 16), AF.Exp, scale=128.0)
            act(esink, sinkb.c(l * 8, (l + 1) * 8), AF.Exp)

            chk("L%d_start" % l)
            memset(Srun, 0.0)
            for t in (3, 2, 1, 0):
                prenorm(t, l, 0)
                chunks = TILES[t]
                wkv = ws.use(w_in_d[l, :, KV:KV + 256], 8, 256, live=1)
                for n in chunks:
                    nl = n - chunks[0]
                    base = TB + (n % 2) * 2048
                    xa = sub(RB, base, base + 1024, F32)
                    rA = sub(RB, base + 1024, base + 1152, F32)
                    rB1 = sub(RB, base + 1152, base + 1216, F32)
                    rB2 = sub(RB, base + 1216, base + 1280, F32)
                    kab = sub(RB, base + 1280, base + 1536)
                    b = banks[4].new()
                    for c in range(8):
                        mm(b.v.c(0, 256), uT[c].c(nl * 128, (nl + 1) * 128), wkv.c(c * 256, (c + 1) * 256),
                           start=b.st(), stop=(c == 7))
                    cp(xa, b.v.c(0, 256), eng="act")
                    xk = xa.c(0, 128)
                    x4 = xk.ap.rearrange("p (h d) -> p h d", h=2)
                    x16 = xk.w(x4[:, :, 0:16].rearrange("p h (a f) -> p h a f", a=2))
                    cs = cosa.c(n * 8, (n + 1) * 8)
                    sn = sina.c(n * 8, (n + 1) * 8)
                    tt(rA.r("p (h a f) -> p h a f", h=2, a=2), x16,
                       cs.w(cs.ap.unsqueeze(1).unsqueeze(1).to_broadcast([128, 2, 2, 8])), ALU.mult)
                    snb = sn.w(sn.ap.unsqueeze(1).to_broadcast([128, 2, 8]))
                    tt(rB1.r("p (h f) -> p h f", h=2), xk.w(x4[:, :, 8:16]), snb, ALU.mult)
                    tt(rB2.r("p (h f) -> p h f", h=2), xk.w(x4[:, :, 0:8]), snb, ALU.mult)
                    k3 = kab.ap.rearrange("p (h d) -> p h d", h=2)
                    rA4 = rA.ap.rearrange("p (h a f) -> p h a f", h=2, a=2)
                    tt(kab.w(k3[:, :, 0:8]), rA.w(rA4[:, :, 0, :]), rB1.r("p (h f) -> p h f", h=2), ALU.subtract)
                    tt(kab.w(k3[:, :, 8:16]), rA.w(rA4[:, :, 1, :]), rB2.r("p (h f) -> p h f", h=2), ALU.add)
                    cp(kab.w(k3[:, :, 16:64]), xk.w(x4[:, :, 16:64]))
                    van = Va.c(n * 130, (n + 1) * 130)
                    cp(van.w(van.ap.rearrange("p (g d) -> p g d", g=2)[:, :, 0:64]),
                       xa.c(128, 256).r("p (g d) -> p g d", g=2), eng="act")
                    b2 = banks[5].new()
                    tr(bfv(b2.v.c(0, 64)), kab, ident_b)
                    cp(KaT.c(n * 128, (n + 1) * 128), bfv(b2.v.c(0, 64)))
                items = [(hp, n) for hp in range(4) for n in reversed(chunks)]
                units = {}

                def p1A(i, it, t=t, chunks=chunks):
                    hp, n = it
                    if n == chunks[-1]:
                        units[hp] = ws.use(w_in_d[l, :, HU(hp) + 128:HU(hp) + 512], 8, 384, live=1)
                    wu = units[hp]
                    nl = n - chunks[0]
                    b = banks[i % 2].new()
                    for c in range(8):
                        mm(b.v.c(0, 384), uT[c].c(nl * 128, (nl + 1) * 128), wu.c(c * 384, (c + 1) * 384),
                           start=b.st(), stop=(c == 7))

                def p1set(i):
                    base = TB + (i % 4) * 3584
                    return dict(x=sub(RB, base, base + 512, F32), A=sub(RB, base + 512, base + 1024, F32),
                                B1=sub(RB, base + 1024, base + 1280, F32), B2=sub(RB, base + 1280, base + 1536, F32),
                                kr=sub(RB, base + 1536, base + 2048, F32), Kd=sub(RB, base + 2048, base + 2560),
                                Vb=sub(RB, base + 2560, base + 3072))

                def p1B(i, it):
                    hp, n = it
                    S = p1set(i)
                    b = banks[i % 2]
                    cp(S["x"], b.v.c(0, 128), eng="act")
                    cp(S["Vb"], b.v.c(128, 384), eng="act")
                    rope64(S["kr"], S["x"], n, 2, S["A"], S["B1"], S["B2"])
                    kr = S["kr"]
                    kb = kr.w(kr.ap.rearrange("p (h d) -> p h d", h=2).unsqueeze(2).to_broadcast([128, 2, 2, 64]))
                    dc = kdbc.c(2 * hp, 2 * hp + 2)
                    tt(S["Kd"].r("p (h a d) -> p h a d", h=2, a=2), kb,
                       dc.w(dc.ap.unsqueeze(2).unsqueeze(3).to_broadcast([128, 2, 2, 64])), ALU.mult)

                def p1D(i, it):
                    S = p1set(i)
                    b = banks[2 + i % 2].new()
                    for j in range(2):
                        mm(b.v.c(j * 128, (j + 1) * 128), S["Kd"].c(j * 128, (j + 1) * 128),
                           S["Vb"].c(j * 128, (j + 1) * 128), start=b.st(), stop=True)

                def p1E(i, it):
                    hp, n = it
                    pp = n % 2
                    b = banks[2 + i % 2]
                    sr = Srun.c(hp * 256, (hp + 1) * 256)
                    slot = SB.c((n // 2) * 1024 + hp * 256, (n // 2) * 1024 + (hp + 1) * 256)
                    cp(slot.p(pp * 64, (pp + 1) * 64), sr.p(pp * 64, (pp + 1) * 64), eng="act")
                    dc = cdb.c(2 * hp, 2 * hp + 2)
                    sr3 = sr.r("p (h e) -> p h e", h=2)
                    tt(sr3, sr3, dc.w(dc.ap.unsqueeze(2).to_broadcast([128, 2, 128])), ALU.mult)
                    tt(sr, sr, b.v.c(0, 256), ALU.add)

                pipeline(items, [p1A, p1B, p1D, p1E], [0, 0, 2, 2], first=2, junk=int(os.environ.get('JP', '4')))
            chk("pass1")

            memset(Srun, 0.0)
            for t in range(4):
                chunks = TILES[t]
                prenorm(t, l, 0)
                items = [(hp, n) for hp in range(4) for n in chunks]
                unitsA, unitsB = {}, {}
                junk = sub(SC, 6272, 7296, F32)

                def rset(i):
                    o = {}
                    eb = (i % 2) * 5120
                    lb = 10240 + (i % 4) * 5696
                    off = [0, 0]

                    def take(name, nb, dt=BF16, late=False):
                        k = 1 if late else 0
                        base = lb if late else eb
                        o[name] = sub(RB, base + off[k], base + off[k] + nb, dt)
                        off[k] += nb
                    take("x", 1024, F32)
                    take("A", 1024, F32)
                    take("B1", 512, F32)
                    take("B2", 512, F32)
                    take("qk", 1024, F32)
                    take("QQ", 512)
                    take("Kdup", 512)
                    take("G", 1024, F32, True)
                    take("Kd", 512, BF16, True)
                    take("Vb", 512, BF16, True)
                    take("QKT", 1024, BF16, True)
                    take("PT", 512, BF16, True)
                    take("Sfb", 512, BF16, True)
                    take("o32", 1024, F32, True)
                    take("og", 512, BF16, True)
                    take("sm", 64, F32, True)
                    assert off[0] == 5120 and off[1] == 5696
                    return o

                def rA_(i, it, chunks=chunks):
                    hp, n = it
                    if n == chunks[0]:
                        unitsA[hp] = ws.use(w_in_d[l, :, HU(hp):HU(hp) + 512], 8, 512)
                        unitsB[hp] = ws.use(w_in_d[l, :, HU(hp) + 512:HU(hp) + 768], 8, 256)
                    wa, wb = unitsA[hp], unitsB[hp]
                    nl = n - chunks[0]
                    b = banks[0].new()
                    for c in range(8):
                        mm(b.v.c(0, 512), uT[c].c(nl * 128, (nl + 1) * 128), wa.c(c * 512, (c + 1) * 512),
                           start=b.st(), stop=(c == 7))
                    b = banks[1].new()
                    for c in range(8):
                        mm(b.v.c(0, 256), uT[c].c(nl * 128, (nl + 1) * 128), wb.c(c * 256, (c + 1) * 256),
                           start=b.st(), stop=(c == 7))

                def hb(col, hp, d):
                    dc = col.c(2 * hp, 2 * hp + 2)
                    return dc.w(dc.ap.unsqueeze(2).to_broadcast([128, 2, d]))

                def rB_(i, it):
                    hp, n = it
                    pp = n % 2
                    S = rset(i)
                    cp(S["x"], banks[0].v.c(0, 256), eng="act")
                    cp(S["Vb"], banks[0].v.c(256, 512), eng="act")
                    act(S["G"], banks[1].v.c(0, 256), AF.Silu)
                    rope64(S["qk"], S["x"], n, 4, S["A"], S["B1"], S["B2"])
                    q = S["qk"].c(0, 128).r("p (h d) -> p h d", h=2)
                    k = S["qk"].c(128, 256)
                    QQ3 = S["QQ"].ap.rearrange("p (h a d) -> p h a d", h=2, a=2)
                    tt(S["QQ"].w(QQ3[:, :, pp, :]), q, hb(qbc, hp, 64), ALU.mult)
                    tt(S["QQ"].w(QQ3[:, :, 1 - pp, :]), q, hb(qfc, hp, 64), ALU.mult)
                    kb = k.w(k.ap.rearrange("p (h d) -> p h d", h=2).unsqueeze(2).to_broadcast([128, 2, 2, 64]))
                    dc = kdfc.c(2 * hp, 2 * hp + 2)
                    tt(S["Kd"].r("p (h a d) -> p h a d", h=2, a=2), kb,
                       dc.w(dc.ap.unsqueeze(2).unsqueeze(3).to_broadcast([128, 2, 2, 64])), ALU.mult, eng="pool")
                    cp(S["Kdup"].r("p (h a d) -> p h a d", h=2, a=2), kb)

                def rC_(i, it):
                    S = rset(i)
                    b = banks[2].new()
                    for j in range(2):
                        tr(bfv(b.v.c(j * 64, (j + 1) * 64)), S["QQ"].c(j * 128, (j + 1) * 128), ident_b)
                    for j in range(2):
                        tr(bfv(b.v.c(128 + j * 64, 128 + (j + 1) * 64)), S["Kdup"].c(j * 128, (j + 1) * 128), ident_b)

                def rC2_(i, it):
                    S = rset(i)
                    cp(S["QKT"], bfv(banks[2].v.c(0, 256)))

                def rD_(i, it):
                    hp, n = it
                    pp = n % 2
                    S = rset(i)
                    hf = 1 - pp
                    b = banks[3].new()
                    QKT = S["QKT"]
                    for j in range(2):
                        mm(b.v.c(256 + j * 128, 256 + (j + 1) * 128), S["Kd"].c(j * 128, (j + 1) * 128),
                           S["Vb"].c(j * 128, (j + 1) * 128), start=b.st(), stop=True)
                    for j in range(2):
                        mm(b.v.c(j * 128, (j + 1) * 128), QKT.p(hf * 64, (hf + 1) * 64).c(256 + j * 128, 256 + (j + 1) * 128),
                           QKT.p(hf * 64, (hf + 1) * 64).c(j * 128, (j + 1) * 128), start=b.st(), stop=True)

                def rE_(i, it):
                    hp, n = it
                    pp = n % 2
                    hf = 1 - pp
                    S = rset(i)
                    b = banks[3]
                    tt(S["PT"], b.v.c(0, 256), DT.c(hp * 256, (hp + 1) * 256), ALU.mult)
                    sr = Srun.c(hp * 256, (hp + 1) * 256)
                    cp(S["Sfb"].p(hf * 64, (hf + 1) * 64), sr.p(hf * 64, (hf + 1) * 64), eng="pool")
                    slot = SB.c((n // 2) * 1024 + hp * 256, (n // 2) * 1024 + (hp + 1) * 256)
                    cp(S["Sfb"].p(pp * 64, (pp + 1) * 64), slot.p(pp * 64, (pp + 1) * 64), eng="act")
                    sr3 = sr.r("p (h e) -> p h e", h=2)
                    tt(sr3, sr3, hb(cdf, hp, 128), ALU.mult)
                    tt(sr, sr, b.v.c(256, 512), ALU.add)

                def rF_(i, it):
                    hp, n = it
                    S = rset(i)
                    b = banks[4].new()
                    QKT = S["QKT"]
                    for j in range(2):
                        mm(b.v.c(j * 128, (j + 1) * 128), S["PT"].c(j * 128, (j + 1) * 128), S["Vb"].c(j * 128, (j + 1) * 128),
                           start=b.st(), stop=False)
                    for j in range(2):
                        mm(b.v.c(j * 128, (j + 1) * 128), QKT.c(j * 128, (j + 1) * 128), S["Sfb"].c(j * 128, (j + 1) * 128),
                           start=b.st(), stop=True)

                def rG_(i, it):
                    S = rset(i)
                    b = banks[4]
                    sm = S["sm"]
                    s1, s2, mean, msq, var, rs = [sm.c(2 * j, 2 * j + 2) for j in range(6)]
                    for j in range(2):
                        act(S["o32"].c(j * 128, (j + 1) * 128), b.v.c(j * 128, (j + 1) * 128), AF.Copy, accum=s1.c(j, j + 1))
                    for j in range(2):
                        act(junk.c(0, 128), S["o32"].c(j * 128, (j + 1) * 128), AF.Square, accum=s2.c(j, j + 1))
                    ts(mean, s1, 1.0 / 128, None, ALU.mult)
                    tt(msq, mean, mean, ALU.mult)
                    ts(var, s2, 1.0 / 128, EPS, ALU.mult, ALU.add)
                    tt(var, var, msq, ALU.subtract)
                    tt(rs, var, nhalf.w(nhalf.ap.to_broadcast([128, 2])), ALU.pow, eng="pool")
                    o3 = S["o32"].r("p (h e) -> p h e", h=2)
                    tt(o3, o3, mean.w(mean.ap.unsqueeze(2).to_broadcast([128, 2, 128])), ALU.subtract)
                    tt(o3, o3, rs.w(rs.ap.unsqueeze(2).to_broadcast([128, 2, 128])), ALU.mult)
                    tt(S["og"], S["o32"], S["G"], ALU.mult, eng="pool")

                def rH_(i, it):
                    S = rset(i)
                    b = banks[5].new()
                    for j in range(2):
                        tr(bfv(b.v.c(j * 64, (j + 1) * 64)), S["og"].c(j * 128, (j + 1) * 128), ident_b)

                def rH2_(i, it, chunks=chunks):
                    hp, n = it
                    nl = n - chunks[0]
                    o = span(orT[2 * hp + j].c(nl * 128, (nl + 1) * 128) for j in range(2))
                    o.ap = sub(RA, 0, 10240).ap.rearrange("p (k t) -> p k t", k=8)[:, 2 * hp:2 * hp + 2, nl * 128:(nl + 1) * 128]
                    cp(o, bfv(banks[5].v.c(0, 128)).r("p (j t) -> p j t", j=2), eng="act")

                pipeline(items, [rA_, rB_, rC_, rC2_, rD_, rE_, rF_, rG_, rH_, rH2_], [0, 0, 1, 1, 2, 2, 3, 3, 4, 4], first=2, junk=int(os.environ.get('JR', '6')))
                chk("ret%d" % t)
                wqa = ws.use(w_in_d[l, :, QA:QA + 512], 8, 512, live=1)

                def aset(n):
                    base = TB + (n % 2) * 11264
                    return dict(
                        xq=sub(RB, base, base + 2048, F32), rA=sub(RB, base + 2048, base + 2560, F32),
                        rB1=sub(RB, base + 2560, base + 2816, F32), rB2=sub(RB, base + 2816, base + 3072, F32),
                        qab=sub(RB, base + 3072, base + 4096), QaT=sub(RB, base + 4096, base + 5120),
                        Pa=[[sub(RB, base + 5120 + (g * 2 + j) * 1024, base + 5120 + (g * 2 + j + 1) * 1024)
                             for j in range(2)] for g in range(2)],
                        oat=sub(RB, base + 9216, base + 10240), den=sub(RB, base + 10240, base + 10272, F32))

                def a0(i, n, chunks=chunks):
                    S = aset(n)
                    nl = n - chunks[0]
                    xq, rA, rB1, rB2, qab = S["xq"], S["rA"], S["rB1"], S["rB2"], S["qab"]
                    b = banks[0].new()
                    for c in range(8):
                        mm(b.v.c(0, 512), uT[c].c(nl * 128, (nl + 1) * 128), wqa.c(c * 512, (c + 1) * 512),
                           start=b.st(), stop=(c == 7))
                    cp(xq, b.v.c(0, 512), eng="act")
                    x4 = xq.ap.rearrange("p (h d) -> p h d", h=8)
                    x16 = xq.w(x4[:, :, 0:16].rearrange("p h (a f) -> p h a f", a=2))
                    cs = cosa.c(n * 8, (n + 1) * 8)
                    sn = sina.c(n * 8, (n + 1) * 8)
                    rA4 = rA.ap.rearrange("p (h a f) -> p h a f", h=8, a=2)
                    tt(rA.w(rA4), x16, cs.w(cs.ap.unsqueeze(1).unsqueeze(1).to_broadcast([128, 8, 2, 8])), ALU.mult)
                    snb = sn.w(sn.ap.unsqueeze(1).to_broadcast([128, 8, 8]))
                    tt(rB1.r("p (h f) -> p h f", h=8), xq.w(x4[:, :, 8:16]), snb, ALU.mult)
                    tt(rB2.r("p (h f) -> p h f", h=8), xq.w(x4[:, :, 0:8]), snb, ALU.mult)
                    q3 = qab.ap.rearrange("p (h d) -> p h d", h=8)
                    tt(qab.w(q3[:, :, 0:8]), rA.w(rA4[:, :, 0, :]), rB1.r("p (h f) -> p h f", h=8), ALU.subtract)
                    tt(qab.w(q3[:, :, 8:16]), rA.w(rA4[:, :, 1, :]), rB2.r("p (h f) -> p h f", h=8), ALU.add)
                    cp(qab.w(q3[:, :, 16:64]), xq.w(x4[:, :, 16:64]), eng="pool")

                def a1(i, n):
                    S = aset(n)
                    b = banks[1].new()
                    for j in range(4):
                        tr(bfv(b.v.c(j * 64, (j + 1) * 64)), S["qab"].c(j * 128, (j + 1) * 128), ident_b)
                    cp(S["QaT"], bfv(b.v.c(0, 256)))

                def a2(i, n):
                    S = aset(n)
                    QaT, Pa, oat, den = S["QaT"], S["Pa"], S["oat"], S["den"]
                    blocks = [(0, mmeta)]
                    if n >= 2:
                        blocks.append((n - 1, mprev))
                    if n >= 1:
                        blocks.append((n, None))
                    if n <= 15:
                        blocks.append((n + 1, mnext))
                    ov = [banks[4].new(), banks[5].new()]
                    steps = [(bi, m, mask, g) for bi, (m, mask) in enumerate(blocks) for g in range(2)]

                    def sc_(k):
                        bi, m, mask, g = steps[k]
                        bs = banks[2 + k % 2].new()
                        if mask is not None:
                            mm(bs.v.r("p (j q) -> p j q", j=4), ident_b,
                               mask.w(mask.ap.unsqueeze(1).to_broadcast([128, 4, 128])), start=bs.st(), stop=False)
                        for j in range(4):
                            mm(bs.v.c(j * 128, (j + 1) * 128), KaT.p(g * 64, (g + 1) * 64).c(m * 128, (m + 1) * 128),
                               QaT.p(g * 64, (g + 1) * 64).c(j * 128, (j + 1) * 128), start=bs.st(), stop=True)
                        act(Pa[g][bi % 2], bs.v.c(0, 512), AF.Exp, scale=0.125)

                    def pv_(k):
                        bi, m, mask, g = steps[k]
                        pa = Pa[g][bi % 2]
                        for j in range(4):
                            mm(ov[g].v.c(j * 65, (j + 1) * 65), pa.c(j * 128, (j + 1) * 128),
                               Va.c(m * 130 + g * 65, m * 130 + (g + 1) * 65), start=ov[g].st(),
                               stop=(bi == len(blocks) - 1))

                    for k in range(len(steps) + 1):
                        if k < len(steps):
                            sc_(k)
                        if k >= 1:
                            pv_(k - 1)
                        warm(int(os.environ.get('JA', '1')))
                    for g in range(2):
                        o3 = ov[g].v.c(0, 260)
                        o3a = o3.ap.rearrange("p (j d) -> p j d", j=4)
                        dn = den.c(g * 4, (g + 1) * 4)
                        tt(dn, o3.w(o3a[:, :, 64]), esink.c(g * 4, (g + 1) * 4), ALU.add)
                        recip(dn, dn)
                        tt(oat.c(g * 256, (g + 1) * 256).r("p (j d) -> p j d", j=4), o3.w(o3a[:, :, 0:64]),
                           dn.w(dn.ap.unsqueeze(2).to_broadcast([128, 4, 64])), ALU.mult)

                def a3(i, n, chunks=chunks):
                    S = aset(n)
                    nl = n - chunks[0]
                    b = banks[0].new()
                    for k in range(4):
                        tr(bfv(b.v.c(k * 64, (k + 1) * 64)), S["oat"].c(k * 128, (k + 1) * 128), ident_b)
                    o = span(oaT[k].c(nl * 128, (nl + 1) * 128) for k in range(4))
                    o.ap = oaT3[:, :, nl * 128:(nl + 1) * 128]
                    cp(o, bfv(b.v.c(0, 256)).r("p (k t) -> p k t", k=4), eng="act")

                pipeline(chunks, [a0, a1, a2, a3], [0, 1, 2, 3], first=1)

                chk("att%d" % t)
                Mfirst[0] = True
                cam, cmeta = ca_main(t), (112, 128)
                vm = vo_main(t)
                nmain = vm[1] - vm[0]
                GS = [[sub(RB, TB + (k * 4 + m4) * 2048, TB + (k * 4 + m4 + 1) * 2048, F32) for m4 in range(4)]
                      for k in range(2)]
                t1 = sub(RB, TB + 16384, TB + 18432, F32)
                t2 = sub(RB, TB + 18432, TB + 20480, F32)
                for mg in range(2):
                    wgr = ws.use(w_in_d[l, :, GR + mg * 512:GR + (mg + 1) * 512], 8, 512)
                    wga = ws.use(w_in_d[l, :, GA + mg * 512:GA + (mg + 1) * 512], 8, 512)
                    for m4 in range(4):
                        m = mg * 4 + m4
                        blk = lambda wv, K: [wv.c(k * 512 + m4 * 128, k * 512 + (m4 + 1) * 128) for k in range(K)]
                        bgr, bga = rbank(), rbank()
                        fm_mm(bgr, (0 * 8 + m) * 16, blk(wgr, 8), uT, t, cam, cmeta)
                        fm_mm(bga, (1 * 8 + m) * 16, blk(wga, 8), uT, t, cam, cmeta)
                        act(GS[0][m4].c(0, nmain), bgr.v.c(0, nmain), AF.Sigmoid)
                        act(GS[1][m4].c(0, nmain), bga.v.c(0, nmain), AF.Sigmoid)
                    wro = ws.use(w_ro_d[l, :, mg * 512:(mg + 1) * 512], 8, 512)
                    wao = ws.use(w_ao_d[l, :, mg * 512:(mg + 1) * 512], 4, 512)
                    for m4 in range(4):
                        m = mg * 4 + m4
                        blk = lambda wv, K: [wv.c(k * 512 + m4 * 128, k * 512 + (m4 + 1) * 128) for k in range(K)]
                        byr, bya = rbank(), rbank()
                        fm_mm(byr, (2 * 8 + m) * 16, blk(wro, 8), orT, t, cam, cmeta)
                        fm_mm(bya, (3 * 8 + m) * 16, blk(wao, 4), oaT, t, cam, cmeta)
                        tt(t1.c(0, nmain), GS[0][m4].c(0, nmain), byr.v.c(0, nmain), ALU.mult)
                        tt(t2.c(0, nmain), GS[1][m4].c(0, nmain), bya.v.c(0, nmain), ALU.mult)
                        tt(zT[m].c(vm[0], vm[1]), t1.c(0, nmain), t2.c(0, nmain), ALU.add)
                if t == 0:
                    sg = sub(RB, TB + 20480, TB + 21504, F32)
                    pr = sub(RB, TB + 21504, TB + 22528, F32)
                    act(sg, MB.v.c(0, 256), AF.Sigmoid)
                    tt(pr, sg, MB.v.c(256, 512), ALU.mult)
                    zm = span(z.c(0, 16) for z in zT)
                    zm.ap = zT3[:, :, 0:16]
                    tt(zm, pr.c(0, 128).r("p (m t) -> p m t", m=8), pr.c(128, 256).r("p (m t) -> p m t", m=8), ALU.add)

                chk("merge%d" % t)
                Mfirst[0] = True
                SSB.new()
                for mg in range(2):
                    wmx = ws.use(w_mx_d[l, :, mg * 512:(mg + 1) * 512], 8, 512, live=1)
                    for m4 in range(4):
                        m = mg * 4 + m4
                        b = rbank()
                        fm_mm(b, m * 16, [wmx.c(k * 512 + m4 * 128, k * 512 + (m4 + 1) * 128) for k in range(8)],
                              zT, t, vm, (0, 16))
                        cp(mixT[m].c(vm[0], vm[1]), b.v.c(0, nmain), eng="act")
                        s_ = sq[m % 2].c(0, nmain)
                        act(s_, mixT[m].c(vm[0], vm[1]), AF.Square)
                        mm(SSB.v.c(0, nmain), ones_b, s_, start=SSB.st(), stop=(m == 7))
                post_residual(t, l, 1, mixT, mixT3)

                chk("mix%d" % t)
                prenorm(t, l, 2)
                Mfirst[0] = True
                for fg in range(8):
                    w1 = ws.use(w_f1_d[l, :, fg * 512:(fg + 1) * 512], 8, 512, live=1)
                    for f4 in range(4):
                        f = fg * 4 + f4
                        b = rbank()
                        fm_mm(b, f * 16, [w1.c(k * 512 + f4 * 128, k * 512 + (f4 + 1) * 128) for k in range(8)],
                              uT, t, cam, cmeta)
                        r_ = ffr[f % 2].c(0, nmain)
                        act(r_, b.v.c(0, nmain), AF.Square)
                        stt(hid[f].c(vm[0], vm[1]), b.v.c(0, nmain), 0.0, r_, ALU.is_gt, ALU.mult)
                if t == 0:
                    r_ = ffr[0]
                    act(r_, MB.v.c(0, 512), AF.Square)
                    hm = span(hh.c(0, 16) for hh in hid)
                    hm.ap = hid3[:, :, 0:16]
                    stt(hm, MB.v.c(0, 512).r("p (f t) -> p f t", f=32), 0.0, r_.r("p (f t) -> p f t", f=32),
                        ALU.is_gt, ALU.mult)
                Mfirst[0] = True
                SSB.new()
                for mp in range(4):
                    bb = [rbank(), rbank()]
                    for fq in range(4):
                        w2 = ws.use(w_f2_d[l, fq * 1024:(fq + 1) * 1024, mp * 256:(mp + 1) * 256], 8, 256, live=1)
                        for f8 in range(8):
                            f = fq * 8 + f8
                            for m2 in range(2):
                                m = mp * 2 + m2
                                L = w2.c(f8 * 256 + m2 * 128, f8 * 256 + (m2 + 1) * 128)
                                mm(bb[m2].v.c(0, nmain), L, hid[f].c(vm[0], vm[1]), start=bb[m2].st(), stop=(f == 31))
                                if t == 0:
                                    mm(MB.v.c(m * 16, (m + 1) * 16), L, hid[f].c(0, 16), start=mstart(), stop=(f == 31))
                    for m2 in range(2):
                        m = mp * 2 + m2
                        cp(mixT[m].c(vm[0], vm[1]), bb[m2].v.c(0, nmain), eng="act")
                        s_ = sq[m % 2].c(0, nmain)
                        act(s_, mixT[m].c(vm[0], vm[1]), AF.Square)
                        mm(SSB.v.c(0, nmain), ones_b, s_, start=SSB.st(), stop=(m == 7))
                post_residual(t, l, 3, mixT, mixT3)
                chk("ffn%d" % t)

        P.muted = False
        P.phase = 3
        ostage = [sub(SC, 0, 4096, F32), sub(SC, 4096, 8192, F32)]
        for n in range(1, NCH):
            st = ostage[n % 2]
            for half in range(2):
                b = rbank()
                for j in range(4):
                    c = half * 4 + j
                    tr(b.v.c(j * 128, (j + 1) * 128), hT[c].c(n * 128, (n + 1) * 128), ident_f)
                cp(st.c(half * 512, (half + 1) * 512), b.v.c(0, 512), eng=("act" if half else "dve"))
            dma("sp", y_d[(n - 1) * 128: n * 128, :], st, slot=("o", n % 2), out_tok=V(None, "d_y", n, n + 1, 1))
        P.op("sp", lambda e: e.nop(), reads=[V(None, "d_y", 0, NCH + 1, 1)])

    print('sbuf bytes remaining', nc.sbuf_bytes_remaining)
    rec = WS(None)
    P0 = P
    emit_all(rec)
    P = Prog()
    for b in banks:
        b.first = True
    ring[0] = 0
    lastpe.clear()
    ws = WS(rec.rec)
    emit_all(ws)
    P.emit(nc, es)
    es.close()
    return nc, P


def _consts():
    f32 = np.float32
    p = np.arange(128)
    pos = (np.arange(NCH)[None, :] * 128 + p[:, None] - 112).astype(f32)

    def tab(theta, rot):
        fr = np.power(f32(theta), -np.arange(0, rot, 2, dtype=f32) / f32(rot)).astype(f32)
        ang = (pos[:, :, None] * fr[None, None, :]).astype(f32)
        return (np.cos(ang).astype(f32).reshape(128, -1), np.sin(ang).astype(f32).reshape(128, -1))

    cosr, sinr = tab(10000.0, 64)
    cosa, sina = tab(500000.0, 16)
    j = p[:, None]
    i = p[None, :]
    rfp = (-(np.minimum(i, j) + 1)).astype(f32)
    rbm = np.maximum(j - i, 0).astype(f32)
    pcols = np.stack([p + 1, 128 - p, 127 - p, p], axis=1).astype(f32)
    k, q = j, i
    mprev = np.where(k >= q, 0.0, NEG).astype(f32)
    mnext = np.where(k <= q, 0.0, NEG).astype(f32)
    mmeta = np.where(k >= 112, 0.0, NEG).astype(f32) + np.zeros((128, 128), f32)
    return dict(cosr=cosr, sinr=sinr, cosa=cosa, sina=sina, rfp=rfp, rbm=rbm, pcols=pcols,
                mprev=mprev, mnext=mnext, mmeta=mmeta, ident=np.eye(128, dtype=f32))


def _perm():
    cols = []
    for hp in range(4):
        a, b = 2 * hp, 2 * hp + 1
        for h in (a, b):
            cols += list(range(h * 64, (h + 1) * 64))
        for h in (a, b):
            cols += list(range(512 + h * 64, 512 + (h + 1) * 64))
        for h in (a, b):
            cols += list(range(1024 + h * 128, 1024 + (h + 1) * 128))
        for h in (a, b):
            cols += list(range(2048 + h * 128, 2048 + (h + 1) * 128))
    for j in range(4):
        cols += list(range(3072 + j * 64, 3072 + (j + 1) * 64))
        cols += list(range(3072 + (4 + j) * 64, 3072 + (5 + j) * 64))
    cols += list(range(3584, 5888))
    return np.asarray(cols)


_CACHE = {}


def host_inputs(x, meta_tokens, w_in, w_ret_o, w_att_o, w_mix_o, w_ff1, w_ff2, norm_mix_pre, norm_mix_post,
                norm_ff_pre, norm_ff_post, ret_decay, attn_sink, ncores=8):
    f32 = np.float32
    A = lambda a: np.ascontiguousarray(np.asarray(a, dtype=f32))
    w_in_p = np.ascontiguousarray(A(w_in)[:, :, _perm()])
    norms = np.stack([A(norm_mix_pre), A(norm_mix_post), A(norm_ff_pre), A(norm_ff_post)], axis=1)
    gains = np.ascontiguousarray(norms.reshape(4, 4, 8, 128).transpose(3, 0, 1, 2).reshape(128, 128))
    shared = dict(meta=A(meta_tokens), w_in=w_in_p, w_ret_o=A(w_ret_o), w_att_o=A(w_att_o), w_mix_o=A(w_mix_o),
                  w_ff1=A(w_ff1), w_ff2=A(w_ff2), gains=gains, ret_decay=A(ret_decay).reshape(64),
                  attn_sink=A(attn_sink).reshape(32))
    shared.update(_consts())
    xs = A(x)
    return [dict(shared, x=xs[b]) for b in range(ncores)]


def kernel(**inputs):
    if "nc" not in _CACHE:
        _CACHE["nc"] = build()[0]
    nc = _CACHE["nc"]
    in_maps = host_inputs(**inputs)
    res = run_bass_kernel_spmd(nc, in_maps, core_ids=list(range(8)))
    return np.stack([np.asarray(r["y"], dtype=np.float32) for r in res.results], axis=0)
```
